# Optimizing a Trainium2 kernel written in Bass

```python
import math, functools
import jax, jax.numpy as jnp
from jax import lax
import numpy as np

D_MODEL = 1024
BATCH = 2
SEQ = 8192
DEPTH = 1
DEC_BATCH = 32
DEC_SEQ = 1
PAST_LEN = 16384
PAGE_SIZE = 128

H_A = 8
HD_A = 64
W_A = H_A * HD_A
ROT_DIM = HD_A // 4
ROPE_THETA = 500000.0
MOBA_BLOCK = 256
MOBA_TOPK = 3
Q_BLOCK = 64
PAGES_PER_BLOCK = MOBA_BLOCK // PAGE_SIZE
H_B = 8
DK_B = 128
DV_B = 128
W_B = H_B * DV_B
CONV_K = 4
C_CONV = H_B * (2 * DK_B + DV_B)
GDN_CHUNK = 64
PLE_DIM = 256
EPS = 1e-6
SPLIT_SIZES = (W_A, W_A, W_A, W_A, H_B * DK_B, H_B * DK_B, W_B, W_B, H_B, H_B, D_MODEL, D_MODEL)
N_IN = sum(SPLIT_SIZES)

kernel_name = 'moba_gdn_hybrid_step'

F32 = jnp.float32


def rms_norm(x, g):
    xf = x.astype(F32)
    y = xf * lax.rsqrt(jnp.mean(xf * xf, axis=-1, keepdims=True) + EPS)
    return (y * g.astype(F32)).astype(x.dtype)


def l2_norm(x):
    xf = x.astype(F32)
    return xf * lax.rsqrt(jnp.sum(xf * xf, axis=-1, keepdims=True) + EPS)


def partial_rope(x, pos):
    half = ROT_DIM // 2
    inv = jnp.power(ROPE_THETA, -jnp.arange(half, dtype=F32) / half)
    ang = pos.astype(F32)[:, None] * inv[None, :]
    cos = jnp.cos(ang)[None, :, None, :]
    sin = jnp.sin(ang)[None, :, None, :]
    xf = x.astype(F32)
    x1, x2 = xf[..., :half], xf[..., half:ROT_DIM]
    out = jnp.concatenate([x1 * cos - x2 * sin, x2 * cos + x1 * sin, xf[..., ROT_DIM:]], axis=-1)
    return out.astype(x.dtype)


def causal_conv_silu(u, buf, w):
    T = u.shape[1]
    up = jnp.concatenate([buf.astype(u.dtype), u], axis=1)
    out = sum(up[:, i:i + T] * w[i] for i in range(CONV_K))
    return jax.nn.silu(out), up[:, T:]


def gated_delta_chunked(q, k, v, g, beta, s0):
    B, T, H, DK = q.shape
    DV = v.shape[-1]
    C = GDN_CHUNK
    n = -(-T // C)
    pad = n * C - T

    def chunks(a):
        a = jnp.pad(a, [(0, 0), (0, pad)] + [(0, 0)] * (a.ndim - 2))
        a = a.reshape((B, n, C) + a.shape[2:])
        return jnp.moveaxis(a, 3, 1)

    q = chunks(q * (DK ** -0.5))
    k, v, g, beta = chunks(k), chunks(v), chunks(g), chunks(beta)
    gc = jnp.cumsum(g, axis=-1)
    causal = jnp.tril(jnp.ones((C, C), bool))
    strict = jnp.tril(jnp.ones((C, C), bool), -1)
    decay = jnp.exp(jnp.where(causal, gc[..., :, None] - gc[..., None, :], -jnp.inf))
    kbeta = k * beta[..., None]
    a_mat = jnp.where(strict, jnp.einsum('bhncd,bhnsd->bhncs', kbeta, k) * decay, 0.0)
    m = a_mat + jnp.eye(C, dtype=F32)
    rhs = jnp.concatenate([v * beta[..., None], kbeta * jnp.exp(gc)[..., None]], axis=-1)
    sol = lax.linalg.triangular_solve(m, rhs, left_side=True, lower=True, unit_diagonal=True)
    u, w = sol[..., :DV], sol[..., DV:]
    attn = jnp.where(causal, jnp.einsum('bhncd,bhnsd->bhncs', q, k) * decay, 0.0)
    q_dec = q * jnp.exp(gc)[..., None]
    k_tail = k * jnp.exp(gc[..., -1:] - gc)[..., None]
    c_dec = jnp.exp(gc[..., -1])
    xs = tuple(jnp.moveaxis(a, 2, 0) for a in (u, w, attn, q_dec, k_tail, c_dec))

    def step(s, xc):
        u_c, w_c, at_c, qd_c, kt_c, cd_c = xc
        v_new = u_c - jnp.einsum('bhcd,bhde->bhce', w_c, s)
        o = jnp.einsum('bhcd,bhde->bhce', qd_c, s) + jnp.einsum('bhcs,bhse->bhce', at_c, v_new)
        s = s * cd_c[..., None, None] + jnp.einsum('bhcd,bhce->bhde', kt_c, v_new)
        return s, o

    s_fin, o = lax.scan(step, s0, xs)
    o = o.transpose(1, 0, 3, 2, 4).reshape(B, n * C, H, DV)[:, :T]
    return o, s_fin


def moba_attend(q, q_pos, kbar, get_kv):
    B, T, H, D = q.shape
    nb = kbar.shape[1]
    if nb < MOBA_TOPK:
        kbar = jnp.pad(kbar, ((0, 0), (0, MOBA_TOPK - nb), (0, 0), (0, 0)))
    own = q_pos // MOBA_BLOCK
    gate = jnp.einsum('bthd,bnhd->bthn', q.astype(F32), kbar.astype(F32))
    fully_past = jnp.arange(kbar.shape[1])[None, :] < own[:, None]
    gate = jnp.where(fully_past[None, :, None, :], gate, -jnp.inf)
    _, top = lax.top_k(gate, MOBA_TOPK)
    sel_ok = top < own[None, :, None, None]
    own_idx = jnp.broadcast_to(own[None, :, None, None], (B, T, H, 1))
    idx = jnp.concatenate([jnp.minimum(top, nb - 1), own_idx], axis=-1).astype(jnp.int32)
    k_g, v_g = get_kv(idx)
    key_pos = idx[..., None] * MOBA_BLOCK + jnp.arange(MOBA_BLOCK)
    ok = jnp.concatenate([sel_ok, jnp.ones((B, T, H, 1), bool)], axis=-1)[..., None]
    ok = ok & (key_pos <= q_pos[None, :, None, None, None])
    s = jnp.einsum('bthd,bthnkd->bthnk', q, k_g).astype(F32) * (HD_A ** -0.5)
    s = jnp.where(ok, s, -jnp.inf).reshape(B, T, H, -1)
    p = jax.nn.softmax(s, axis=-1).reshape(k_g.shape[:-1]).astype(v_g.dtype)
    return jnp.einsum('bthnk,bthnkd->bthd', p, v_g)


def moba_prompt(q, k, v):
    B, T, H, D = q.shape
    nb = -(-T // MOBA_BLOCK)
    pad = nb * MOBA_BLOCK - T
    to_blocks = lambda a: jnp.pad(a, ((0, 0), (0, pad), (0, 0), (0, 0))).reshape(B, nb, MOBA_BLOCK, H, D)
    kblk, vblk = to_blocks(k), to_blocks(v)
    kbar = jnp.mean(kblk.astype(F32), axis=2)
    kbh = kblk.transpose(0, 3, 1, 2, 4)
    vbh = vblk.transpose(0, 3, 1, 2, 4)
    bi = jnp.arange(B)[:, None, None, None]
    hi = jnp.arange(H)[None, None, :, None]

    def get_kv(idx):
        return kbh[bi, hi, idx], vbh[bi, hi, idx]

    nq = T // Q_BLOCK
    qs = q.reshape(B, nq, Q_BLOCK, H, D).transpose(1, 0, 2, 3, 4)
    ps = jnp.arange(T, dtype=jnp.int32).reshape(nq, Q_BLOCK)
    o = lax.map(lambda a: moba_attend(a[0], a[1], kbar, get_kv), (qs, ps))
    return o.transpose(1, 0, 2, 3, 4).reshape(B, T, H, D)


def moba_paged(q, k, v, cache_k, cache_v, page_table):
    B, T, H, D = q.shape
    n_pages = page_table.shape[1]
    past_len = n_pages * PAGE_SIZE
    nb = -(-(past_len + T) // MOBA_BLOCK)
    n_new = nb * PAGES_PER_BLOCK - n_pages
    pad = n_new * PAGE_SIZE - T
    to_pages = lambda a, ref: jnp.pad(a.astype(ref.dtype), ((0, 0), (0, pad), (0, 0), (0, 0))).reshape(B, n_new, PAGE_SIZE, H, D)
    k_new, v_new = to_pages(k, cache_k), to_pages(v, cache_v)
    past_sums = jnp.sum(cache_k[page_table].astype(F32), axis=2)
    page_sums = jnp.concatenate([past_sums, jnp.sum(k_new.astype(F32), axis=2)], axis=1)
    kbar = page_sums.reshape(B, nb, PAGES_PER_BLOCK, H, D).sum(axis=2) / MOBA_BLOCK
    bi = jnp.arange(B)[:, None, None, None, None]
    hi = jnp.arange(H)[None, None, :, None, None]

    def get_kv(idx):
        n_sel = idx.shape[-1]
        pages = idx[..., None] * PAGES_PER_BLOCK + jnp.arange(PAGES_PER_BLOCK)
        in_past = (pages < n_pages)[..., None, None]
        phys = page_table[bi, jnp.minimum(pages, n_pages - 1)]
        new_i = jnp.clip(pages - n_pages, 0, n_new - 1)

        def fetch(pool, new):
            rows = jnp.where(in_past, pool[phys, :, hi], new[bi, new_i, :, hi])
            return rows.reshape(B, T, H, n_sel, MOBA_BLOCK, D)

        return fetch(cache_k, k_new), fetch(cache_v, v_new)

    q_pos = past_len + jnp.arange(T, dtype=jnp.int32)
    return moba_attend(q, q_pos, kbar, get_kv)


def decoder_layer(x, p, pos, s0, conv0, attend, g_mix, w_in, conv_w, a_log, dt_bias, g_onorm,
                  w_pa, w_pb, w_o, g_ple, w_ple_gate, w_ple):
    B, T, _ = x.shape
    h = rms_norm(x, g_mix)
    cuts = np.cumsum(SPLIT_SIZES)[:-1].tolist()
    qa, ka, va, za, qb, kb, vb, zb, b_lin, a_lin, ga, gb = jnp.split(h @ w_in, cuts, axis=-1)
    qa = partial_rope(qa.reshape(B, T, H_A, HD_A), pos)
    ka = partial_rope(ka.reshape(B, T, H_A, HD_A), pos)
    va = va.reshape(B, T, H_A, HD_A)
    oa = attend(qa, ka, va)
    ya = (oa.reshape(B, T, W_A) * jax.nn.silu(za)) @ w_pa
    conv_out, conv_new = causal_conv_silu(jnp.concatenate([qb, kb, vb], axis=-1), conv0, conv_w)
    qc, kc, vc = jnp.split(conv_out, [H_B * DK_B, 2 * H_B * DK_B], axis=-1)
    qc = l2_norm(qc.reshape(B, T, H_B, DK_B))
    kc = l2_norm(kc.reshape(B, T, H_B, DK_B))
    vc = vc.reshape(B, T, H_B, DV_B).astype(F32)
    beta = jax.nn.sigmoid(b_lin.astype(F32))
    g = -jnp.exp(a_log.astype(F32)) * jax.nn.softplus(a_lin.astype(F32) + dt_bias.astype(F32))
    ob, s_new = gated_delta_chunked(qc, kc, vc, g, beta, s0.astype(F32))
    ob = rms_norm(ob, g_onorm).astype(x.dtype) * jax.nn.silu(zb.reshape(B, T, H_B, DV_B))
    yb = ob.reshape(B, T, W_B) @ w_pb
    mixed = jax.nn.sigmoid(ga) * ya + jax.nn.sigmoid(gb) * yb
    x = x + mixed @ w_o
    x = x + jax.nn.sigmoid(rms_norm(x, g_ple) @ w_ple_gate) * (p @ w_ple)
    return x, ka, va, s_new, conv_new


def setup_inputs(seed: int = 0) -> dict:
    key = jax.random.key(seed)
    ks = jax.random.split(key, 24)
    n_pages = PAST_LEN // PAGE_SIZE
    n_used = DEC_BATCH * n_pages
    n_pool = n_used + n_used // 4

    def nrm(k, shape, scale=1.0):
        return jax.random.normal(k, shape, F32) * scale

    dt = jnp.exp(jax.random.uniform(ks[13], (DEPTH, H_B), F32, math.log(1e-3), math.log(1e-1)))
    return {
        'x_prompt': nrm(ks[0], (BATCH, SEQ, D_MODEL)),
        'x_sample': nrm(ks[1], (DEC_BATCH, DEC_SEQ, D_MODEL)),
        'p_prompt': nrm(ks[2], (DEPTH, BATCH, SEQ, PLE_DIM)),
        'p_sample': nrm(ks[3], (DEPTH, DEC_BATCH, DEC_SEQ, PLE_DIM)),
        'cache_k': nrm(ks[4], (DEPTH, n_pool, PAGE_SIZE, H_A, HD_A)),
        'cache_v': nrm(ks[5], (DEPTH, n_pool, PAGE_SIZE, H_A, HD_A)),
        'page_table': jax.random.permutation(ks[6], n_pool)[:n_used].reshape(DEC_BATCH, n_pages).astype(jnp.int32),
        'state_gdn_s': nrm(ks[7], (DEPTH, DEC_BATCH, H_B, DK_B, DV_B), 0.3),
        'state_gdn_conv': nrm(ks[8], (DEPTH, DEC_BATCH, CONV_K - 1, C_CONV)),
        'g_mix': 1.0 + nrm(ks[9], (DEPTH, D_MODEL), 0.02),
        'w_in': nrm(ks[10], (DEPTH, D_MODEL, N_IN), D_MODEL ** -0.5),
        'conv_w': nrm(ks[11], (DEPTH, CONV_K, C_CONV), CONV_K ** -0.5),
        'a_log': jnp.log(jax.random.uniform(ks[12], (DEPTH, H_B), F32, 1.0, 16.0)),
        'dt_bias': dt + jnp.log(-jnp.expm1(-dt)),
        'g_onorm': 1.0 + nrm(ks[14], (DEPTH, DV_B), 0.02),
        'w_pa': nrm(ks[15], (DEPTH, W_A, D_MODEL), W_A ** -0.5),
        'w_pb': nrm(ks[16], (DEPTH, W_B, D_MODEL), W_B ** -0.5),
        'w_o': nrm(ks[17], (DEPTH, D_MODEL, D_MODEL), D_MODEL ** -0.5),
        'g_ple': 1.0 + nrm(ks[18], (DEPTH, D_MODEL), 0.02),
        'w_ple_gate': nrm(ks[19], (DEPTH, D_MODEL, D_MODEL), D_MODEL ** -0.5),
        'w_ple': nrm(ks[20], (DEPTH, PLE_DIM, D_MODEL), PLE_DIM ** -0.5),
        'g_final': 1.0 + nrm(ks[21], (D_MODEL,), 0.02),
    }


def reference(x_prompt, x_sample, p_prompt, p_sample, cache_k, cache_v, page_table, state_gdn_s,
              state_gdn_conv, g_mix, w_in, conv_w, a_log, dt_bias, g_onorm, w_pa, w_pb, w_o, g_ple,
              w_ple_gate, w_ple, g_final):
    bp, tp, _ = x_prompt.shape
    ts = x_sample.shape[1]
    pos_p = jnp.arange(tp, dtype=jnp.int32)
    pos_s = page_table.shape[1] * PAGE_SIZE + jnp.arange(ts, dtype=jnp.int32)
    xp, xs = x_prompt, x_sample
    kp_l, vp_l, sp_l, cp_l, ks_l, vs_l, ss_l, cs_l = [], [], [], [], [], [], [], []
    for l in range(DEPTH):
        s0p = jnp.zeros((bp, H_B, DK_B, DV_B), F32)
        c0p = jnp.zeros((bp, CONV_K - 1, C_CONV), xp.dtype)
        xp, kp, vp, sp, cp = decoder_layer(
            xp, p_prompt[l], pos_p, s0p, c0p, moba_prompt,
            g_mix[l], w_in[l], conv_w[l], a_log[l], dt_bias[l], g_onorm[l],
            w_pa[l], w_pb[l], w_o[l], g_ple[l], w_ple_gate[l], w_ple[l])
        attend_s = functools.partial(moba_paged, cache_k=cache_k[l], cache_v=cache_v[l], page_table=page_table)
        xs, ksm, vsm, ssm, csm = decoder_layer(
            xs, p_sample[l], pos_s, state_gdn_s[l], state_gdn_conv[l], attend_s,
            g_mix[l], w_in[l], conv_w[l], a_log[l], dt_bias[l], g_onorm[l],
            w_pa[l], w_pb[l], w_o[l], g_ple[l], w_ple_gate[l], w_ple[l])
        kp_l.append(kp); vp_l.append(vp); sp_l.append(sp.astype(state_gdn_s.dtype)); cp_l.append(cp)
        ks_l.append(ksm); vs_l.append(vsm); ss_l.append(ssm.astype(state_gdn_s.dtype)); cs_l.append(csm)
    y_prompt = rms_norm(xp, g_final)
    y_sample = rms_norm(xs, g_final)
    k_prompt, v_prompt = jnp.stack(kp_l), jnp.stack(vp_l)
    s_prompt, conv_prompt = jnp.stack(sp_l), jnp.stack(cp_l)
    k_sample, v_sample = jnp.stack(ks_l), jnp.stack(vs_l)
    s_sample, conv_sample = jnp.stack(ss_l), jnp.stack(cs_l)
    return (y_prompt, y_sample, k_prompt, v_prompt, s_prompt, conv_prompt, k_sample, v_sample, s_sample, conv_sample)
```

```python
import contextlib
import numpy as np
import ml_dtypes
import concourse.bass as bass
import concourse.mybir as mybir
from concourse.bass_utils import run_bass_kernel_spmd

F32 = mybir.dt.float32
BF16 = mybir.dt.bfloat16
I32 = mybir.dt.int32
AF = mybir.ActivationFunctionType
ALU = mybir.AluOpType
AX = mybir.AxisListType

FULL = dict(D=1024, SEQ=8192, HA=8, HB=8, PLE=256, PAST=16384, NPOOL=5120, NS=4, NCORES=8)
EPS = 1e-6
import os as _os
LIMIT = int(_os.environ.get('KLIMIT', '1000000000'))
NEG = -30000.0


class Buf:
    __slots__ = ("w", "r", "name")

    def __init__(self, name=""):
        self.w = None
        self.r = {}
        self.name = name


class Sched:
    ENGS = ("pe", "act", "dve", "pool", "sp")
    NRING = 8

    def __init__(self, nc, stack):
        self.nc = nc
        self.stack = stack
        self.ops = {e: [] for e in self.ENGS}
        self.sem = {}
        self.cnt = {}
        self.seen = {e: {} for e in self.ENGS}
        self.nsem = 0
        for e in ("pe", "act", "dve", "pool"):
            self._newsem(e)
        self.ring = {}
        self.ringcnt = {}
        self.ringpos = {}
        for q in ("sp", "pool", "act"):
            self.ring[q] = [stack.enter_context(nc.semaphore(f"dq_{q}_{i}")) for i in range(self.NRING)]
            self.ringcnt[q] = [0] * self.NRING
            self.ringpos[q] = 0
        self.final = []

    def _newsem(self, e):
        self.nsem += 1
        self.sem[e] = self.stack.enter_context(self.nc.semaphore(f"s_{e}_{self.nsem}"))
        self.cnt[e] = 0

    def _waits(self, e, R, W):
        need = {}

        def add(tok):
            if tok is None:
                return
            s, v = tok
            if need.get(s, (None, 0))[1] < v:
                need[s] = (s, v)

        for b in R:
            add(b.w)
        for b in W:
            add(b.w)
            for tok in b.r.values():
                add(tok)
        out = []
        seen = self.seen[e]
        for s, v in need.values():
            if seen.get(id(s), 0) >= v:
                continue
            seen[id(s)] = v
            out.append((s, v))
        return out

    def _commit(self, tok, R, W):
        for b in W:
            b.w = tok
            b.r = {}
        for b in R:
            if b in W:
                continue
            old = b.r.get(id(tok[0]))
            if old is None or old[1] < tok[1]:
                b.r[id(tok[0])] = tok

    def op(self, e, fn, R=(), W=()):
        self.total = getattr(self, "total", 0) + 1
        if self.total > LIMIT:
            return None
        waits = self._waits(e, R, W)
        if self.cnt[e] >= 30000:
            self._newsem(e)
        self.cnt[e] += 1
        tok = (self.sem[e], self.cnt[e])
        self.ops[e].append((waits, fn, tok[0], 1))
        self._commit(tok, R, W)
        return tok

    def dma(self, q, fn, R=(), W=(), final=False):
        self.total = getattr(self, "total", 0) + 1
        if self.total > LIMIT:
            return None
        waits = self._waits(q, R, W)
        i = self.ringpos[q]
        self.ringpos[q] = (i + 1) % self.NRING
        s = self.ring[q][i]
        prev = 16 * self.ringcnt[q][i]
        if prev > 0 and self.seen[q].get(id(s), 0) < prev:
            self.seen[q][id(s)] = prev
            waits.append((s, prev))
        self.ringcnt[q][i] += 1
        tok = (s, 16 * self.ringcnt[q][i])
        self.ops[q].append((waits, fn, s, 16))
        self._commit(tok, R, W)
        if final:
            self.final.append(tok)
        return tok

    def barrier(self):
        toks = []
        for e in ("pe", "act", "dve", "pool"):
            if self.cnt[e] > 0:
                toks.append((self.sem[e], self.cnt[e]))
        for q in self.ring:
            for i, s in enumerate(self.ring[q]):
                if self.ringcnt[q][i] > 0:
                    toks.append((s, 16 * self.ringcnt[q][i]))
        for e in self.ENGS:
            w = []
            for s, v in toks:
                if self.seen[e].get(id(s), 0) < v:
                    self.seen[e][id(s)] = v
                    w.append((s, v))
            self.ops[e].append((w, None, None, 0))

    def emit(self):
        nc = self.nc
        fin = {}
        for s, v in self.final:
            if fin.get(id(s), (None, 0))[1] < v:
                fin[id(s)] = (s, v)
        for q in self.ring:
            for i, s in enumerate(self.ring[q]):
                v = 16 * self.ringcnt[q][i]
                if v > 0:
                    fin[id(s)] = (s, v)
        lists = self.ops

        def run(eng, lst, tail=None):
            for waits, fn, s, inc in lst:
                for ws, wv in waits:
                    eng.wait_ge(ws, wv)
                if fn is not None:
                    fn(eng).then_inc(s, inc)
            if tail:
                for ws, wv in tail:
                    eng.wait_ge(ws, wv)

        with nc.Block() as block:
            @block.tensor
            def _(e):
                run(e, lists["pe"])

            @block.scalar
            def _(e):
                run(e, lists["act"])

            @block.vector
            def _(e):
                run(e, lists["dve"])

            @block.gpsimd
            def _(e):
                run(e, lists["pool"])

            @block.sync
            def _(e):
                run(e, lists["sp"], tail=list(fin.values()))


class KB:
    def __init__(self, cfg):
        self.cfg = cfg
        self.nc = bass.Bass("TRN2", target_bir_lowering=False)
        self.stack = contextlib.ExitStack()
        self.S = Sched(self.nc, self.stack)
        self.bufs = {}
        self.rr = 0
        self.pstack = None

    def dram(self, name, shape, dt, kind="Internal"):
        t = self.nc.dram_tensor(name, list(shape), dt, kind=kind).ap()
        self.bufs[name] = Buf(name)
        return t

    def sb(self, name, shape, dt):
        st = self.pstack if self.pstack is not None else self.stack
        t = st.enter_context(self.nc.sbuf_tensor(name, list(shape), dt))
        return t

    def phase_begin(self):
        self.pstack = contextlib.ExitStack()

    def phase_end(self):
        self.S.barrier()
        self.pstack.close()
        self.pstack = None

    def ps(self, name, shape, dt):
        t = self.stack.enter_context(self.nc.psum_tensor(name, list(shape), dt))
        return t

    def B(self, name):
        b = self.bufs.get(name)
        if b is None:
            b = self.bufs[name] = Buf(name)
        return b

    def bl(self, names):
        return [self.B(n) for n in names]

    def dma(self, out, in_, R, W, q="sp", final=False, slow=False):
        kw = dict(allow_slow_non_contiguous=True) if slow else {}
        self.S.dma(q, lambda e: e.dma_start(out=out, in_=in_, **kw), self.bl(R), self.bl(W), final=final)

    def op(self, e, fn, R, W):
        W = list(W) + [n for n in R if n.startswith("ps") and n not in W]
        self.S.op(e, fn, self.bl(R), self.bl(W))

    def mm(self, out, lhsT, rhs, R, W, start=True, stop=True):
        self.op("pe", lambda e: e.matmul(out, lhsT, rhs, start=start, stop=stop, skip_group_check=True), R, W)

    def tr(self, out, in_, ident, R, W):
        self.op("pe", lambda e: e.transpose(out, in_, ident), R, W)

    def act(self, out, in_, func, R, W, bias=None, scale=None, accum=None, eng="act"):
        kw = {}
        if bias is not None:
            kw["bias"] = bias
        if scale is not None:
            kw["scale"] = scale
        if accum is not None:
            kw["accum_out"] = accum
        self.op("act", lambda e: e.activation(out, in_, func, **kw), R, W)

    def ts(self, e, out, in0, s1, s2, op0, op1, R, W):
        if s2 is None:
            self.op(e, lambda g: g.tensor_scalar(out, in0, s1, None, op0), R, W)
        else:
            self.op(e, lambda g: g.tensor_scalar(out, in0, s1, s2, op0, op1), R, W)

    def tt(self, e, out, in0, in1, op, R, W):
        self.op(e, lambda g: g.tensor_tensor(out, in0, in1, op), R, W)

    def stt(self, e, out, in0, sc, in1, op0, op1, R, W):
        self.op(e, lambda g: g.scalar_tensor_tensor(out, in0, sc, in1, op0, op1), R, W)

    def cp(self, e, out, in_, R, W):
        if e == "act":
            self.op("act", lambda g: g.copy(out, in_), R, W)
        else:
            self.op(e, lambda g: g.tensor_copy(out, in_), R, W)

    def rot(self, engs=("act", "dve", "pool")):
        self.rr += 1
        return engs[self.rr % len(engs)]

    def rsqrt(self, out, in_, mul, add, R, W, e="dve"):
        self.ts("dve", out, in_, mul, add, ALU.mult, ALU.add, R, W)
        self.act(out, out, AF.Ln, W, W)
        self.act(out, out, AF.Exp, W, W, scale=-0.5)


def col_offsets(cfg):
    HA, HB, D = cfg["HA"], cfg["HB"], cfg["D"]
    WA, WB = HA * 64, HB * 128
    sizes = [WA, WA, WA, WA, WB, WB, WB, WB, HB, HB, D, D]
    names = ["qa", "ka", "va", "za", "qb", "kb", "vb", "zb", "beta", "alpha", "ga", "gb"]
    off, o = {}, 0
    for n, s in zip(names, sizes):
        off[n] = o
        o += s
    off["N_IN"] = o
    return off


def host_consts(cfg):
    SEQ, HA, PAST = cfg["SEQ"], cfg["HA"], cfg["PAST"]
    c = {}
    c["identb"] = np.eye(128, dtype=np.float32).astype(ml_dtypes.bfloat16)
    c["identf"] = np.eye(128, dtype=np.float32)
    r = np.arange(128)
    c["u1"] = (r[:, None] <= r[None, :]).astype(np.float32)
    c["sl1"] = (r[:, None] > r[None, :]).astype(np.float32)
    c["onesf"] = np.ones((128, 128), np.float32)
    c["onesb"] = np.ones((128, 128), np.float32).astype(ml_dtypes.bfloat16)
    c["trib"] = (r[:, None] <= r[None, :]).astype(np.float32).astype(ml_dtypes.bfloat16)
    half = 8
    inv = np.power(500000.0, -np.arange(half, dtype=np.float32) / half).astype(np.float32)

    def tab(pos):
        ang = (pos.astype(np.float32)[:, None] * inv[None, :]).astype(np.float32)
        return np.cos(ang).astype(np.float32), np.sin(ang).astype(np.float32)

    cp, sp_ = tab(np.arange(SEQ))
    cp = np.concatenate([cp, np.tile(tab(np.array([PAST]))[0], (128, 1))], 0)
    sp_ = np.concatenate([sp_, np.tile(tab(np.array([PAST]))[1], (128, 1))], 0)
    c["cosp"] = np.tile(cp[:, None, :], (1, HA, 1)).reshape(SEQ + 128, HA * 8).copy()
    c["sinp"] = np.tile(sp_[:, None, :], (1, HA, 1)).reshape(SEQ + 128, HA * 8).copy()
    cs, ss = tab(np.array([PAST]))
    c["coss"] = np.tile(cs[:, None, :], (cfg["NS"], HA, 1)).reshape(cfg["NS"], HA * 8).copy()
    c["sins"] = np.tile(ss[:, None, :], (cfg["NS"], HA, 1)).reshape(cfg["NS"], HA * 8).copy()
    nkt = SEQ // 128
    oh = np.zeros((128, nkt, 32), np.float32)
    for kt in range(nkt):
        oh[:, kt, (kt // 2) % 32] = 1.0
    c["onehot"] = oh.astype(ml_dtypes.bfloat16)
    c["md8"] = (r[:, None] // 8 == r[None, :] // 8).astype(np.float32)
    for b in (8, 16, 32, 64):
        mo = ((r[:, None] // (2 * b) == r[None, :] // (2 * b)) & (r[:, None] % (2 * b) >= b)
              & (r[None, :] % (2 * b) < b)).astype(np.float32)
        c[f"mo{b}T"] = np.ascontiguousarray(mo.T)
    pair = np.zeros((128, 64), np.float32)
    pair[r, r // 2] = 1.0
    c["pair"] = pair
    return c


def riota_table(cfg):
    HA = cfg["HA"]
    r = np.arange(128, dtype=np.int32)[:, None]
    h = np.repeat(np.arange(HA, dtype=np.int32), 6)[None, :]
    return (r * HA + h).astype(np.int32)


CONST_DT = dict(identb=BF16, identf=F32, u1=F32, sl1=F32, onesf=F32, onesb=BF16, trib=BF16,
                cosp=F32, sinp=F32, coss=F32, sins=F32, onehot=BF16, pair=F32,
                md8=F32, mo8T=F32, mo16T=F32, mo32T=F32, mo64T=F32)


def build(cfg, phases=("p0", "p1", "p2", "p3", "p4", "ps")):
    kb = KB(cfg)
    nc = kb.nc
    D, SEQ, HA, HB, PLE = cfg["D"], cfg["SEQ"], cfg["HA"], cfg["HB"], cfg["PLE"]
    KD = D // 128
    WA, WB = HA * 64, HB * 128
    CC = 3 * WB
    off = col_offsets(cfg)
    NIN = off["N_IN"]
    NT = SEQ // 128
    NS = cfg["NS"]
    SX = SEQ + 128

    def inp(name, shape, dt=F32):
        return kb.dram(name, shape, dt, kind="ExternalInput")

    def outp(name, shape, dt=F32):
        return kb.dram(name, shape, dt, kind="ExternalOutput")

    x = inp("x", [SX, D])
    p_in = inp("p", [SX, PLE])
    w_in = inp("w_in", [D, NIN])
    g_mix = inp("g_mix", [D])
    conv_w = inp("conv_w", [4, CC])
    a_log = inp("a_log", [HB])
    dt_bias = inp("dt_bias", [HB])
    g_onorm = inp("g_onorm", [128])
    w_pa = inp("w_pa", [WA, D])
    w_pb = inp("w_pb", [WB, D])
    w_o = inp("w_o", [D, D])
    g_ple = inp("g_ple", [D])
    w_pg = inp("w_pg", [D, D])
    w_ple = inp("w_ple", [PLE, D])
    g_final = inp("g_final", [D])
    cst = {k: inp("c_" + k, list(v.shape), CONST_DT[k]) for k, v in host_consts(cfg).items()}
    y_p = outp("y_p", [SX, D])
    k_p = outp("k_p", [SX, WA])
    v_p = outp("v_p", [SX, WA])
    s_p = outp("s_p", [HB, 128, 128])
    conv_p = outp("conv_p", [3, CC])
    w_in_b = kb.dram("w_in_b", [D, NIN], BF16)
    w_pa_b = kb.dram("w_pa_b", [WA, D], BF16)
    w_pb_b = kb.dram("w_pb_b", [WB, D], BF16)
    w_o_b = kb.dram("w_o_b", [D, D], BF16)
    w_pg_b = kb.dram("w_pg_b", [D, D], BF16)
    w_ple_b = kb.dram("w_ple_b", [PLE, D], BF16)
    Qtok = kb.dram("Qtok", [SX, WA], BF16)
    Ktok = kb.dram("Ktok", [SX, WA], BF16)
    Vtok = kb.dram("Vtok", [SX, WA], BF16)
    zaT = kb.dram("zaT", [WA, SX], BF16)
    zbs = kb.dram("zbs", [SX, WB], BF16)
    uT = kb.dram("uT", [CC, SX], BF16)
    sgT = kb.dram("sgT", [2 * D, SX], BF16)
    GBs = kb.dram("GBs", [SX, 2 * HB], F32)
    GaT = kb.dram("GaT", [WA, SX], BF16)
    GbT = kb.dram("GbT", [WB, SX], BF16)

    identb = kb.sb("identb", [128, 128], BF16)
    identf = kb.sb("identf", [128, 128], F32)
    u1 = kb.sb("u1", [128, 128], F32)
    sl1 = kb.sb("sl1", [128, 128], F32)
    onesf = kb.sb("onesf", [128, 128], F32)
    onesb = kb.sb("onesb", [128, 128], BF16)
    trib = kb.sb("trib", [128, 128], BF16)
    for nm, t in (("identb", identb), ("identf", identf), ("u1", u1), ("sl1", sl1), ("onesf", onesf),
                  ("onesb", onesb), ("trib", trib)):
        kb.dma(t[:], cst[nm], ["c_" + nm], ["k_" + nm])
    CN = ["k_identb", "k_identf", "k_u1", "k_sl1", "k_onesf", "k_onesb", "k_trib"]

    zt = kb.sb("zt", [128, 128], BF16)
    kb.op("pool", lambda g: g.memset(zt[:], 0.0), [], ["zt"])
    for r0 in range(0, WA, 128):
        kb.dma(GaT[r0:r0 + 128, SEQ:SX], zt[:], ["zt"], ["GaT"], q="pool")
    for r0 in range(0, WB, 128):
        kb.dma(GbT[r0:r0 + 128, SEQ:SX], zt[:], ["zt"], ["GbT"], q="pool")
    psf = [kb.ps(f"psf{i}", [128, 512], F32) for i in range(6)]
    psb = [kb.ps(f"psb{i}", [128, 1024], BF16) for i in range(2)]

    if "p0" in phases:
        kb.phase_begin()
        CW = 2048
        stg_f = [kb.sb(f"p0f{i}", [128, CW], F32) for i in range(2)]
        stg_b = [kb.sb(f"p0b{i}", [128, CW], BF16) for i in range(2)]
        it = 0
        for (src, dst, rows, cols, dn) in ((w_in, w_in_b, D, NIN, "w_in_b"), (w_pa, w_pa_b, WA, D, "w_pa_b"),
                                           (w_pb, w_pb_b, WB, D, "w_pb_b"), (w_o, w_o_b, D, D, "w_o_b"),
                                           (w_pg, w_pg_b, D, D, "w_pg_b"), (w_ple, w_ple_b, PLE, D, "w_ple_b")):
            for r0 in range(0, rows, 128):
                for c0 in range(0, cols, CW):
                    cw = min(CW, cols - c0)
                    i = it % 2
                    it += 1
                    kb.dma(stg_f[i][:, :cw], src[r0:r0 + 128, c0:c0 + cw], [], [f"p0f{i}"])
                    kb.cp(kb.rot(), stg_b[i][:, :cw], stg_f[i][:, :cw], [f"p0f{i}"], [f"p0b{i}"])
                    kb.dma(dst[r0:r0 + 128, c0:c0 + cw], stg_b[i][:, :cw], [f"p0b{i}"], [dn], q="pool")

    if "p0" in phases:
        kb.phase_end()
    ST = min(SEQ, 2048)
    if "p1" in phases:
        kb.phase_begin()
        hT = kb.sb("hT", [128, KD, ST], BF16)
        gmixT = kb.sb("gmixT", [128, KD], F32)
        kb.dma(gmixT[:], g_mix.rearrange("(k p) -> p k", p=128), [], ["gmixT"], slow=True)
        xt = [kb.sb(f"xt{i}", [128, D], F32) for i in range(2)]
        xn = [kb.sb(f"xn{i}", [128, D], BF16) for i in range(2)]
        junk = kb.sb("junk", [128, D], F32)
        ssq = [kb.sb(f"ssq{i}", [128, 1], F32) for i in range(2)]
        negA = kb.sb("negA", [128, HB], F32)
        dtb = kb.sb("dtb", [128, HB], F32)
        kb.dma(negA[:], a_log.partition_broadcast(128), [], ["negA"])
        kb.dma(dtb[:], dt_bias.partition_broadcast(128), [], ["dtb"])
        kb.act(negA[:], negA[:], AF.Exp, ["negA"], ["negA"])
        kb.ts("dve", negA[:], negA[:], -1.0, None, ALU.mult, None, ["negA"], ["negA"])
        cosb = [kb.sb(f"cosb{i}", [128, HA * 8], F32) for i in range(2)]
        sinb = [kb.sb(f"sinb{i}", [128, HA * 8], F32) for i in range(2)]
        GW = 512
        wg = [kb.sb(f"wg{i}", [128, KD, GW], BF16) for i in range(2)]
        fm_st = [kb.sb(f"fmst{i}", [128, 512], BF16) for i in range(3)]
        tm_f = [kb.sb(f"tmf{i}", [128, 512], F32) for i in range(2)]
        tm_b = [kb.sb(f"tmb{i}", [128, 512], BF16) for i in range(2)]
        rtmp = kb.sb("rtmp", [128, 4, HA * 8], F32)
        gbt = kb.sb("gbt", [128, 2 * HB], F32)
        gbt2 = kb.sb("gbt2", [128, HB], F32)
        convst = kb.sb("convst", [128, CC // 128, 3], F32)
        w_in_bv = w_in_b.rearrange("(k p) n -> p k n", p=128)

        fm_groups = [("za", off["za"], WA, zaT, 0), ("u", off["qb"], CC, uT, 0),
                     ("sg", off["ga"], 2 * D, sgT, 0)]
        tm_groups = [("qa", off["qa"], WA), ("ka", off["ka"], WA), ("va", off["va"], WA),
                     ("zb", off["zb"], WB), ("ab", off["beta"], 2 * HB)]
        wcount = 0
        fcount = 0
        tcount = 0
        for st0, sw in [(a, ST) for a in range(0, SEQ, ST)] + [(SEQ, 128)]:
            cwd = min(512, sw)
            for tt in range(sw // 128):
                i = tt % 2
                t0 = st0 + tt * 128
                kb.dma(xt[i][:], x[t0:t0 + 128, :], [], [f"xt{i}"])
                kb.act(junk[:], xt[i][:], AF.Square, [f"xt{i}"], ["junk", f"ssq{i}"], accum=ssq[i][:])
                kb.rsqrt(ssq[i][:], ssq[i][:], 1.0 / D, EPS, [f"ssq{i}"], [f"ssq{i}"])
                kb.act(xn[i][:], xt[i][:], AF.Copy, [f"xt{i}", f"ssq{i}"], [f"xn{i}"], scale=ssq[i][:])
                pb = psb[tt % 2]
                for k in range(KD):
                    kb.tr(pb[:, k * 128:(k + 1) * 128], xn[i][:, k * 128:(k + 1) * 128], identb[:],
                          [f"xn{i}", "k_identb"], [f"psb{tt % 2}"])
                for k in range(KD):
                    kb.ts(kb.rot(("dve", "pool")) if False else "dve", hT[:, k, tt * 128:(tt + 1) * 128],
                          pb[:, k * 128:(k + 1) * 128], gmixT[:, k:k + 1], None, ALU.mult, None,
                          [f"psb{tt % 2}", "gmixT"], ["hT"])
            for (kind, c0g, ncols, dst, r0d) in fm_groups:
                for g0 in range(0, ncols, GW):
                    gw = min(GW, ncols - g0)
                    wi = wcount % 2
                    wcount += 1
                    kb.dma(wg[wi][:, :, :gw], w_in_bv[:, :, c0g + g0:c0g + g0 + gw], ["w_in_b"], [f"wg{wi}"])
                    for u0 in range(0, gw, 128):
                        for tc0 in range(0, sw, cwd):
                            pi = fcount % 4
                            si = fcount % 3
                            fcount += 1
                            pt = psf[pi]
                            for k in range(KD):
                                kb.mm(pt[:, :cwd], wg[wi][:, k, u0:u0 + 128], hT[:, k, tc0:tc0 + cwd],
                                      [f"wg{wi}", "hT"], [f"psf{pi}"], start=(k == 0), stop=(k == KD - 1))
                            func = {"za": AF.Silu, "u": AF.Copy, "sg": AF.Sigmoid}[kind]
                            kb.act(fm_st[si][:, :cwd], pt[:, :cwd], func, [f"psf{pi}"], [f"fmst{si}"])
                            row = r0d + g0 + u0
                            if kind == "u" and st0 + tc0 + 512 == SEQ:
                                kb.cp("dve", convst[:, row // 128, :], pt[:, 509:512], [f"psf{pi}"], ["convst"])
                            kb.dma(dst[row:row + 128, st0 + tc0:st0 + tc0 + cwd], fm_st[si][:, :cwd],
                                   [f"fmst{si}"], [{"za": "zaT", "u": "uT", "sg": "sgT"}[kind]], q="pool")
            for (kind, c0g, ncols) in tm_groups:
                for g0 in range(0, ncols, GW):
                    gw = min(GW, ncols - g0)
                    wi = wcount % 2
                    wcount += 1
                    kb.dma(wg[wi][:, :, :gw], w_in_bv[:, :, c0g + g0:c0g + g0 + gw], ["w_in_b"], [f"wg{wi}"])
                    for tt in range(sw // 128):
                        t0 = st0 + tt * 128
                        pi = 4 + tcount % 2
                        fi = tcount % 2
                        tcount += 1
                        pt = psf[pi]
                        for k in range(KD):
                            kb.mm(pt[:, :gw], hT[:, k, tt * 128:(tt + 1) * 128], wg[wi][:, k, :gw],
                                  [f"wg{wi}", "hT"], [f"psf{pi}"], start=(k == 0), stop=(k == KD - 1))
                        PR, TF, TB = [f"psf{pi}"], [f"tmf{fi}"], [f"tmb{fi}"]
                        if kind in ("qa", "ka"):
                            kb.dma(cosb[fi][:], cst["cosp"][t0:t0 + 128, :], [], [f"cosb{fi}"])
                            kb.dma(sinb[fi][:], cst["sinp"][t0:t0 + 128, :], [], [f"sinb{fi}"])
                            kb.cp("act", tm_f[fi][:, :gw], pt[:, :gw], PR, TF)
                            pv = pt[:, :gw].rearrange("p (h d) -> p h d", d=64)
                            ov = tm_f[fi][:, :gw].rearrange("p (h d) -> p h d", d=64)
                            nh = gw // 64
                            h0 = g0 // 64
                            cv = cosb[fi][:, h0 * 8:(h0 + nh) * 8].rearrange("p (h d) -> p h d", d=8)
                            sv = sinb[fi][:, h0 * 8:(h0 + nh) * 8].rearrange("p (h d) -> p h d", d=8)
                            rv = [rtmp[:, j, :nh * 8].rearrange("p (h d) -> p h d", d=8) for j in range(4)]
                            CS = [f"cosb{fi}", f"sinb{fi}"]
                            _ord = [int(c) for c in _os.environ.get("RORD", "0123")]
                            _defs = {0: (rv[0], ov[:, :, 0:8], cv, "rtmp0"), 1: (rv[1], ov[:, :, 8:16], sv, "rtmp1"),
                                     2: (rv[2], ov[:, :, 8:16], cv, "rtmp2"), 3: (rv[3], ov[:, :, 0:8], sv, "rtmp3")}
                            for _o in _ord:
                                _a, _b, _c, _n = _defs[_o]
                                kb.tt("dve", _a, _b, _c, ALU.mult, TF + CS, [_n])
                            kb.tt("dve", ov[:, :, 0:8], rv[0], rv[1], ALU.subtract, ["rtmp0", "rtmp1"], TF)
                            kb.tt("dve", ov[:, :, 8:16], rv[2], rv[3], ALU.add, ["rtmp2", "rtmp3"], TF)
                            if kind == "qa":
                                kb.act(tm_b[fi][:, :gw], tm_f[fi][:, :gw], AF.Copy, TF, TB, scale=0.125)
                                kb.dma(Qtok[t0:t0 + 128, g0:g0 + gw], tm_b[fi][:, :gw], TB, ["Qtok"], q="pool")
                            else:
                                kb.cp("pool", tm_b[fi][:, :gw], tm_f[fi][:, :gw], TF, TB)
                                kb.dma(k_p[t0:t0 + 128, g0:g0 + gw], tm_f[fi][:, :gw], TF, ["k_p"], q="pool", final=True)
                                kb.dma(Ktok[t0:t0 + 128, g0:g0 + gw], tm_b[fi][:, :gw], TB, ["Ktok"], q="pool")
                        elif kind == "va":
                            kb.cp("act", tm_f[fi][:, :gw], pt[:, :gw], PR, TF)
                            kb.cp("dve", tm_b[fi][:, :gw], pt[:, :gw], PR, TB)
                            kb.dma(v_p[t0:t0 + 128, g0:g0 + gw], tm_f[fi][:, :gw], TF, ["v_p"], q="pool", final=True)
                            kb.dma(Vtok[t0:t0 + 128, g0:g0 + gw], tm_b[fi][:, :gw], TB, ["Vtok"], q="pool")
                        elif kind == "zb":
                            kb.act(tm_b[fi][:, :gw], pt[:, :gw], AF.Silu, PR, TB)
                            kb.dma(zbs[t0:t0 + 128, g0:g0 + gw], tm_b[fi][:, :gw], TB, ["zbs"], q="pool")
                        else:
                            kb.tt("dve", gbt2[:], pt[:, HB:2 * HB], dtb[:], ALU.add, PR + ["dtb"], ["gbt2"])
                            kb.act(gbt2[:], gbt2[:], AF.Exp, ["gbt2"], ["gbt2"])
                            kb.act(gbt2[:], gbt2[:], AF.Ln, ["gbt2"], ["gbt2"], bias=1.0)
                            kb.tt("dve", gbt[:, 0:HB], gbt2[:], negA[:], ALU.mult, ["gbt2", "negA", "gbt"], ["gbt"])
                            kb.act(gbt[:, HB:2 * HB], pt[:, 0:HB], AF.Sigmoid, PR + ["gbt"], ["gbt"])
                            kb.dma(GBs[t0:t0 + 128, :], gbt[:], ["gbt"], ["GBs"], q="pool")
        for u in range(CC // 128):
            kb.dma(conv_p[:, u * 128:(u + 1) * 128].rearrange("j p -> p j"), convst[:, u, :], ["convst"],
                   ["conv_p"], q="pool", final=True, slow=True)


    if "p1" in phases:
        kb.phase_end()
    if "p2" in phases:
        kb.phase_begin()
        NB = SEQ // 256
        NCH = SEQ // 512
        KA = kb.sb("KA", [128, NT, 96], BF16)
        QA = kb.sb("QA", [128, NT, 96], BF16)
        VA = kb.sb("VA", [128, NT, 65], BF16)
        KTa = kb.sb("KTa", [96, SEQ], BF16)
        kbf = kb.sb("kbf", [64, 32], F32)
        kbarT = kb.sb("kbarT", [64, 32], BF16)
        QTs = [kb.sb(f"QTs{i}", [64, 128], BF16) for i in range(2)]
        QTa = [kb.sb(f"QTa{i}", [96, 512], BF16) for i in range(2)]
        gs = kb.sb("gs", [128, 32], F32)
        top8 = kb.sb("top8", [128, 8], F32)
        PT = [kb.sb(f"PT{i}", [128, 512], BF16) for i in range(3)]
        osb = [kb.sb(f"osb{i}", [65, 512], F32) for i in range(2)]
        rec = [kb.sb(f"rec{i}", [65, 512], F32) for i in range(2)]
        zat = [kb.sb(f"zat{i}", [64, 512], BF16) for i in range(2)]
        otmp = [kb.sb(f"otmp{i}", [64, 512], F32) for i in range(2)]
        gat = [kb.sb(f"gat{i}", [64, 512], BF16) for i in range(2)]
        kb.dma(KA[:, :, 64:96], cst["onehot"], [], ["KA"])
        kb.op("pool", lambda g: g.memset(VA[:, :, 64:65], 1.0), [], ["VA"])
        scount = 0
        for h in range(HA):
            hs = slice(h * 64, (h + 1) * 64)
            kb.dma(KA[:, :, 0:64], Ktok[0:SEQ, hs].rearrange("(kt p) d -> p kt d", p=128), ["Ktok"], ["KA"])
            kb.dma(QA[:, :, 0:64], Qtok[0:SEQ, hs].rearrange("(kt p) d -> p kt d", p=128), ["Qtok"], ["QA"])
            kb.dma(VA[:, :, 0:64], Vtok[0:SEQ, hs].rearrange("(kt p) d -> p kt d", p=128), ["Vtok"], ["VA"])
            for kt0 in range(0, NT, 8):
                n8 = min(8, NT - kt0)
                bi = (kt0 // 8) % 2
                for j in range(n8):
                    kb.tr(psb[bi][0:96, j * 128:(j + 1) * 128], KA[:, kt0 + j, :], identb[:],
                          ["KA", "k_identb"], [f"psb{bi}"])
                kb.cp(kb.rot(("act", "dve")), KTa[:, kt0 * 128:(kt0 + n8) * 128], psb[bi][0:96, 0:n8 * 128],
                      [f"psb{bi}"], ["KTa"])
            kb.op("pool", lambda g: g.memset(kbf[:], 0.0), [], ["kbf"])
            kb.op("dve", lambda g: g.tensor_reduce(kbf[:, 0:NB], KTa[0:64, :].rearrange("p (n k) -> p n k", k=256),
                                                   AX.X, ALU.add), ["KTa"], ["kbf"])
            kb.cp("dve", kbarT[:], kbf[:], ["kbf"], ["kbarT"])
            kb.op("pool", lambda g: g.memset(gs[:], -1e30), [], ["gs"])
            for c in range(NCH):
                ci = c % 2
                q0 = c * 512
                for j in range(4):
                    qt = 4 * c + j
                    own = qt // 2
                    qi = qt % 2
                    bi = qt % 2
                    kb.tr(psb[bi][0:64, 0:128], QA[:, qt, 0:64], identb[:], ["QA", "k_identb"], [f"psb{bi}"])
                    kb.cp("act", QTs[qi][:], psb[bi][0:64, 0:128], [f"psb{bi}"], [f"QTs{qi}"])
                    mb = QA[:, qt, 64:96]
                    kb.op("pool", lambda g, mb=mb: g.memset(mb, NEG), [], ["QA"])
                    kb.op("pool", lambda g, mb=mb, own=own: g.memset(mb[:, own:own + 1], 0.0), [], ["QA"])
                    if 1 <= own <= 3:
                        kb.op("pool", lambda g, mb=mb, own=own: g.memset(mb[:, 0:own], 0.0), [], ["QA"])
                    elif own > 3:
                        kb.mm(psf[4][:, 0:32], QTs[qi][:], kbarT[:], [f"QTs{qi}", "kbarT"], ["psf4"])
                        kb.cp("dve", gs[:, 0:own], psf[4][:, 0:own], ["psf4"], ["gs"])
                        kb.op("dve", lambda g: g.max(top8[:], gs[:]), ["gs"], ["top8"])
                        kb.ts("dve", mb[:, 0:own], gs[:, 0:own], top8[:, 2:3], NEG, ALU.is_lt, ALU.mult,
                              ["gs", "top8"], ["QA"])
                    kb.tr(psb[bi][0:96, 128:256], QA[:, qt, :], identb[:], ["QA", "k_identb"], [f"psb{bi}"])
                    kb.cp("dve", QTa[ci][:, j * 128:(j + 1) * 128], psb[bi][0:96, 128:256], [f"psb{bi}"],
                          [f"QTa{ci}"])
                nk = 4 * c + 4
                oi = 2 + c % 2
                for kt in range(nk):
                    j0 = max(0, kt - 4 * c)
                    n = 512 - j0 * 128
                    si = scount % 2
                    pi = scount % 3
                    scount += 1
                    kb.mm(psf[si][:, 0:n], KTa[:, kt * 128:(kt + 1) * 128], QTa[ci][:, j0 * 128:512],
                          ["KTa", f"QTa{ci}"], [f"psf{si}"])
                    kb.act(PT[pi][:, 0:n], psf[si][:, 0:n], AF.Exp, [f"psf{si}"], [f"PT{pi}"])
                    if kt >= 4 * c:
                        kb.tt("pool", PT[pi][:, 0:128], PT[pi][:, 0:128], trib[:], ALU.mult,
                              [f"PT{pi}", "k_trib"], [f"PT{pi}"])
                    kb.mm(psf[oi][0:65, j0 * 128:512], VA[:, kt, :], PT[pi][:, 0:n], ["VA", f"PT{pi}"],
                          [f"psf{oi}"], start=(kt == 0), stop=(kt == nk - 1))
                kb.cp("act", osb[ci][:], psf[oi][0:65, :], [f"psf{oi}"], [f"osb{ci}"])
                kb.op("dve", lambda g, ci=ci: g.reciprocal(rec[ci][64:65, :], osb[ci][64:65, :]),
                      [f"osb{ci}"], [f"rec{ci}"])
                kb.mm(psf[5][0:64, :], onesf[64:65, 0:64], rec[ci][64:65, :], ["k_onesf", f"rec{ci}"], ["psf5"])
                kb.dma(zat[ci][:], zaT[hs, q0:q0 + 512], ["zaT"], [f"zat{ci}"])
                kb.tt("dve", otmp[ci][:], psf[5][0:64, :], osb[ci][0:64, :], ALU.mult, ["psf5", f"osb{ci}"],
                      [f"otmp{ci}"])
                kb.tt("pool", gat[ci][:], otmp[ci][:], zat[ci][:], ALU.mult, [f"otmp{ci}", f"zat{ci}"],
                      [f"gat{ci}"])
                kb.dma(GaT[hs, q0:q0 + 512], gat[ci][:], [f"gat{ci}"], ["GaT"], q="pool")
        kb.phase_end()


    QKVn = kb.dram("QKVn", [CC, SEQ], BF16)
    KnF = kb.dram("KnF", [WB, SEQ], F32)
    if "p3" in phases:
        kb.phase_begin()
        U = [kb.sb(f"U{i}", [128, 520], BF16) for i in range(2)]
        Yc = [kb.sb(f"Yc{i}", [128, 512], F32) for i in range(2)]
        sq = [kb.sb(f"sq{i}", [128, 512], BF16) for i in range(2)]
        rs = [kb.sb(f"rs{i}", [128, 512], F32) for i in range(2)]
        sq32 = [kb.sb(f"sq32{i}", [128, 512], F32) for i in range(2)]
        yb = [kb.sb(f"yb{i}", [128, 512], BF16) for i in range(2)]
        cw = [kb.sb(f"cw{i}", [128, 4], F32) for i in range(2)]
        uc = 0
        for _d in range(int(_os.environ.get("NDUM", "0"))):
            kb.op("dve", lambda g: g.memset(sq32[0][:, 0:8], 0.0), [], ["dummy"])
        for j in range(3):
            for h in range(HB):
                r0 = j * WB + h * 128
                wi = (j * HB + h) % 2
                kb.dma(cw[wi][:], conv_w[:, r0:r0 + 128].rearrange("i p -> p i"), [], [f"cw{wi}"], slow=True)
                for t0 in range(0, SEQ, 512):
                    i = uc % 2
                    uc += 1
                    if t0 == 0:
                        kb.op("pool", lambda g, i=i: g.memset(U[i][:, 0:3], 0.0), [], [f"U{i}"])
                        kb.dma(U[i][:, 3:515], uT[r0:r0 + 128, 0:512], ["uT"], [f"U{i}"])
                    else:
                        kb.dma(U[i][:, 0:515], uT[r0:r0 + 128, t0 - 3:t0 + 512], ["uT"], [f"U{i}"])
                    kb.ts("dve", Yc[i][:], U[i][:, 0:512], cw[wi][:, 0:1], None, ALU.mult, None,
                          [f"U{i}", f"cw{wi}"], [f"Yc{i}"])
                    for k in range(1, 4):
                        kb.stt("dve", Yc[i][:], U[i][:, k:k + 512], cw[wi][:, k:k + 1], Yc[i][:], ALU.mult, ALU.add,
                               [f"U{i}", f"cw{wi}", f"Yc{i}"], [f"Yc{i}"])
                    kb.act(Yc[i][:], Yc[i][:], AF.Silu, [f"Yc{i}"], [f"Yc{i}"])
                    if j == 2:
                        kb.cp("pool", yb[i][:], Yc[i][:], [f"Yc{i}"], [f"yb{i}"])
                    else:
                        pi = uc % 2
                        kb.act(sq[i][:], Yc[i][:], AF.Square, [f"Yc{i}"], [f"sq{i}"])
                        kb.mm(psf[pi][:, :], onesb[:], sq[i][:], ["k_onesb", f"sq{i}"], [f"psf{pi}"])
                        kb.ts("dve", rs[i][:], psf[pi][:, :], EPS, None, ALU.add, None, [f"psf{pi}"], [f"rs{i}"])
                        if _os.environ.get("RSV", "1") == "0":
                            kb.act(rs[i][:], rs[i][:], AF.Ln, [f"rs{i}"], [f"rs{i}"])
                            kb.act(rs[i][:], rs[i][:], AF.Exp, [f"rs{i}"], [f"rs{i}"], scale=-0.5)
                        else:
                            kb.act(sq32[i][:], rs[i][:], AF.Sqrt, [f"rs{i}"], [f"sq32{i}"])
                            kb.op("dve", lambda g, i=i: g.reciprocal(rs[i][:], sq32[i][:]), [f"sq32{i}"], [f"rs{i}"])
                        if j == 0:
                            kb.stt("dve", yb[i][:], Yc[i][:], 128.0 ** -0.5, rs[i][:], ALU.mult, ALU.mult,
                                   [f"Yc{i}", f"rs{i}"], [f"yb{i}"])
                        else:
                            kb.tt("dve", Yc[i][:], Yc[i][:], rs[i][:], ALU.mult, [f"Yc{i}", f"rs{i}"], [f"Yc{i}"])
                            kb.cp("pool", yb[i][:], Yc[i][:], [f"Yc{i}"], [f"yb{i}"])
                            kb.dma(KnF[h * 128:(h + 1) * 128, t0:t0 + 512], Yc[i][:], [f"Yc{i}"], ["KnF"], q="pool")
                    kb.dma(QKVn[r0:r0 + 128, t0:t0 + 512], yb[i][:], [f"yb{i}"], ["QKVn"], q="pool")
        kb.phase_end()
        P3CUT = int(_os.environ.get("P3CUT", "9"))

        kb.phase_begin()
        NHS = min(HB, 4)
        HBX = HB if P3CUT >= 2 else 0
        GBall = kb.sb("GBall", [128, NT, 2 * HB], F32)
        kb.dma(GBall[:], GBs[0:SEQ, :].rearrange("(c p) j -> p c j", p=128), ["GBs"], ["GBall"])
        gon = kb.sb("gon", [128, 128], F32)
        kb.dma(gon[:], g_onorm.partition_broadcast(128), [], ["gon"])

        def T_(nm, shape, dt):
            return [kb.sb(f"{nm}{i}", shape, dt) for i in range(NHS)]
        gH, bH, gcs, eg, etail, cdec, nbeg, nbet = [T_(n, [128, NT], F32) for n in
                                                   ("gH", "bH", "gcs", "eg", "etail", "cdec", "nbeg", "nbet")]
        S32 = [kb.sb(f"S32_{h}", [128, 128], F32) for h in range(HB)]
        Sbf = [kb.sb(f"Sbf_{h}", [128, 128], BF16) for h in range(HB)]
        KcT, QcT, VcT = T_("KcT", [128, 128], BF16), T_("QcT", [128, 128], BF16), T_("VcT", [128, 128], BF16)
        Ktail, bV = T_("Ktail", [128, 128], BF16), T_("bV", [128, 128], F32)
        KcF = T_("KcF", [128, 128], F32)
        An, AnT, T32, TT32 = T_("An", [128, 128], F32), T_("AnT", [128, 128], F32), T_("T32", [128, 128], F32), \
            T_("TT32", [128, 128], F32)
        Pb, PTb, Tb, TTb, AoT, M1b = [T_(n, [128, 128], BF16) for n in ("Pb", "PTb", "Tb", "TTb", "AoT", "M1b")]
        cm = {}
        for nm in ("md8", "mo8T", "mo16T", "mo32T", "mo64T"):
            cm[nm] = kb.sb("cm_" + nm, [128, 128], F32)
            kb.dma(cm[nm][:], cst[nm], [], ["k_" + nm])
        gU, Em, ETm = T_("gU", [128, 128], F32), T_("Em", [128, 128], F32), T_("ETm", [128, 128], F32)
        Pm, PTm = [T_("Pm_a", [128, 128], BF16), T_("Pm_b", [128, 128], BF16)], \
                  [T_("PTm_a", [128, 128], BF16), T_("PTm_b", [128, 128], BF16)]
        Y32, Ybf, attnT = T_("Y32", [128, 128], F32), T_("Ybf", [128, 128], BF16), T_("attnT", [128, 128], BF16)
        Rt, Vnb, O2s, Ot = T_("Rt", [128, 128], BF16), T_("Vnb", [128, 128], BF16), T_("O2s", [128, 128], F32), \
            T_("Ot", [128, 128], F32)
        oss, zbt, Gbt, GbTt = T_("oss", [128, 1], F32), T_("zbt", [128, 128], BF16), T_("Gbt", [128, 128], BF16), \
            T_("GbTt", [128, 128], BF16)
        ojunk = T_("ojunk", [128, 128], F32)
        pc = [0]

        def PS():
            pc[0] += 1
            i = pc[0] % 6
            return psf[i], f"psf{i}"

        def PB():
            pc[0] += 1
            i = pc[0] % 2
            return psb[i], f"psb{i}"

        for h in range(HB):
            kb.op("pool", lambda g, h=h: g.memset(S32[h][:], 0.0), [], [f"S32_{h}"])
            kb.op("pool", lambda g, h=h: g.memset(Sbf[h][:], 0.0), [], [f"Sbf_{h}"])
        for hg in range(0, HB, NHS):
            heads = list(range(hg, min(HB, hg + NHS)))
            for h in heads:
                s_ = h % NHS
                n_ = lambda nm: f"{nm}{s_}"
                kb.cp("dve", gH[s_][:], GBall[:, :, h], ["GBall"], [n_("gH")])
                kb.cp("dve", bH[s_][:], GBall[:, :, HB + h], ["GBall"], [n_("bH")])
                p1, p1n = PS()
                kb.mm(p1[:, 0:NT], u1[:], gH[s_][:], ["k_u1", n_("gH")], [p1n])
                kb.cp("act", gcs[s_][:], p1[:, 0:NT], [p1n], [n_("gcs")])
                p2, p2n = PS()
                kb.mm(p2[:, 0:NT], onesf[:], gH[s_][:], ["k_onesf", n_("gH")], [p2n])
                kb.act(eg[s_][:], gcs[s_][:], AF.Exp, [n_("gcs")], [n_("eg")])
                kb.act(cdec[s_][:], p2[:, 0:NT], AF.Exp, [p2n], [n_("cdec")])
                kb.tt("dve", etail[s_][:], p2[:, 0:NT], gcs[s_][:], ALU.subtract, [p2n, n_("gcs")], [n_("etail")])
                kb.act(etail[s_][:], etail[s_][:], AF.Exp, [n_("etail")], [n_("etail")])
                kb.stt("dve", nbeg[s_][:], eg[s_][:], -1.0, bH[s_][:], ALU.mult, ALU.mult, [n_("eg"), n_("bH")],
                       [n_("nbeg")])
                kb.ts("dve", nbet[s_][:], bH[s_][:], -1.0, None, ALU.mult, None, [n_("bH")], [n_("nbet")])
            for c in range(NT):
                t0 = c * 128
                for h in heads:
                    s_ = h % NHS
                    n_ = lambda nm, s_=s_: f"{nm}{s_}"
                    kb.dma(QcT[s_][:], QKVn[h * 128:(h + 1) * 128, t0:t0 + 128], ["QKVn"], [n_("QcT")])
                    kb.dma(KcT[s_][:], QKVn[WB + h * 128:WB + (h + 1) * 128, t0:t0 + 128], ["QKVn"], [n_("KcT")])
                    kb.dma(VcT[s_][:], QKVn[2 * WB + h * 128:2 * WB + (h + 1) * 128, t0:t0 + 128], ["QKVn"],
                           [n_("VcT")])
                    kb.dma(zbt[s_][:], zbs[t0:t0 + 128, h * 128:(h + 1) * 128], ["zbs"], [n_("zbt")])
                    kb.dma(KcF[s_][:], KnF[h * 128:(h + 1) * 128, t0:t0 + 128], ["KnF"], [n_("KcF")])
                    pb, pbn = PB()
                    kb.tr(pb[:, 0:128], KcT[s_][:], identb[:], [n_("KcT"), "k_identb"], [pbn])
                    kb.tr(pb[:, 128:256], VcT[s_][:], identb[:], [n_("VcT"), "k_identb"], [pbn])
                    kb.ts("dve", Ktail[s_][:], pb[:, 0:128], etail[s_][:, c:c + 1], None, ALU.mult, None,
                          [pbn, n_("etail")], [n_("Ktail")])
                    kb.ts("dve", bV[s_][:], pb[:, 128:256], bH[s_][:, c:c + 1], None, ALU.mult, None,
                          [pbn, n_("bH")], [n_("bV")])
                    kb.ts("dve", gU[s_][:], sl1[:], gH[s_][:, c:c + 1], None, ALU.mult, None,
                          ["k_sl1", n_("gH")], [n_("gU")])
                    pD, pDn = PS()
                    kb.mm(pD[:, 0:128], u1[:], gU[s_][:], ["k_u1", n_("gU")], [pDn])
                    kb.mm(pD[:, 128:256], gU[s_][:], u1[:], ["k_u1", n_("gU")], [pDn])
                    kb.act(Em[s_][:], pD[:, 0:128], AF.Exp, [pDn], [n_("Em")])
                    kb.act(ETm[s_][:], pD[:, 128:256], AF.Exp, [pDn], [n_("ETm")])
                    kb.tt("pool", Em[s_][:], Em[s_][:], sl1[:], ALU.mult, [n_("Em"), "k_sl1"], [n_("Em")])
                    kb.tt("pool", ETm[s_][:], ETm[s_][:], u1[:], ALU.mult, [n_("ETm"), "k_u1"], [n_("ETm")])
                    pG, pGn = PS()
                    kb.mm(pG[:, 0:128], KcF[s_][:], KcF[s_][:], [n_("KcF")], [pGn])
                    kb.mm(pG[:, 128:256], KcT[s_][:], QcT[s_][:], [n_("KcT"), n_("QcT")], [pGn])
                    kb.stt("dve", An[s_][:], pG[:, 0:128], nbet[s_][:, c:c + 1], Em[s_][:], ALU.mult, ALU.mult,
                           [pGn, n_("nbet"), n_("Em")], [n_("An")])
                    kb.tt("dve", attnT[s_][:], pG[:, 128:256], ETm[s_][:], ALU.mult, [pGn, n_("ETm")],
                          [n_("attnT")])
                    pA, pAn = PS()
                    kb.op("pe", lambda e, pA=pA, s_=s_: e.transpose(pA[:, 0:128], An[s_][:], identf[:]),
                          [n_("An"), "k_identf"], [pAn])
                    kb.cp("act", AnT[s_][:], pA[:, 0:128], [pAn], [n_("AnT")])
                    kb.tt("pool", Pb[s_][:], An[s_][:], cm["md8"][:], ALU.mult, [n_("An"), "k_md8"], [n_("Pb")])
                    kb.tt("pool", PTb[s_][:], AnT[s_][:], cm["md8"][:], ALU.mult, [n_("AnT"), "k_md8"], [n_("PTb")])
                    kb.tt("dve", T32[s_][:], Pb[s_][:], identf[:], ALU.add, [n_("Pb"), "k_identf"], [n_("T32")])
                    kb.tt("dve", TT32[s_][:], PTb[s_][:], identf[:], ALU.add, [n_("PTb"), "k_identf"], [n_("TT32")])
                    kb.cp("act", Tb[s_][:], T32[s_][:], [n_("T32")], [n_("Tb")])
                    kb.cp("act", TTb[s_][:], TT32[s_][:], [n_("TT32")], [n_("TTb")])
                    for jj in range(2):
                        pP, pPn = PS()
                        kb.mm(pP[:, 0:128], Pb[s_][:], PTb[s_][:], [n_("Pb"), n_("PTb")], [pPn])
                        kb.mm(pP[:, 128:256], PTb[s_][:], Pb[s_][:], [n_("Pb"), n_("PTb")], [pPn])
                        kb.cp("act", PTb[s_][:], pP[:, 0:128], [pPn], [n_("PTb")])
                        kb.cp("dve", Pb[s_][:], pP[:, 128:256], [pPn], [n_("Pb")])
                        pY, pYn = PS()
                        kb.mm(pY[:, 0:128], PTb[s_][:], Tb[s_][:], [n_("PTb"), n_("Tb")], [pYn])
                        kb.mm(pY[:, 128:256], Tb[s_][:], PTb[s_][:], [n_("PTb"), n_("Tb")], [pYn])
                        kb.tt("dve", T32[s_][:], pY[:, 0:128], T32[s_][:], ALU.add, [pYn, n_("T32")], [n_("T32")])
                        kb.tt("dve", TT32[s_][:], pY[:, 128:256], TT32[s_][:], ALU.add, [pYn, n_("TT32")],
                              [n_("TT32")])
                        kb.cp("act", Tb[s_][:], T32[s_][:], [n_("T32")], [n_("Tb")])
                        kb.cp("pool", TTb[s_][:], TT32[s_][:], [n_("TT32")], [n_("TTb")])
                    for li, b in enumerate((8, 16, 32, 64)):
                        last = (b == 64)
                        kb.tt("pool", AoT[s_][:], AnT[s_][:], cm[f"mo{b}T"][:], ALU.mult, [n_("AnT"), f"k_mo{b}T"],
                              [n_("AoT")])
                        pM, pMn = PS()
                        kb.mm(pM[:, 0:128], AoT[s_][:], Tb[s_][:], [n_("AoT"), n_("Tb")], [pMn])
                        kb.cp("act", M1b[s_][:], pM[:, 0:128], [pMn], [n_("M1b")])
                        pT, pTn = PS()
                        kb.mm(pT[:, 128:256], M1b[s_][:], TTb[s_][:], [n_("M1b"), n_("TTb")], [pTn])
                        if not last:
                            kb.mm(pT[:, 0:128], TTb[s_][:], M1b[s_][:], [n_("M1b"), n_("TTb")], [pTn])
                            kb.tt("dve", T32[s_][:], pT[:, 0:128], T32[s_][:], ALU.add, [pTn, n_("T32")],
                                  [n_("T32")])
                            kb.cp("act", Tb[s_][:], T32[s_][:], [n_("T32")], [n_("Tb")])
                        kb.tt("dve", TT32[s_][:], pT[:, 128:256], TT32[s_][:], ALU.add, [pTn, n_("TT32")],
                              [n_("TT32")])
                        if last:
                            kb.cp("pool", Ybf[s_][:], TT32[s_][:], [n_("TT32")], [n_("Ybf")])
                        else:
                            kb.cp("pool", TTb[s_][:], TT32[s_][:], [n_("TT32")], [n_("TTb")])
                pKS, pVn, pO1, pO2, pdS = {}, {}, {}, {}, {}
                for h in heads:
                    s_ = h % NHS
                    pKS[h] = PS()
                    kb.mm(pKS[h][0][:, 0:128], KcF[s_][:], S32[h][:], [f"KcF{s_}", f"S32_{h}"], [pKS[h][1]])
                    pO1[h] = PS()
                    kb.mm(pO1[h][0][:, 0:128], QcT[s_][:], Sbf[h][:], [f"QcT{s_}", f"Sbf_{h}"], [pO1[h][1]])
                    kb.stt("dve", Rt[s_][:], pKS[h][0][:, 0:128], nbeg[s_][:, c:c + 1], bV[s_][:], ALU.mult, ALU.add,
                           [pKS[h][1], f"nbeg{s_}", f"bV{s_}"], [f"Rt{s_}"])
                    kb.mm(pKS[h][0][:, 128:256], Ybf[s_][:], Rt[s_][:], [f"Ybf{s_}", f"Rt{s_}"], [pKS[h][1]])
                    kb.cp("act", Vnb[s_][:], pKS[h][0][:, 128:256], [pKS[h][1]], [f"Vnb{s_}"])
                    kb.mm(pO1[h][0][:, 128:256], attnT[s_][:], Vnb[s_][:], [f"attnT{s_}", f"Vnb{s_}"], [pO1[h][1]])
                    kb.mm(pKS[h][0][:, 256:384], Ktail[s_][:], Vnb[s_][:], [f"Ktail{s_}", f"Vnb{s_}"], [pKS[h][1]])
                    kb.cp("act", O2s[s_][:], pO1[h][0][:, 128:256], [pO1[h][1]], [f"O2s{s_}"])
                    kb.stt("dve", Ot[s_][:], pO1[h][0][:, 0:128], eg[s_][:, c:c + 1], O2s[s_][:], ALU.mult, ALU.add,
                           [pO1[h][1], f"eg{s_}", f"O2s{s_}"], [f"Ot{s_}"])
                    kb.stt("dve", S32[h][:], S32[h][:], cdec[s_][:, c:c + 1], pKS[h][0][:, 256:384], ALU.mult, ALU.add,
                           [f"S32_{h}", f"cdec{s_}", pKS[h][1]], [f"S32_{h}"])
                    kb.cp("act", Sbf[h][:], S32[h][:], [f"S32_{h}"], [f"Sbf_{h}"])
                    kb.act(ojunk[s_][:], Ot[s_][:], AF.Square, [f"Ot{s_}"], [f"ojunk{s_}", f"oss{s_}"],
                           accum=oss[s_][:])
                    kb.rsqrt(oss[s_][:], oss[s_][:], 1.0 / 128, EPS, [f"oss{s_}"], [f"oss{s_}"])
                    kb.stt("dve", Ot[s_][:], Ot[s_][:], oss[s_][:, 0:1], gon[:], ALU.mult, ALU.mult,
                           [f"Ot{s_}", f"oss{s_}", "gon"], [f"Ot{s_}"])
                    kb.tt("pool", Gbt[s_][:], Ot[s_][:], zbt[s_][:], ALU.mult, [f"Ot{s_}", f"zbt{s_}"], [f"Gbt{s_}"])
                    pb, pbn = PB()
                    kb.tr(pb[:, 0:128], Gbt[s_][:], identb[:], [f"Gbt{s_}", "k_identb"], [pbn])
                    kb.cp("act", GbTt[s_][:], pb[:, 0:128], [pbn], [f"GbTt{s_}"])
                    kb.dma(GbT[h * 128:(h + 1) * 128, t0:t0 + 128], GbTt[s_][:], [f"GbTt{s_}"], ["GbT"], q="pool")
        for h in range(HB):
            kb.dma(s_p[h], S32[h][:], [f"S32_{h}"], ["s_p"], q="pool", final=True)
        kb.phase_end()


    if "ps" in phases:
        kb.phase_begin()
        PAST, NPOOL = cfg["PAST"], cfg["NPOOL"]
        NPG = PAST // 128
        NBK = NPG // 2
        U32 = mybir.dt.uint32
        ck = inp("ck", [NPOOL, 128 * WA])
        cv = inp("cv", [NPOOL, 128 * WA])
        ptab = inp("ptab", [NS, NPG], I32)
        sg_in = inp("sg_in", [NS * HB, 128, 128])
        cb_in = inp("cb_in", [NS, 3, CC])
        riota_d = inp("riota", [128, HA * 6], I32)
        s_s = outp("s_s", [NS * HB, 128, 128])
        conv_s = outp("conv_s", [NS, 3, CC])
        selb = kb.dram("selb", [HA * 3, 1], I32)
        pgs = kb.dram("pgs", [1, HA * 6], I32)
        oas = kb.dram("oas", [NS, WA], F32)
        ck16 = ck.rearrange("n (j c) -> (n j) c", j=16)
        pidx = kb.sb("pidx", [128, 16], I32)
        ckrows = ck.rearrange("n (r h d) -> (n r h) d", h=HA, d=64)
        cvrows = cv.rearrange("n (r h d) -> (n r h) d", h=HA, d=64)
        pair = kb.sb("pair", [128, 64], F32)
        kb.dma(pair[:], cst["pair"], [], ["pair"])
        riota = kb.sb("riota_t", [128, HA * 6], I32)
        kb.dma(riota[:], riota_d, [], ["riota"])
        ptc = kb.sb("ptc", [128, 1], I32)
        Gp = [kb.sb(f"Gp{i}", [128, 8 * WA], F32) for i in range(2)]
        red = kb.sb("red", [128, WA], F32)
        acc = kb.sb("acc", [128, WA], F32)
        qb = kb.sb("qb", [128, WA], BF16)
        prod = kb.sb("prod", [128, WA], F32)
        gp = kb.sb("gp", [128, HA], F32)
        NG = max(NBK, 8)
        gsx = kb.sb("gsx", [HA, NG], F32)
        m8 = kb.sb("m8", [HA, 8], F32)
        i8 = kb.sb("i8", [HA, 8], U32)
        sel24 = kb.sb("sel24", [HA * 3, 1], I32)
        pg24 = kb.sb("pg24", [HA * 3, 2], I32)
        pgb = kb.sb("pgb", [128, HA * 6], I32)
        idx = kb.sb("idx", [128, HA * 6], I32)
        Kg = kb.sb("Kg", [128, HA * 6, 64], F32)
        Vg = kb.sb("Vg", [128, HA * 6, 64], F32)
        sc6 = kb.sb("sc6", [128, HA * 6], F32)
        e6 = kb.sb("e6", [128, HA * 6], F32)
        qrow = kb.sb("qrow", [1, WA], BF16)
        krow = kb.sb("krow", [1, WA], F32)
        vrow = kb.sb("vrow", [1, WA], F32)
        r1 = kb.sb("r1", [1, WA], F32)
        sn = kb.sb("sn", [1, HA], F32)
        den = kb.sb("den", [1, HA], F32)
        tot = kb.sb("tot", [1, HA * 6], F32)
        orow = kb.sb("orow", [1, WA], F32)
        ptv2 = ptab.rearrange("i (n e) -> (i n) e", e=2)
        for i in range(NS):
            kb.dma(ptc[0:NPG, :], ptab[i:i + 1, :].rearrange("o n -> n o"), [], ["ptc"], slow=True)
            for j in range(16):
                kb.ts("pool", pidx[0:NPG, j:j + 1], ptc[0:NPG, 0:1], 16, j, ALU.mult, ALU.add, ["ptc"], ["pidx"])
            for j in range(16):
                gi = j % 2
                kb.S.dma("pool", lambda e, gi=gi, j=j: e.indirect_dma_start(
                    out=Gp[gi][0:NPG, :], out_offset=None, in_=ck16,
                    in_offset=bass.IndirectOffsetOnAxis(ap=pidx[0:NPG, j:j + 1], axis=0)),
                    kb.bl(["pidx"]), kb.bl([f"Gp{gi}"]))
                if j == 0:
                    kb.op("dve", lambda g, gi=gi: g.tensor_reduce(
                        acc[0:NPG, :], Gp[gi][0:NPG, :].rearrange("p (r c) -> p c r", r=8), AX.X, ALU.add),
                        [f"Gp{gi}"], ["acc"])
                else:
                    kb.op("dve", lambda g, gi=gi: g.tensor_reduce(
                        red[0:NPG, :], Gp[gi][0:NPG, :].rearrange("p (r c) -> p c r", r=8), AX.X, ALU.add),
                        [f"Gp{gi}"], ["red"])
                    kb.tt("dve", acc[0:NPG, :], acc[0:NPG, :], red[0:NPG, :], ALU.add, ["acc", "red"], ["acc"])
            kb.dma(qb[:], Qtok[SEQ + i, :].partition_broadcast(128), ["Qtok"], ["qb"])
            kb.tt("dve", prod[0:NPG, :], acc[0:NPG, :], qb[0:NPG, :], ALU.mult, ["acc", "qb"], ["prod"])
            kb.op("dve", lambda g: g.tensor_reduce(gp[0:NPG, :], prod[0:NPG, :].rearrange("p (h d) -> p h d", d=64),
                                                   AX.X, ALU.add), ["prod"], ["gp"])
            kb.mm(psf[0][0:HA, 0:NBK], gp[0:NPG, :], pair[0:NPG, 0:NBK], ["gp", "pair"], ["psf0"])
            kb.op("pool", lambda g: g.memset(gsx[:], -1e30), [], ["gsx"])
            kb.cp("dve", gsx[:, 0:NBK], psf[0][0:HA, 0:NBK], ["psf0"], ["gsx"])
            kb.op("dve", lambda g: g.max(m8[:], gsx[:]), ["gsx"], ["m8"])
            kb.op("dve", lambda g: g.max_index(i8[:], m8[:], gsx[:]), ["gsx", "m8"], ["i8"])
            kb.dma(selb.rearrange("(h j) o -> h (j o)", j=3), i8[:, 0:3].bitcast(I32), ["i8"], ["selb"], q="pool")
            kb.dma(sel24[:], selb, ["selb"], ["sel24"])
            if i > 0:
                kb.ts("pool", sel24[:], sel24[:], i * NBK, None, ALU.add, None, ["sel24"], ["sel24"])
            kb.S.dma("pool", lambda e, i=i: e.indirect_dma_start(
                out=pg24[:], out_offset=None, in_=ptv2,
                in_offset=bass.IndirectOffsetOnAxis(ap=sel24[:, 0:1], axis=0)),
                kb.bl(["sel24"]), kb.bl(["pg24"]))
            kb.dma(pgs.rearrange("o (m e) -> (o m) e", e=2), pg24[:], ["pg24"], ["pgs"], q="pool")
            kb.dma(pgb[:], pgs[0, :].partition_broadcast(128), ["pgs"], ["pgb"])
            kb.ts("pool", idx[:], pgb[:], 128 * HA, None, ALU.mult, None, ["pgb"], ["idx"])
            kb.tt("pool", idx[:], idx[:], riota[:], ALU.add, ["idx", "riota"], ["idx"])
            for m in range(HA * 6):
                kb.S.dma("pool", lambda e, m=m: e.indirect_dma_start(
                    out=Kg[:, m, :], out_offset=None, in_=ckrows,
                    in_offset=bass.IndirectOffsetOnAxis(ap=idx[:, m:m + 1], axis=0)), kb.bl(["idx"]), kb.bl(["Kg"]))
                kb.S.dma("pool", lambda e, m=m: e.indirect_dma_start(
                    out=Vg[:, m, :], out_offset=None, in_=cvrows,
                    in_offset=bass.IndirectOffsetOnAxis(ap=idx[:, m:m + 1], axis=0)), kb.bl(["idx"]), kb.bl(["Vg"]))
            for h in range(HA):
                for je in range(6):
                    m = h * 6 + je
                    kb.tt("dve", Kg[:, m, :], Kg[:, m, :], qb[:, h * 64:(h + 1) * 64], ALU.mult, ["Kg", "qb"], ["Kg"])
            kb.op("dve", lambda g: g.tensor_reduce(sc6[:], Kg[:], AX.X, ALU.add), ["Kg"], ["sc6"])
            kb.act(e6[:], sc6[:], AF.Exp, ["sc6"], ["e6"])
            kb.dma(qrow[:], Qtok[SEQ + i:SEQ + i + 1, :], ["Qtok"], ["qrow"])
            kb.dma(krow[:], k_p[SEQ + i:SEQ + i + 1, :], ["k_p"], ["krow"])
            kb.dma(vrow[:], v_p[SEQ + i:SEQ + i + 1, :], ["v_p"], ["vrow"])
            kb.tt("dve", r1[:], krow[:], qrow[:], ALU.mult, ["krow", "qrow"], ["r1"])
            kb.op("dve", lambda g: g.tensor_reduce(sn[:], r1[:].rearrange("p (h d) -> p h d", d=64), AX.X, ALU.add),
                  ["r1"], ["sn"])
            kb.act(sn[:], sn[:], AF.Exp, ["sn"], ["sn"])
            kb.mm(psf[1][0:1, 0:HA * 6], onesf[:, 0:1], e6[:], ["k_onesf", "e6"], ["psf1"])
            kb.cp("act", tot[:], psf[1][0:1, 0:HA * 6], ["psf1"], ["tot"])
            kb.op("dve", lambda g: g.tensor_reduce(den[:], tot[:].rearrange("p (h j) -> p h j", j=6), AX.X, ALU.add),
                  ["tot"], ["den"])
            kb.tt("dve", den[:], den[:], sn[:], ALU.add, ["den", "sn"], ["den"])
            kb.op("dve", lambda g: g.reciprocal(den[:], den[:]), ["den"], ["den"])
            for h in range(HA):
                for je in range(6):
                    m = h * 6 + je
                    kb.mm(psf[2][0:1, h * 64:(h + 1) * 64], e6[:, m:m + 1], Vg[:, m, :], ["e6", "Vg"], ["psf2"],
                          start=(je == 0), stop=(je == 5))
            for h in range(HA):
                hs = slice(h * 64, (h + 1) * 64)
                kb.stt("dve", orow[:, hs], vrow[:, hs], sn[:, h:h + 1], psf[2][0:1, hs], ALU.mult, ALU.add,
                       ["vrow", "sn", "psf2", "orow"], ["orow"])
                kb.ts("dve", orow[:, hs], orow[:, hs], den[:, h:h + 1], None, ALU.mult, None, ["orow", "den"], ["orow"])
            kb.dma(oas[i:i + 1, :], orow[:], ["orow"], ["oas"], q="pool")
        KA_ = WA // 128
        oaT = kb.sb("oaT", [128, KA_, NS], F32)
        zaS = kb.sb("zaS", [128, KA_, NS], BF16)
        gaS = kb.sb("gaS", [128, KA_, NS], BF16)
        for k in range(KA_):
            kb.dma(oaT[:, k, :], oas[:, k * 128:(k + 1) * 128].rearrange("i p -> p i"), ["oas"], ["oaT"], slow=True)
            kb.dma(zaS[:, k, :], zaT[k * 128:(k + 1) * 128, SEQ:SEQ + NS], ["zaT"], ["zaS"], slow=True)
        kb.tt("dve", gaS[:], oaT[:], zaS[:], ALU.mult, ["oaT", "zaS"], ["gaS"])
        for k in range(KA_):
            kb.dma(GaT[k * 128:(k + 1) * 128, SEQ:SEQ + NS], gaS[:, k, :], ["gaS"], ["GaT"], q="pool", slow=True)

        NU = CC // 128
        M = HB * NS
        uS = kb.sb("uS", [128, NU, NS], BF16)
        uF = kb.sb("uF", [128, NU, NS], F32)
        cbS = kb.sb("cbS", [128, NU, NS, 3], F32)
        cwS = kb.sb("cwS", [128, NU, 4], F32)
        yS = kb.sb("yS", [128, NU, NS], F32)
        tS = kb.sb("tS", [128, NU, NS], F32)
        for u in range(NU):
            kb.dma(uS[:, u, :], uT[u * 128:(u + 1) * 128, SEQ:SEQ + NS], ["uT"], ["uS"], slow=True)
            kb.dma(cwS[:, u, :], conv_w[:, u * 128:(u + 1) * 128].rearrange("j p -> p j"), [], ["cwS"], slow=True)
            for i in range(NS):
                kb.dma(cbS[:, u, i, :], cb_in[i, :, u * 128:(u + 1) * 128].rearrange("j p -> p j"), [], ["cbS"],
                       slow=True)
        kb.cp("dve", uF[:], uS[:], ["uS"], ["uF"])
        for i in range(NS):
            kb.dma(conv_s[i, 0:2, :], cb_in[i, 1:3, :], [], ["conv_s"], q="pool", final=True)
            for u in range(NU):
                kb.dma(conv_s[i, 2:3, u * 128:(u + 1) * 128].rearrange("o p -> p o"), uF[:, u, i:i + 1], ["uF"],
                       ["conv_s"], q="pool", final=True, slow=True)
        for u in range(NU):
            kb.ts("dve", yS[:, u, :], uF[:, u, :], cwS[:, u, 3:4], None, ALU.mult, None, ["uF", "cwS"], ["yS"])
            for j in range(3):
                kb.stt("dve", yS[:, u, :], cbS[:, u, :, j], cwS[:, u, j:j + 1], yS[:, u, :], ALU.mult, ALU.add,
                       ["cbS", "cwS", "yS"], ["yS"])
        kb.act(yS[:], yS[:], AF.Silu, ["yS"], ["yS"])
        kb.tt("dve", tS[:, 0:2 * HB, :], yS[:, 0:2 * HB, :], yS[:, 0:2 * HB, :], ALU.mult, ["yS"], ["tS"])
        kb.mm(psf[3][:, 0:2 * M], onesf[:], tS[:, 0:2 * HB, :].rearrange("p u i -> p (u i)"), ["k_onesf", "tS"],
              ["psf3"])
        rS = kb.sb("rS", [128, 2 * HB, NS], F32)
        kb.ts("dve", rS[:].rearrange("p u i -> p (u i)"), psf[3][:, 0:2 * M], EPS, None, ALU.add, None, ["psf3"], ["rS"])
        kb.act(rS[:], rS[:], AF.Ln, ["rS"], ["rS"])
        kb.act(rS[:], rS[:], AF.Exp, ["rS"], ["rS"], scale=-0.5)
        kb.tt("dve", yS[:, 0:2 * HB, :], yS[:, 0:2 * HB, :], rS[:], ALU.mult, ["yS", "rS"], ["yS"])
        kb.ts("dve", yS[:, 0:HB, :], yS[:, 0:HB, :], 128.0 ** -0.5, None, ALU.mult, None, ["yS"], ["yS"])
        gbb = kb.sb("gbb", [128, NS, 2 * HB], F32)
        kb.dma(gbb[:].rearrange("p i j -> p (i j)"),
               GBs[SEQ:SEQ + NS, :].rearrange("i j -> (i j)").partition_broadcast(128), ["GBs"], ["gbb"])
        egb = kb.sb("egb", [128, NS, HB], F32)
        negb = kb.sb("negb", [128, NS, HB], F32)
        kb.act(egb[:], gbb[:, :, 0:HB], AF.Exp, ["gbb"], ["egb"])
        kb.ts("dve", negb[:], egb[:], -1.0, None, ALU.mult, None, ["egb"], ["negb"])
        S0 = [kb.sb(f"S0_{i}", [128, 128], F32) for i in range(M)] if M <= 8 else None
        Sn = [kb.sb(f"Sn{i}", [128, 128], F32) for i in range(2)]
        Sall = kb.sb("Sall", [128, M, 128], F32)
        dcol = kb.sb("dcol", [128, M], F32)
        kcol = kb.sb("kcol", [128, M], F32)
        ocol = kb.sb("ocol", [128, M], F32)
        for i in range(NS):
            for h in range(HB):
                m = i * HB + h
                kb.dma(Sall[:, m, :], sg_in[m], [], [f"Sall{m}"])
                kb.cp("pool", kcol[:, m:m + 1], yS[:, HB + h, i:i + 1], ["yS"], ["kcol"])
                kb.mm(psf[4][:, m:m + 1], Sall[:, m, :], yS[:, HB + h, i:i + 1], [f"Sall{m}", "yS"], ["psf4"])
                kb.stt("dve", dcol[:, m:m + 1], psf[4][:, m:m + 1], negb[:, i, h:h + 1], yS[:, 2 * HB + h, i:i + 1],
                       ALU.mult, ALU.add, ["psf4", "negb", "yS"], ["dcol"])
                kb.ts("dve", dcol[:, m:m + 1], dcol[:, m:m + 1], gbb[:, i, HB + h:HB + h + 1], None, ALU.mult, None,
                      ["dcol", "gbb"], ["dcol"])
        Krows = kb.sb("Krows", [M, 128], F32)
        Drows = kb.sb("Drows", [M, 128], F32)
        Dm = [kb.sb(f"Dm{i}", [M, 128], F32) for i in range(2)]
        pT0 = kb.ps("pT0", [128, 256], F32) if False else None
        kb.op("pe", lambda e: e.transpose(psf[5][0:M, 0:128], kcol[:], identf[:]), ["kcol", "k_identf"], ["psf5"])
        kb.op("pe", lambda e: e.transpose(psf[5][0:M, 128:256], dcol[:], identf[:]), ["dcol", "k_identf"], ["psf5"])
        kb.cp("act", Krows[:], psf[5][0:M, 0:128], ["psf5"], ["Krows"])
        kb.cp("act", Drows[:], psf[5][0:M, 128:256], ["psf5"], ["Drows"])
        for i in range(NS):
            for h in range(HB):
                m = i * HB + h
                di = m % 2
                kb.ts("dve", Dm[di][:], Drows[:], identf[0:M, m:m + 1], None, ALU.mult, None, ["Drows", "k_identf"],
                      [f"Dm{di}"])
                kb.mm(psf[di][:, 0:128], Krows[:], Dm[di][:], ["Krows", f"Dm{di}"], [f"psf{di}"])
                kb.stt("dve", Sn[di][:], Sall[:, m, :], egb[:, i, h:h + 1], psf[di][:, 0:128], ALU.mult, ALU.add,
                       [f"Sall{m}", "egb", f"psf{di}"], [f"Sn{di}"])
                kb.dma(s_s[m], Sn[di][:], [f"Sn{di}"], ["s_s"], q="pool", final=True)
                kb.mm(psf[3][:, m:m + 1], Sn[di][:], yS[:, h, i:i + 1], [f"Sn{di}", "yS"], ["psf3"])
        kb.cp("act", ocol[:], psf[3][:, 0:M], ["psf3"], ["ocol"])
        osq = kb.sb("osq", [128, M], F32)
        kb.tt("dve", osq[:], ocol[:], ocol[:], ALU.mult, ["ocol"], ["osq"])
        kb.mm(psf[4][:, 0:M], onesf[:], osq[:], ["k_onesf", "osq"], ["psf4"])
        orr = kb.sb("orr", [128, M], F32)
        kb.ts("dve", orr[:], psf[4][:, 0:M], 1.0 / 128, EPS, ALU.mult, ALU.add, ["psf4"], ["orr"])
        kb.act(orr[:], orr[:], AF.Ln, ["orr"], ["orr"])
        kb.act(orr[:], orr[:], AF.Exp, ["orr"], ["orr"], scale=-0.5)
        gcol = kb.sb("gcol", [128, 1], F32)
        kb.dma(gcol[:], g_onorm.rearrange("(p o) -> p o", o=1), [], ["gcol"], slow=True)
        kb.stt("dve", ocol[:], ocol[:], gcol[:, 0:1], orr[:], ALU.mult, ALU.mult, ["ocol", "gcol", "orr"], ["ocol"])
        zbS = kb.sb("zbS", [128, NS, HB], BF16)
        for i in range(NS):
            kb.dma(zbS[:, i, :], zbs[SEQ + i, :].rearrange("(h p) -> p h", p=128), ["zbs"], ["zbS"], slow=True)
        gbS = kb.sb("gbS", [128, NS, HB], BF16)
        kb.tt("dve", gbS[:].rearrange("p i h -> p (i h)"), ocol[:], zbS[:].rearrange("p i h -> p (i h)"), ALU.mult,
              ["ocol", "zbS"], ["gbS"])
        for h in range(HB):
            kb.dma(GbT[h * 128:(h + 1) * 128, SEQ:SEQ + NS], gbS[:, :, h], ["gbS"], ["GbT"], q="pool", slow=True)
        kb.phase_end()

    if "p4" in phases:
        kb.phase_begin()
        KA_, KB_, KP_ = WA // 128, WB // 128, PLE // 128
        wpa = kb.sb("wpa", [128, KA_, D], BF16)
        wpb = kb.sb("wpb", [128, KB_, D], BF16)
        wo = kb.sb("wo", [128, KD, D], BF16)
        wpg = kb.sb("wpg", [128, KD, D], BF16)
        wpl = kb.sb("wpl", [128, KP_, D], BF16)
        for t_, src, nm in ((wpa, w_pa_b, "w_pa_b"), (wpb, w_pb_b, "w_pb_b"), (wo, w_o_b, "w_o_b"),
                            (wpg, w_pg_b, "w_pg_b"), (wpl, w_ple_b, "w_ple_b")):
            kb.dma(t_[:], src.rearrange("(k p) n -> p k n", p=128), [nm], ["w4_" + nm])
        W4 = ["w4_w_pa_b", "w4_w_pb_b", "w4_w_o_b", "w4_w_pg_b", "w4_w_ple_b"]
        gple = kb.sb("gple", [128, D], F32)
        gfin = kb.sb("gfin", [128, D], F32)
        kb.dma(gple[:], g_ple.partition_broadcast(128), [], ["gple"])
        kb.dma(gfin[:], g_final.partition_broadcast(128), [], ["gfin"])
        GaTt = [kb.sb(f"GaTt{i}", [128, KA_, 512], BF16) for i in range(2)]
        GbTt4 = [kb.sb(f"GbTt4{i}", [128, KB_, 512], BF16) for i in range(2)]
        sga = [kb.sb(f"sga{i}", [128, 512], BF16) for i in range(2)]
        sgb = [kb.sb(f"sgb{i}", [128, 512], BF16) for i in range(2)]
        m1 = [kb.sb(f"m1{i}", [128, 512], F32) for i in range(2)]
        m2 = [kb.sb(f"m2{i}", [128, 512], F32) for i in range(2)]
        mixT = [kb.sb(f"mixT{i}", [128, KD, 512], BF16) for i in range(2)]
        x1 = [kb.sb(f"x1{i}", [128, D], F32) for i in range(2)]
        x2 = [kb.sb(f"x2{i}", [128, D], F32) for i in range(2)]
        hpb = [kb.sb(f"hpb{i}", [128, D], BF16) for i in range(2)]
        hpT = [kb.sb(f"hpT{i}", [128, KD, 128], BF16) for i in range(2)]
        pt_ = [kb.sb(f"pt{i}", [128, PLE], F32) for i in range(2)]
        ptb = [kb.sb(f"ptb{i}", [128, PLE], BF16) for i in range(2)]
        pTt = [kb.sb(f"pTt{i}", [128, KP_, 128], BF16) for i in range(2)]
        g2 = [kb.sb(f"g2{i}", [128, 512], F32) for i in range(2)]
        j4 = kb.sb("j4", [128, D], F32)
        ss4 = [kb.sb(f"ss4{i}", [128, 1], F32) for i in range(2)]
        yt = [kb.sb(f"yt{i}", [128, D], F32) for i in range(2)]
        sc = 0
        tc = 0
        pcn = [0]

        def PS4():
            pcn[0] += 1
            i = pcn[0] % 6
            return psf[i], f"psf{i}"
        for st0, sw in [(a, min(512, SEQ - a)) for a in range(0, SEQ, 512)] + [(SEQ, 128)]:
            i = sc % 2
            sc += 1
            kb.dma(GaTt[i][:, :, :sw], GaT.rearrange("(k p) t -> p k t", p=128)[:, :, st0:st0 + sw], ["GaT"],
                   [f"GaTt{i}"])
            kb.dma(GbTt4[i][:, :, :sw], GbT.rearrange("(k p) t -> p k t", p=128)[:, :, st0:st0 + sw], ["GbT"],
                   [f"GbTt4{i}"])
            for cu in range(KD):
                ui = cu % 2
                cs_ = slice(cu * 128, (cu + 1) * 128)
                kb.dma(sga[ui][:, :sw], sgT[cu * 128:(cu + 1) * 128, st0:st0 + sw], ["sgT"], [f"sga{ui}"])
                kb.dma(sgb[ui][:, :sw], sgT[D + cu * 128:D + (cu + 1) * 128, st0:st0 + sw], ["sgT"], [f"sgb{ui}"])
                pa, pan = PS4()
                for k in range(KA_):
                    kb.mm(pa[:, :sw], wpa[:, k, cs_], GaTt[i][:, k, :sw], W4 + [f"GaTt{i}"], [pan],
                          start=(k == 0), stop=(k == KA_ - 1))
                pb_, pbn_ = PS4()
                for k in range(KB_):
                    kb.mm(pb_[:, :sw], wpb[:, k, cs_], GbTt4[i][:, k, :sw], W4 + [f"GbTt4{i}"], [pbn_],
                          start=(k == 0), stop=(k == KB_ - 1))
                kb.tt("dve", m1[ui][:, :sw], pa[:, :sw], sga[ui][:, :sw], ALU.mult, [pan, f"sga{ui}"], [f"m1{ui}"])
                kb.tt("dve", m2[ui][:, :sw], pb_[:, :sw], sgb[ui][:, :sw], ALU.mult, [pbn_, f"sgb{ui}"], [f"m2{ui}"])
                kb.tt("pool", mixT[i][:, cu, :sw], m1[ui][:, :sw], m2[ui][:, :sw], ALU.add, [f"m1{ui}", f"m2{ui}"],
                      [f"mixT{i}"])
            for tt in range(sw // 128):
                t0 = st0 + tt * 128
                j = tc % 2
                tc += 1
                ts_ = slice(tt * 128, (tt + 1) * 128)
                kb.dma(x1[j][:], x[t0:t0 + 128, :], [], [f"x1{j}"])
                kb.dma(pt_[j][:], p_in[t0:t0 + 128, :], [], [f"pt{j}"])
                for n0 in range(0, D, 512):
                    nw = min(512, D - n0)
                    po, pon = PS4()
                    for k in range(KD):
                        kb.mm(po[:, :nw], mixT[i][:, k, ts_], wo[:, k, n0:n0 + nw], W4 + [f"mixT{i}"], [pon],
                              start=(k == 0), stop=(k == KD - 1))
                    kb.tt("dve", x1[j][:, n0:n0 + nw], po[:, :nw], x1[j][:, n0:n0 + nw], ALU.add, [pon, f"x1{j}"],
                          [f"x1{j}"])
                kb.act(j4[:], x1[j][:], AF.Square, [f"x1{j}"], ["j4", f"ss4{j}"], accum=ss4[j][:])
                kb.rsqrt(ss4[j][:], ss4[j][:], 1.0 / D, EPS, [f"ss4{j}"], [f"ss4{j}"])
                kb.stt("dve", hpb[j][:], x1[j][:], ss4[j][:, 0:1], gple[:], ALU.mult, ALU.mult,
                       [f"x1{j}", f"ss4{j}", "gple"], [f"hpb{j}"])
                kb.cp("pool", ptb[j][:], pt_[j][:], [f"pt{j}"], [f"ptb{j}"])
                bi = tc % 2
                for k in range(KD):
                    kb.tr(psb[bi][:, k * 128:(k + 1) * 128], hpb[j][:, k * 128:(k + 1) * 128], identb[:],
                          [f"hpb{j}", "k_identb"], [f"psb{bi}"])
                kb.cp("act", hpT[j][:].rearrange("p k t -> p (k t)"), psb[bi][:, 0:KD * 128], [f"psb{bi}"],
                      [f"hpT{j}"])
                for k in range(KP_):
                    kb.tr(psb[bi][:, k * 128:(k + 1) * 128], ptb[j][:, k * 128:(k + 1) * 128], identb[:],
                          [f"ptb{j}", "k_identb", f"hpT{j}"], [f"psb{bi}"])
                kb.cp("act", pTt[j][:].rearrange("p k t -> p (k t)"), psb[bi][:, 0:KP_ * 128], [f"psb{bi}"],
                      [f"pTt{j}"])
                for n0 in range(0, D, 512):
                    nw = min(512, D - n0)
                    pg_, pgn = PS4()
                    for k in range(KD):
                        kb.mm(pg_[:, :nw], hpT[j][:, k, :], wpg[:, k, n0:n0 + nw], W4 + [f"hpT{j}"], [pgn],
                              start=(k == 0), stop=(k == KD - 1))
                    pl_, pln = PS4()
                    for k in range(KP_):
                        kb.mm(pl_[:, :nw], pTt[j][:, k, :], wpl[:, k, n0:n0 + nw], W4 + [f"pTt{j}"], [pln],
                              start=(k == 0), stop=(k == KP_ - 1))
                    kb.act(g2[j][:, :nw], pg_[:, :nw], AF.Sigmoid, [pgn], [f"g2{j}"])
                    kb.tt("dve", g2[j][:, :nw], pl_[:, :nw], g2[j][:, :nw], ALU.mult, [pln, f"g2{j}"], [f"g2{j}"])
                    kb.tt("pool", x2[j][:, n0:n0 + nw], x1[j][:, n0:n0 + nw], g2[j][:, :nw], ALU.add,
                          [f"x1{j}", f"g2{j}"], [f"x2{j}"])
                kb.act(j4[:], x2[j][:], AF.Square, [f"x2{j}"], ["j4", f"ss4{j}"], accum=ss4[j][:])
                kb.rsqrt(ss4[j][:], ss4[j][:], 1.0 / D, EPS, [f"ss4{j}"], [f"ss4{j}"])
                kb.stt("dve", yt[j][:], x2[j][:], ss4[j][:, 0:1], gfin[:], ALU.mult, ALU.mult,
                       [f"x2{j}", f"ss4{j}", "gfin"], [f"yt{j}"])
                kb.dma(y_p[t0:t0 + 128, :], yt[j][:], [f"yt{j}"], ["y_p"], q="pool", final=True)
        kb.phase_end()


    print('total ops', getattr(kb.S, 'total', 0))
    kb.S.barrier()
    kb.S.emit()
    kb.stack.close()
    return nc


_CACHE = {}


def kernel(x_prompt, x_sample, p_prompt, p_sample, cache_k, cache_v, page_table, state_gdn_s, state_gdn_conv,
           g_mix, w_in, conv_w, a_log, dt_bias, g_onorm, w_pa, w_pb, w_o, g_ple, w_ple_gate, w_ple, g_final):
    cfg = FULL
    NC_, NS, SEQ, D, PLE = cfg["NCORES"], cfg["NS"], cfg["SEQ"], cfg["D"], cfg["PLE"]
    HA, HB = cfg["HA"], cfg["HB"]
    WA, WB = HA * 64, HB * 128
    CC = 3 * WB
    if "nc" not in _CACHE:
        _CACHE["nc"] = build(cfg)
    nc = _CACHE["nc"]
    f32 = lambda a: np.ascontiguousarray(np.asarray(a), dtype=np.float32)
    consts = host_consts(cfg)
    rio = riota_table(cfg)
    ck = f32(cache_k[0]).reshape(cfg["NPOOL"], 128 * WA)
    cv = f32(cache_v[0]).reshape(cfg["NPOOL"], 128 * WA)
    xp, xs, pp, ps_ = f32(x_prompt), f32(x_sample), f32(p_prompt[0]), f32(p_sample[0])
    pt = np.ascontiguousarray(np.asarray(page_table), dtype=np.int32)
    sg = f32(state_gdn_s[0])
    cb = f32(state_gdn_conv[0])
    shared = dict(w_in=f32(w_in[0]), g_mix=f32(g_mix[0]), conv_w=f32(conv_w[0]), a_log=f32(a_log[0]),
                  dt_bias=f32(dt_bias[0]), g_onorm=f32(g_onorm[0]), w_pa=f32(w_pa[0]), w_pb=f32(w_pb[0]),
                  w_o=f32(w_o[0]), g_ple=f32(g_ple[0]), w_pg=f32(w_ple_gate[0]), w_ple=f32(w_ple[0]),
                  g_final=f32(g_final), ck=ck, cv=cv, riota=rio)
    for k, v in consts.items():
        shared["c_" + k] = v
    in_maps = []
    for c in range(NC_):
        b = c % 2
        sl = slice(c * NS, (c + 1) * NS)
        xe = np.zeros((SEQ + 128, D), np.float32)
        xe[:SEQ] = xp[b]
        xe[SEQ:SEQ + NS] = xs[sl, 0]
        pe = np.zeros((SEQ + 128, PLE), np.float32)
        pe[:SEQ] = pp[b]
        pe[SEQ:SEQ + NS] = ps_[sl, 0]
        m = dict(shared)
        m.update(x=xe, p=pe, ptab=np.ascontiguousarray(pt[sl]), sg_in=np.ascontiguousarray(sg[sl]).reshape(NS * HB, 128, 128),
                 cb_in=np.ascontiguousarray(cb[sl]))
        in_maps.append(m)
    res = run_bass_kernel_spmd(nc, in_maps, core_ids=list(range(NC_))).results
    B = 2
    y_prompt = np.stack([res[b]["y_p"][:SEQ] for b in range(B)])
    y_sample = np.concatenate([res[c]["y_p"][SEQ:SEQ + NS] for c in range(NC_)])[:, None, :]
    k_prompt = np.stack([res[b]["k_p"][:SEQ] for b in range(B)]).reshape(1, B, SEQ, HA, 64)
    v_prompt = np.stack([res[b]["v_p"][:SEQ] for b in range(B)]).reshape(1, B, SEQ, HA, 64)
    s_prompt = np.stack([res[b]["s_p"] for b in range(B)])[None]
    conv_prompt = np.stack([res[b]["conv_p"] for b in range(B)])[None]
    k_sample = np.concatenate([res[c]["k_p"][SEQ:SEQ + NS] for c in range(NC_)]).reshape(1, NC_ * NS, 1, HA, 64)
    v_sample = np.concatenate([res[c]["v_p"][SEQ:SEQ + NS] for c in range(NC_)]).reshape(1, NC_ * NS, 1, HA, 64)
    s_sample = np.concatenate([res[c]["s_s"] for c in range(NC_)]).reshape(1, NC_ * NS, HB, 128, 128)
    conv_sample = np.concatenate([res[c]["conv_s"] for c in range(NC_)])[None]
    return tuple(np.ascontiguousarray(a, dtype=np.float32) for a in
                 (y_prompt, y_sample, k_prompt, v_prompt, s_prompt, conv_prompt, k_sample, v_sample, s_sample,
                  conv_sample))
```

```python
import contextlib
import numpy as np
import ml_dtypes
import concourse.bass as bass
import concourse.mybir as mybir
from concourse.bass_utils import run_bass_kernel_spmd

F32 = mybir.dt.float32
BF16 = mybir.dt.bfloat16
I32 = mybir.dt.int32
AF = mybir.ActivationFunctionType
ALU = mybir.AluOpType
AX = mybir.AxisListType

FULL = dict(D=1024, SEQ=8192, HA=8, HB=8, PLE=256, PAST=16384, NPOOL=5120, NS=4, NCORES=8)
EPS = 1e-6
import os as _os
LIMIT = int(_os.environ.get('KLIMIT', '1000000000'))
NEG = -30000.0


class Buf:
    __slots__ = ("w", "r", "name")

    def __init__(self, name=""):
        self.w = None
        self.r = {}
        self.name = name


class Sched:
    ENGS = ("pe", "act", "dve", "pool", "sp")
    NRING = 8

    def __init__(self, nc, stack):
        self.nc = nc
        self.stack = stack
        self.ops = {e: [] for e in self.ENGS}
        self.sem = {}
        self.cnt = {}
        self.seen = {e: {} for e in self.ENGS}
        self.nsem = 0
        for e in ("pe", "act", "dve", "pool"):
            self._newsem(e)
        self.ring = {}
        self.ringcnt = {}
        self.ringpos = {}
        for q in ("sp", "pool", "act"):
            self.ring[q] = [stack.enter_context(nc.semaphore(f"dq_{q}_{i}")) for i in range(self.NRING)]
            self.ringcnt[q] = [0] * self.NRING
            self.ringpos[q] = 0
        self.final = []

    def _newsem(self, e):
        self.nsem += 1
        self.sem[e] = self.stack.enter_context(self.nc.semaphore(f"s_{e}_{self.nsem}"))
        self.cnt[e] = 0

    def _waits(self, e, R, W):
        need = {}

        def add(tok):
            if tok is None:
                return
            s, v = tok
            if need.get(s, (None, 0))[1] < v:
                need[s] = (s, v)

        for b in R:
            add(b.w)
        for b in W:
            add(b.w)
            for tok in b.r.values():
                add(tok)
        out = []
        seen = self.seen[e]
        for s, v in need.values():
            if seen.get(id(s), 0) >= v:
                continue
            seen[id(s)] = v
            out.append((s, v))
        return out

    def _commit(self, tok, R, W):
        for b in W:
            b.w = tok
            b.r = {}
        for b in R:
            if b in W:
                continue
            old = b.r.get(id(tok[0]))
            if old is None or old[1] < tok[1]:
                b.r[id(tok[0])] = tok

    def op(self, e, fn, R=(), W=()):
        self.total = getattr(self, "total", 0) + 1
        if self.total > LIMIT:
            return None
        waits = self._waits(e, R, W)
        if self.cnt[e] >= 30000:
            self._newsem(e)
        self.cnt[e] += 1
        tok = (self.sem[e], self.cnt[e])
        self.ops[e].append((waits, fn, tok[0], 1))
        self._commit(tok, R, W)
        return tok

    def dma(self, q, fn, R=(), W=(), final=False):
        self.total = getattr(self, "total", 0) + 1
        if self.total > LIMIT:
            return None
        waits = self._waits(q, R, W)
        i = self.ringpos[q]
        self.ringpos[q] = (i + 1) % self.NRING
        s = self.ring[q][i]
        prev = 16 * self.ringcnt[q][i]
        if prev > 0 and self.seen[q].get(id(s), 0) < prev:
            self.seen[q][id(s)] = prev
            waits.append((s, prev))
        self.ringcnt[q][i] += 1
        tok = (s, 16 * self.ringcnt[q][i])
        self.ops[q].append((waits, fn, s, 16))
        self._commit(tok, R, W)
        if final:
            self.final.append(tok)
        return tok

    def barrier(self):
        toks = []
        for e in ("pe", "act", "dve", "pool"):
            if self.cnt[e] > 0:
                toks.append((self.sem[e], self.cnt[e]))
        for q in self.ring:
            for i, s in enumerate(self.ring[q]):
                if self.ringcnt[q][i] > 0:
                    toks.append((s, 16 * self.ringcnt[q][i]))
        for e in self.ENGS:
            w = []
            for s, v in toks:
                if self.seen[e].get(id(s), 0) < v:
                    self.seen[e][id(s)] = v
                    w.append((s, v))
            self.ops[e].append((w, None, None, 0))

    def emit(self):
        nc = self.nc
        fin = {}
        for s, v in self.final:
            if fin.get(id(s), (None, 0))[1] < v:
                fin[id(s)] = (s, v)
        for q in self.ring:
            for i, s in enumerate(self.ring[q]):
                v = 16 * self.ringcnt[q][i]
                if v > 0:
                    fin[id(s)] = (s, v)
        lists = self.ops

        def run(eng, lst, tail=None):
            for waits, fn, s, inc in lst:
                for ws, wv in waits:
                    eng.wait_ge(ws, wv)
                if fn is not None:
                    fn(eng).then_inc(s, inc)
            if tail:
                for ws, wv in tail:
                    eng.wait_ge(ws, wv)

        with nc.Block() as block:
            @block.tensor
            def _(e):
                run(e, lists["pe"])

            @block.scalar
            def _(e):
                run(e, lists["act"])

            @block.vector
            def _(e):
                run(e, lists["dve"])

            @block.gpsimd
            def _(e):
                run(e, lists["pool"])

            @block.sync
            def _(e):
                run(e, lists["sp"], tail=list(fin.values()))


import threading


class RR:
    def __init__(self, fns, kb):
        self.fns, self.kb, self.n = fns, kb, len(fns)
        self.sems = [threading.Semaphore(0) for _ in fns]
        self.done = [False] * self.n
        self.tls = threading.local()
        self.err = None

    def _next(self, i):
        for k in range(1, self.n + 1):
            j = (i + k) % self.n
            if not self.done[j]:
                self.sems[j].release()
                return True
        return False

    def _wrap(self, i):
        self.sems[i].acquire()
        self.tls.idx = i
        try:
            self.fns[i]()
        except BaseException as e:
            self.err = e
        finally:
            self.done[i] = True
            self._next(i)

    def yield_point(self):
        i = getattr(self.tls, "idx", None)
        if i is None:
            return
        if all(self.done[j] for j in range(self.n) if j != i):
            return
        self._next(i)
        self.sems[i].acquire()

    def run(self):
        if self.n == 0:
            return
        self.kb.rrs = self
        ths = [threading.Thread(target=self._wrap, args=(i,)) for i in range(self.n)]
        for t in ths:
            t.start()
        self.sems[0].release()
        for t in ths:
            t.join()
        self.kb.rrs = None
        if self.err is not None:
            raise self.err


class KB:
    def __init__(self, cfg):
        self.cfg = cfg
        self.nc = bass.Bass("TRN2", target_bir_lowering=False)
        self.stack = contextlib.ExitStack()
        self.S = Sched(self.nc, self.stack)
        self.bufs = {}
        self.rr = 0
        self.pstack = None
        self.rrs = None

    def dram(self, name, shape, dt, kind="Internal"):
        t = self.nc.dram_tensor(name, list(shape), dt, kind=kind).ap()
        self.bufs[name] = Buf(name)
        return t

    def sb(self, name, shape, dt):
        st = self.pstack if self.pstack is not None else self.stack
        t = st.enter_context(self.nc.sbuf_tensor(name, list(shape), dt))
        return t

    def phase_begin(self):
        self.pstack = contextlib.ExitStack()

    def phase_end(self):
        self.S.barrier()
        self.pstack.close()
        self.pstack = None

    def ps(self, name, shape, dt):
        t = self.stack.enter_context(self.nc.psum_tensor(name, list(shape), dt))
        return t

    def B(self, name):
        b = self.bufs.get(name)
        if b is None:
            b = self.bufs[name] = Buf(name)
        return b

    def bl(self, names):
        return [self.B(n) for n in names]

    def dma(self, out, in_, R, W, q="sp", final=False, slow=False):
        kw = dict(allow_slow_non_contiguous=True) if slow else {}
        self.S.dma(q, lambda e: e.dma_start(out=out, in_=in_, **kw), self.bl(R), self.bl(W), final=final)

    def op(self, e, fn, R, W):
        W = list(W) + [n for n in R if n.startswith("ps") and n not in W]
        self.S.op(e, fn, self.bl(R), self.bl(W))
        if self.rrs is not None:
            self.rrs.yield_point()

    def mm(self, out, lhsT, rhs, R, W, start=True, stop=True):
        self.op("pe", lambda e: e.matmul(out, lhsT, rhs, start=start, stop=stop, skip_group_check=True), R, W)

    def tr(self, out, in_, ident, R, W):
        self.op("pe", lambda e: e.transpose(out, in_, ident), R, W)

    def act(self, out, in_, func, R, W, bias=None, scale=None, accum=None, eng="act"):
        kw = {}
        if bias is not None:
            kw["bias"] = bias
        if scale is not None:
            kw["scale"] = scale
        if accum is not None:
            kw["accum_out"] = accum
        self.op("act", lambda e: e.activation(out, in_, func, **kw), R, W)

    def ts(self, e, out, in0, s1, s2, op0, op1, R, W):
        if s2 is None:
            self.op(e, lambda g: g.tensor_scalar(out, in0, s1, None, op0), R, W)
        else:
            self.op(e, lambda g: g.tensor_scalar(out, in0, s1, s2, op0, op1), R, W)

    def tt(self, e, out, in0, in1, op, R, W):
        self.op(e, lambda g: g.tensor_tensor(out, in0, in1, op), R, W)

    def stt(self, e, out, in0, sc, in1, op0, op1, R, W):
        self.op(e, lambda g: g.scalar_tensor_tensor(out, in0, sc, in1, op0, op1), R, W)

    def cp(self, e, out, in_, R, W):
        if e == "act":
            self.op("act", lambda g: g.copy(out, in_), R, W)
        else:
            self.op(e, lambda g: g.tensor_copy(out, in_), R, W)

    def rot(self, engs=("act", "dve", "pool")):
        self.rr += 1
        return engs[self.rr % len(engs)]

    def rsqrt(self, out, in_, mul, add, R, W, e="dve"):
        self.ts("dve", out, in_, mul, add, ALU.mult, ALU.add, R, W)
        self.act(out, out, AF.Ln, W, W)
        self.act(out, out, AF.Exp, W, W, scale=-0.5)


def col_offsets(cfg):
    HA, HB, D = cfg["HA"], cfg["HB"], cfg["D"]
    WA, WB = HA * 64, HB * 128
    sizes = [WA, WA, WA, WA, WB, WB, WB, WB, HB, HB, D, D]
    names = ["qa", "ka", "va", "za", "qb", "kb", "vb", "zb", "beta", "alpha", "ga", "gb"]
    off, o = {}, 0
    for n, s in zip(names, sizes):
        off[n] = o
        o += s
    off["N_IN"] = o
    return off


def host_consts(cfg):
    SEQ, HA, PAST = cfg["SEQ"], cfg["HA"], cfg["PAST"]
    c = {}
    c["identb"] = np.eye(128, dtype=np.float32).astype(ml_dtypes.bfloat16)
    c["identf"] = np.eye(128, dtype=np.float32)
    r = np.arange(128)
    c["u1"] = (r[:, None] <= r[None, :]).astype(np.float32)
    c["sl1"] = (r[:, None] > r[None, :]).astype(np.float32)
    c["onesf"] = np.ones((128, 128), np.float32)
    c["onesb"] = np.ones((128, 128), np.float32).astype(ml_dtypes.bfloat16)
    c["trib"] = (r[:, None] <= r[None, :]).astype(np.float32).astype(ml_dtypes.bfloat16)
    half = 8
    inv = np.power(500000.0, -np.arange(half, dtype=np.float32) / half).astype(np.float32)

    def tab(pos):
        ang = (pos.astype(np.float32)[:, None] * inv[None, :]).astype(np.float32)
        return np.cos(ang).astype(np.float32), np.sin(ang).astype(np.float32)

    cp, sp_ = tab(np.arange(SEQ))
    cp = np.concatenate([cp, np.tile(tab(np.array([PAST]))[0], (128, 1))], 0)
    sp_ = np.concatenate([sp_, np.tile(tab(np.array([PAST]))[1], (128, 1))], 0)
    c["cosp"] = np.tile(cp[:, None, :], (1, HA, 1)).reshape(SEQ + 128, HA * 8).copy()
    c["sinp"] = np.tile(sp_[:, None, :], (1, HA, 1)).reshape(SEQ + 128, HA * 8).copy()
    cs, ss = tab(np.array([PAST]))
    c["coss"] = np.tile(cs[:, None, :], (cfg["NS"], HA, 1)).reshape(cfg["NS"], HA * 8).copy()
    c["sins"] = np.tile(ss[:, None, :], (cfg["NS"], HA, 1)).reshape(cfg["NS"], HA * 8).copy()
    nkt = SEQ // 128
    oh = np.zeros((128, nkt, 32), np.float32)
    for kt in range(nkt):
        oh[:, kt, (kt // 2) % 32] = 1.0
    c["onehot"] = oh.astype(ml_dtypes.bfloat16)
    c["md8"] = (r[:, None] // 8 == r[None, :] // 8).astype(np.float32)
    for b in (8, 16, 32, 64):
        mo = ((r[:, None] // (2 * b) == r[None, :] // (2 * b)) & (r[:, None] % (2 * b) >= b)
              & (r[None, :] % (2 * b) < b)).astype(np.float32)
        c[f"mo{b}T"] = np.ascontiguousarray(mo.T)
    pair = np.zeros((128, 64), np.float32)
    pair[r, r // 2] = 1.0
    c["pair"] = pair
    return c


def riota_table(cfg):
    HA = cfg["HA"]
    r = np.arange(128, dtype=np.int32)[:, None]
    h = np.repeat(np.arange(HA, dtype=np.int32), 6)[None, :]
    return (r * HA + h).astype(np.int32)


CONST_DT = dict(identb=BF16, identf=F32, u1=F32, sl1=F32, onesf=F32, onesb=BF16, trib=BF16,
                cosp=F32, sinp=F32, coss=F32, sins=F32, onehot=BF16, pair=F32,
                md8=F32, mo8T=F32, mo16T=F32, mo32T=F32, mo64T=F32)


def build(cfg, phases=("p0", "p1", "p2", "p3", "p4", "ps")):
    kb = KB(cfg)
    nc = kb.nc
    D, SEQ, HA, HB, PLE = cfg["D"], cfg["SEQ"], cfg["HA"], cfg["HB"], cfg["PLE"]
    KD = D // 128
    WA, WB = HA * 64, HB * 128
    CC = 3 * WB
    off = col_offsets(cfg)
    NIN = off["N_IN"]
    NT = SEQ // 128
    NS = cfg["NS"]
    SX = SEQ + 128

    def inp(name, shape, dt=F32):
        return kb.dram(name, shape, dt, kind="ExternalInput")

    def outp(name, shape, dt=F32):
        return kb.dram(name, shape, dt, kind="ExternalOutput")

    x = inp("x", [SX, D])
    p_in = inp("p", [SX, PLE])
    w_in = inp("w_in", [D, NIN])
    g_mix = inp("g_mix", [D])
    conv_w = inp("conv_w", [4, CC])
    a_log = inp("a_log", [HB])
    dt_bias = inp("dt_bias", [HB])
    g_onorm = inp("g_onorm", [128])
    w_pa = inp("w_pa", [WA, D])
    w_pb = inp("w_pb", [WB, D])
    w_o = inp("w_o", [D, D])
    g_ple = inp("g_ple", [D])
    w_pg = inp("w_pg", [D, D])
    w_ple = inp("w_ple", [PLE, D])
    g_final = inp("g_final", [D])
    cst = {k: inp("c_" + k, list(v.shape), CONST_DT[k]) for k, v in host_consts(cfg).items()}
    y_p = outp("y_p", [SX, D])
    k_p = outp("k_p", [SX, WA])
    v_p = outp("v_p", [SX, WA])
    s_p = outp("s_p", [HB, 128, 128])
    conv_p = outp("conv_p", [3, CC])
    w_in_b = kb.dram("w_in_b", [D, NIN], BF16)
    w_pa_b = kb.dram("w_pa_b", [WA, D], BF16)
    w_pb_b = kb.dram("w_pb_b", [WB, D], BF16)
    w_o_b = kb.dram("w_o_b", [D, D], BF16)
    w_pg_b = kb.dram("w_pg_b", [D, D], BF16)
    w_ple_b = kb.dram("w_ple_b", [PLE, D], BF16)
    Qtok = kb.dram("Qtok", [SX, WA], BF16)
    Ktok = kb.dram("Ktok", [SX, WA], BF16)
    Vtok = kb.dram("Vtok", [SX, WA], BF16)
    zaT = kb.dram("zaT", [WA, SX], BF16)
    zbs = kb.dram("zbs", [SX, WB], BF16)
    uT = kb.dram("uT", [CC, SX], BF16)
    sgT = kb.dram("sgT", [2 * D, SX], BF16)
    GBs = kb.dram("GBs", [SX, 2 * HB], F32)
    GaT = kb.dram("GaT", [WA, SX], BF16)
    GbT = kb.dram("GbT", [WB, SX], BF16)

    identb = kb.sb("identb", [128, 128], BF16)
    identf = kb.sb("identf", [128, 128], F32)
    u1 = kb.sb("u1", [128, 128], F32)
    sl1 = kb.sb("sl1", [128, 128], F32)
    onesf = kb.sb("onesf", [128, 128], F32)
    onesb = kb.sb("onesb", [128, 128], BF16)
    trib = kb.sb("trib", [128, 128], BF16)
    for nm, t in (("identb", identb), ("identf", identf), ("u1", u1), ("sl1", sl1), ("onesf", onesf),
                  ("onesb", onesb), ("trib", trib)):
        kb.dma(t[:], cst[nm], ["c_" + nm], ["k_" + nm])
    CN = ["k_identb", "k_identf", "k_u1", "k_sl1", "k_onesf", "k_onesb", "k_trib"]

    zt = kb.sb("zt", [128, 128], BF16)
    kb.op("pool", lambda g: g.memset(zt[:], 0.0), [], ["zt"])
    for r0 in range(0, WA, 128):
        kb.dma(GaT[r0:r0 + 128, SEQ:SX], zt[:], ["zt"], ["GaT"], q="pool")
    for r0 in range(0, WB, 128):
        kb.dma(GbT[r0:r0 + 128, SEQ:SX], zt[:], ["zt"], ["GbT"], q="pool")
    psf = [kb.ps(f"psf{i}", [128, 512], F32) for i in range(6)]
    psb = [kb.ps(f"psb{i}", [128, 1024], BF16) for i in range(2)]

    if "p0" in phases:
        kb.phase_begin()
        CW = 2048
        stg_f = [kb.sb(f"p0f{i}", [128, CW], F32) for i in range(2)]
        stg_b = [kb.sb(f"p0b{i}", [128, CW], BF16) for i in range(2)]
        it = 0
        for (src, dst, rows, cols, dn) in ((w_in, w_in_b, D, NIN, "w_in_b"), (w_pa, w_pa_b, WA, D, "w_pa_b"),
                                           (w_pb, w_pb_b, WB, D, "w_pb_b"), (w_o, w_o_b, D, D, "w_o_b"),
                                           (w_pg, w_pg_b, D, D, "w_pg_b"), (w_ple, w_ple_b, PLE, D, "w_ple_b")):
            for r0 in range(0, rows, 128):
                for c0 in range(0, cols, CW):
                    cw = min(CW, cols - c0)
                    i = it % 2
                    it += 1
                    kb.dma(stg_f[i][:, :cw], src[r0:r0 + 128, c0:c0 + cw], [], [f"p0f{i}"])
                    kb.cp(kb.rot(), stg_b[i][:, :cw], stg_f[i][:, :cw], [f"p0f{i}"], [f"p0b{i}"])
                    kb.dma(dst[r0:r0 + 128, c0:c0 + cw], stg_b[i][:, :cw], [f"p0b{i}"], [dn], q="pool")

    if "p0" in phases:
        kb.phase_end()
    ST = min(SEQ, 2048)
    if "p1" in phases:
        kb.phase_begin()
        hT = kb.sb("hT", [128, KD, ST], BF16)
        gmixT = kb.sb("gmixT", [128, KD], F32)
        kb.dma(gmixT[:], g_mix.rearrange("(k p) -> p k", p=128), [], ["gmixT"], slow=True)
        xt = [kb.sb(f"xt{i}", [128, D], F32) for i in range(2)]
        xn = [kb.sb(f"xn{i}", [128, D], BF16) for i in range(2)]
        junk = kb.sb("junk", [128, D], F32)
        ssq = [kb.sb(f"ssq{i}", [128, 1], F32) for i in range(2)]
        negA = kb.sb("negA", [128, HB], F32)
        dtb = kb.sb("dtb", [128, HB], F32)
        kb.dma(negA[:], a_log.partition_broadcast(128), [], ["negA"])
        kb.dma(dtb[:], dt_bias.partition_broadcast(128), [], ["dtb"])
        kb.act(negA[:], negA[:], AF.Exp, ["negA"], ["negA"])
        kb.ts("dve", negA[:], negA[:], -1.0, None, ALU.mult, None, ["negA"], ["negA"])
        cosb = [kb.sb(f"cosb{i}", [128, HA * 8], F32) for i in range(2)]
        sinb = [kb.sb(f"sinb{i}", [128, HA * 8], F32) for i in range(2)]
        GW = 512
        wg = [kb.sb(f"wg{i}", [128, KD, GW], BF16) for i in range(2)]
        fm_st = [kb.sb(f"fmst{i}", [128, 512], BF16) for i in range(3)]
        tm_f = [kb.sb(f"tmf{i}", [128, 512], F32) for i in range(2)]
        tm_b = [kb.sb(f"tmb{i}", [128, 512], BF16) for i in range(2)]
        rtmp = kb.sb("rtmp", [128, 4, HA * 8], F32)
        gbt = kb.sb("gbt", [128, 2 * HB], F32)
        gbt2 = kb.sb("gbt2", [128, HB], F32)
        convst = kb.sb("convst", [128, CC // 128, 3], F32)
        w_in_bv = w_in_b.rearrange("(k p) n -> p k n", p=128)

        fm_groups = [("za", off["za"], WA, zaT, 0), ("u", off["qb"], CC, uT, 0),
                     ("sg", off["ga"], 2 * D, sgT, 0)]
        tm_groups = [("qa", off["qa"], WA), ("ka", off["ka"], WA), ("va", off["va"], WA),
                     ("zb", off["zb"], WB), ("ab", off["beta"], 2 * HB)]
        wcount = 0
        fcount = 0
        tcount = 0
        for st0, sw in [(a, ST) for a in range(0, SEQ, ST)] + [(SEQ, 128)]:
            cwd = min(512, sw)
            for tt in range(sw // 128):
                i = tt % 2
                t0 = st0 + tt * 128
                kb.dma(xt[i][:], x[t0:t0 + 128, :], [], [f"xt{i}"])
                kb.act(junk[:], xt[i][:], AF.Square, [f"xt{i}"], ["junk", f"ssq{i}"], accum=ssq[i][:])
                kb.rsqrt(ssq[i][:], ssq[i][:], 1.0 / D, EPS, [f"ssq{i}"], [f"ssq{i}"])
                kb.act(xn[i][:], xt[i][:], AF.Copy, [f"xt{i}", f"ssq{i}"], [f"xn{i}"], scale=ssq[i][:])
                pb = psb[tt % 2]
                for k in range(KD):
                    kb.tr(pb[:, k * 128:(k + 1) * 128], xn[i][:, k * 128:(k + 1) * 128], identb[:],
                          [f"xn{i}", "k_identb"], [f"psb{tt % 2}"])
                for k in range(KD):
                    kb.ts(kb.rot(("dve", "pool")) if False else "dve", hT[:, k, tt * 128:(tt + 1) * 128],
                          pb[:, k * 128:(k + 1) * 128], gmixT[:, k:k + 1], None, ALU.mult, None,
                          [f"psb{tt % 2}", "gmixT"], ["hT"])
            for (kind, c0g, ncols, dst, r0d) in fm_groups:
                for g0 in range(0, ncols, GW):
                    gw = min(GW, ncols - g0)
                    wi = wcount % 2
                    wcount += 1
                    kb.dma(wg[wi][:, :, :gw], w_in_bv[:, :, c0g + g0:c0g + g0 + gw], ["w_in_b"], [f"wg{wi}"])
                    for u0 in range(0, gw, 128):
                        for tc0 in range(0, sw, cwd):
                            pi = fcount % 4
                            si = fcount % 3
                            fcount += 1
                            pt = psf[pi]
                            for k in range(KD):
                                kb.mm(pt[:, :cwd], wg[wi][:, k, u0:u0 + 128], hT[:, k, tc0:tc0 + cwd],
                                      [f"wg{wi}", "hT"], [f"psf{pi}"], start=(k == 0), stop=(k == KD - 1))
                            func = {"za": AF.Silu, "u": AF.Copy, "sg": AF.Sigmoid}[kind]
                            kb.act(fm_st[si][:, :cwd], pt[:, :cwd], func, [f"psf{pi}"], [f"fmst{si}"])
                            row = r0d + g0 + u0
                            if kind == "u" and st0 + tc0 + 512 == SEQ:
                                kb.cp("dve", convst[:, row // 128, :], pt[:, 509:512], [f"psf{pi}"], ["convst"])
                            kb.dma(dst[row:row + 128, st0 + tc0:st0 + tc0 + cwd], fm_st[si][:, :cwd],
                                   [f"fmst{si}"], [{"za": "zaT", "u": "uT", "sg": "sgT"}[kind]], q="pool")
            for (kind, c0g, ncols) in tm_groups:
                for g0 in range(0, ncols, GW):
                    gw = min(GW, ncols - g0)
                    wi = wcount % 2
                    wcount += 1
                    kb.dma(wg[wi][:, :, :gw], w_in_bv[:, :, c0g + g0:c0g + g0 + gw], ["w_in_b"], [f"wg{wi}"])
                    for tt in range(sw // 128):
                        t0 = st0 + tt * 128
                        pi = 4 + tcount % 2
                        fi = tcount % 2
                        tcount += 1
                        pt = psf[pi]
                        for k in range(KD):
                            kb.mm(pt[:, :gw], hT[:, k, tt * 128:(tt + 1) * 128], wg[wi][:, k, :gw],
                                  [f"wg{wi}", "hT"], [f"psf{pi}"], start=(k == 0), stop=(k == KD - 1))
                        PR, TF, TB = [f"psf{pi}"], [f"tmf{fi}"], [f"tmb{fi}"]
                        if kind in ("qa", "ka"):
                            kb.dma(cosb[fi][:], cst["cosp"][t0:t0 + 128, :], [], [f"cosb{fi}"])
                            kb.dma(sinb[fi][:], cst["sinp"][t0:t0 + 128, :], [], [f"sinb{fi}"])
                            kb.cp("act", tm_f[fi][:, :gw], pt[:, :gw], PR, TF)
                            pv = pt[:, :gw].rearrange("p (h d) -> p h d", d=64)
                            ov = tm_f[fi][:, :gw].rearrange("p (h d) -> p h d", d=64)
                            nh = gw // 64
                            h0 = g0 // 64
                            cv = cosb[fi][:, h0 * 8:(h0 + nh) * 8].rearrange("p (h d) -> p h d", d=8)
                            sv = sinb[fi][:, h0 * 8:(h0 + nh) * 8].rearrange("p (h d) -> p h d", d=8)
                            rv = [rtmp[:, j, :nh * 8].rearrange("p (h d) -> p h d", d=8) for j in range(4)]
                            CS = [f"cosb{fi}", f"sinb{fi}"]
                            _ord = [int(c) for c in _os.environ.get("RORD", "0123")]
                            _defs = {0: (rv[0], ov[:, :, 0:8], cv, "rtmp0"), 1: (rv[1], ov[:, :, 8:16], sv, "rtmp1"),
                                     2: (rv[2], ov[:, :, 8:16], cv, "rtmp2"), 3: (rv[3], ov[:, :, 0:8], sv, "rtmp3")}
                            for _o in _ord:
                                _a, _b, _c, _n = _defs[_o]
                                kb.tt("dve", _a, _b, _c, ALU.mult, TF + CS, [_n])
                            kb.tt("dve", ov[:, :, 0:8], rv[0], rv[1], ALU.subtract, ["rtmp0", "rtmp1"], TF)
                            kb.tt("dve", ov[:, :, 8:16], rv[2], rv[3], ALU.add, ["rtmp2", "rtmp3"], TF)
                            if kind == "qa":
                                kb.act(tm_b[fi][:, :gw], tm_f[fi][:, :gw], AF.Copy, TF, TB, scale=0.125)
                                kb.dma(Qtok[t0:t0 + 128, g0:g0 + gw], tm_b[fi][:, :gw], TB, ["Qtok"], q="pool")
                            else:
                                kb.cp("pool", tm_b[fi][:, :gw], tm_f[fi][:, :gw], TF, TB)
                                kb.dma(k_p[t0:t0 + 128, g0:g0 + gw], tm_f[fi][:, :gw], TF, ["k_p"], q="pool", final=True)
                                kb.dma(Ktok[t0:t0 + 128, g0:g0 + gw], tm_b[fi][:, :gw], TB, ["Ktok"], q="pool")
                        elif kind == "va":
                            kb.cp("act", tm_f[fi][:, :gw], pt[:, :gw], PR, TF)
                            kb.cp("dve", tm_b[fi][:, :gw], pt[:, :gw], PR, TB)
                            kb.dma(v_p[t0:t0 + 128, g0:g0 + gw], tm_f[fi][:, :gw], TF, ["v_p"], q="pool", final=True)
                            kb.dma(Vtok[t0:t0 + 128, g0:g0 + gw], tm_b[fi][:, :gw], TB, ["Vtok"], q="pool")
                        elif kind == "zb":
                            kb.act(tm_b[fi][:, :gw], pt[:, :gw], AF.Silu, PR, TB)
                            kb.dma(zbs[t0:t0 + 128, g0:g0 + gw], tm_b[fi][:, :gw], TB, ["zbs"], q="pool")
                        else:
                            kb.tt("dve", gbt2[:], pt[:, HB:2 * HB], dtb[:], ALU.add, PR + ["dtb"], ["gbt2"])
                            kb.act(gbt2[:], gbt2[:], AF.Exp, ["gbt2"], ["gbt2"])
                            kb.act(gbt2[:], gbt2[:], AF.Ln, ["gbt2"], ["gbt2"], bias=1.0)
                            kb.tt("dve", gbt[:, 0:HB], gbt2[:], negA[:], ALU.mult, ["gbt2", "negA", "gbt"], ["gbt"])
                            kb.act(gbt[:, HB:2 * HB], pt[:, 0:HB], AF.Sigmoid, PR + ["gbt"], ["gbt"])
                            kb.dma(GBs[t0:t0 + 128, :], gbt[:], ["gbt"], ["GBs"], q="pool")
        for u in range(CC // 128):
            kb.dma(conv_p[:, u * 128:(u + 1) * 128].rearrange("j p -> p j"), convst[:, u, :], ["convst"],
                   ["conv_p"], q="pool", final=True, slow=True)


    if "p1" in phases:
        kb.phase_end()
    if "p2" in phases:
        kb.phase_begin()
        NB = SEQ // 256
        NCH = SEQ // 512
        KA = kb.sb("KA", [128, NT, 96], BF16)
        QA = kb.sb("QA", [128, NT, 96], BF16)
        VA = kb.sb("VA", [128, NT, 65], BF16)
        KTa = kb.sb("KTa", [96, SEQ], BF16)
        kbf = kb.sb("kbf", [64, 32], F32)
        kbarT = kb.sb("kbarT", [64, 32], BF16)
        QTs = [kb.sb(f"QTs{i}", [64, 128], BF16) for i in range(2)]
        QTa = [kb.sb(f"QTa{i}", [96, 512], BF16) for i in range(2)]
        gs = kb.sb("gs", [128, 32], F32)
        top8 = kb.sb("top8", [128, 8], F32)
        PT = [kb.sb(f"PT{i}", [128, 512], BF16) for i in range(3)]
        osb = [kb.sb(f"osb{i}", [65, 512], F32) for i in range(2)]
        rec = [kb.sb(f"rec{i}", [65, 512], F32) for i in range(2)]
        zat = [kb.sb(f"zat{i}", [64, 512], BF16) for i in range(2)]
        otmp = [kb.sb(f"otmp{i}", [64, 512], F32) for i in range(2)]
        gat = [kb.sb(f"gat{i}", [64, 512], BF16) for i in range(2)]
        kb.dma(KA[:, :, 64:96], cst["onehot"], [], ["KA"])
        kb.op("pool", lambda g: g.memset(VA[:, :, 64:65], 1.0), [], ["VA"])
        scount = 0
        for h in range(HA):
            hs = slice(h * 64, (h + 1) * 64)
            kb.dma(KA[:, :, 0:64], Ktok[0:SEQ, hs].rearrange("(kt p) d -> p kt d", p=128), ["Ktok"], ["KA"])
            kb.dma(QA[:, :, 0:64], Qtok[0:SEQ, hs].rearrange("(kt p) d -> p kt d", p=128), ["Qtok"], ["QA"])
            kb.dma(VA[:, :, 0:64], Vtok[0:SEQ, hs].rearrange("(kt p) d -> p kt d", p=128), ["Vtok"], ["VA"])
            for kt0 in range(0, NT, 8):
                n8 = min(8, NT - kt0)
                bi = (kt0 // 8) % 2
                for j in range(n8):
                    kb.tr(psb[bi][0:96, j * 128:(j + 1) * 128], KA[:, kt0 + j, :], identb[:],
                          ["KA", "k_identb"], [f"psb{bi}"])
                kb.cp(kb.rot(("act", "dve")), KTa[:, kt0 * 128:(kt0 + n8) * 128], psb[bi][0:96, 0:n8 * 128],
                      [f"psb{bi}"], ["KTa"])
            kb.op("pool", lambda g: g.memset(kbf[:], 0.0), [], ["kbf"])
            kb.op("dve", lambda g: g.tensor_reduce(kbf[:, 0:NB], KTa[0:64, :].rearrange("p (n k) -> p n k", k=256),
                                                   AX.X, ALU.add), ["KTa"], ["kbf"])
            kb.cp("dve", kbarT[:], kbf[:], ["kbf"], ["kbarT"])
            kb.op("pool", lambda g: g.memset(gs[:], -1e30), [], ["gs"])
            for c in range(NCH):
                ci = c % 2
                q0 = c * 512
                for j in range(4):
                    qt = 4 * c + j
                    own = qt // 2
                    qi = qt % 2
                    bi = qt % 2
                    kb.tr(psb[bi][0:64, 0:128], QA[:, qt, 0:64], identb[:], ["QA", "k_identb"], [f"psb{bi}"])
                    kb.cp("act", QTs[qi][:], psb[bi][0:64, 0:128], [f"psb{bi}"], [f"QTs{qi}"])
                    mb = QA[:, qt, 64:96]
                    kb.op("pool", lambda g, mb=mb: g.memset(mb, NEG), [], ["QA"])
                    kb.op("pool", lambda g, mb=mb, own=own: g.memset(mb[:, own:own + 1], 0.0), [], ["QA"])
                    if 1 <= own <= 3:
                        kb.op("pool", lambda g, mb=mb, own=own: g.memset(mb[:, 0:own], 0.0), [], ["QA"])
                    elif own > 3:
                        kb.mm(psf[4][:, 0:32], QTs[qi][:], kbarT[:], [f"QTs{qi}", "kbarT"], ["psf4"])
                        kb.cp("dve", gs[:, 0:own], psf[4][:, 0:own], ["psf4"], ["gs"])
                        kb.op("dve", lambda g: g.max(top8[:], gs[:]), ["gs"], ["top8"])
                        kb.ts("dve", mb[:, 0:own], gs[:, 0:own], top8[:, 2:3], NEG, ALU.is_lt, ALU.mult,
                              ["gs", "top8"], ["QA"])
                    kb.tr(psb[bi][0:96, 128:256], QA[:, qt, :], identb[:], ["QA", "k_identb"], [f"psb{bi}"])
                    kb.cp("dve", QTa[ci][:, j * 128:(j + 1) * 128], psb[bi][0:96, 128:256], [f"psb{bi}"],
                          [f"QTa{ci}"])
                nk = 4 * c + 4
                oi = 2 + c % 2
                for kt in range(nk):
                    j0 = max(0, kt - 4 * c)
                    n = 512 - j0 * 128
                    si = scount % 2
                    pi = scount % 3
                    scount += 1
                    kb.mm(psf[si][:, 0:n], KTa[:, kt * 128:(kt + 1) * 128], QTa[ci][:, j0 * 128:512],
                          ["KTa", f"QTa{ci}"], [f"psf{si}"])
                    kb.act(PT[pi][:, 0:n], psf[si][:, 0:n], AF.Exp, [f"psf{si}"], [f"PT{pi}"])
                    if kt >= 4 * c:
                        kb.tt("pool", PT[pi][:, 0:128], PT[pi][:, 0:128], trib[:], ALU.mult,
                              [f"PT{pi}", "k_trib"], [f"PT{pi}"])
                    kb.mm(psf[oi][0:65, j0 * 128:512], VA[:, kt, :], PT[pi][:, 0:n], ["VA", f"PT{pi}"],
                          [f"psf{oi}"], start=(kt == 0), stop=(kt == nk - 1))
                kb.cp("act", osb[ci][:], psf[oi][0:65, :], [f"psf{oi}"], [f"osb{ci}"])
                kb.op("dve", lambda g, ci=ci: g.reciprocal(rec[ci][64:65, :], osb[ci][64:65, :]),
                      [f"osb{ci}"], [f"rec{ci}"])
                kb.mm(psf[5][0:64, :], onesf[64:65, 0:64], rec[ci][64:65, :], ["k_onesf", f"rec{ci}"], ["psf5"])
                kb.dma(zat[ci][:], zaT[hs, q0:q0 + 512], ["zaT"], [f"zat{ci}"])
                kb.tt("dve", otmp[ci][:], psf[5][0:64, :], osb[ci][0:64, :], ALU.mult, ["psf5", f"osb{ci}"],
                      [f"otmp{ci}"])
                kb.tt("pool", gat[ci][:], otmp[ci][:], zat[ci][:], ALU.mult, [f"otmp{ci}", f"zat{ci}"],
                      [f"gat{ci}"])
                kb.dma(GaT[hs, q0:q0 + 512], gat[ci][:], [f"gat{ci}"], ["GaT"], q="pool")
        kb.phase_end()


    QKVn = kb.dram("QKVn", [CC, SEQ], BF16)
    KnF = kb.dram("KnF", [WB, SEQ], F32)
    if "p3" in phases:
        kb.phase_begin()
        U = [kb.sb(f"U{i}", [128, 520], BF16) for i in range(2)]
        Yc = [kb.sb(f"Yc{i}", [128, 512], F32) for i in range(2)]
        sq = [kb.sb(f"sq{i}", [128, 512], BF16) for i in range(2)]
        rs = [kb.sb(f"rs{i}", [128, 512], F32) for i in range(2)]
        sq32 = [kb.sb(f"sq32{i}", [128, 512], F32) for i in range(2)]
        yb = [kb.sb(f"yb{i}", [128, 512], BF16) for i in range(2)]
        cw = [kb.sb(f"cw{i}", [128, 4], F32) for i in range(2)]
        uc = 0
        for _d in range(int(_os.environ.get("NDUM", "0"))):
            kb.op("dve", lambda g: g.memset(sq32[0][:, 0:8], 0.0), [], ["dummy"])
        for j in range(3):
            for h in range(HB):
                r0 = j * WB + h * 128
                wi = (j * HB + h) % 2
                kb.dma(cw[wi][:], conv_w[:, r0:r0 + 128].rearrange("i p -> p i"), [], [f"cw{wi}"], slow=True)
                for t0 in range(0, SEQ, 512):
                    i = uc % 2
                    uc += 1
                    if t0 == 0:
                        kb.op("pool", lambda g, i=i: g.memset(U[i][:, 0:3], 0.0), [], [f"U{i}"])
                        kb.dma(U[i][:, 3:515], uT[r0:r0 + 128, 0:512], ["uT"], [f"U{i}"])
                    else:
                        kb.dma(U[i][:, 0:515], uT[r0:r0 + 128, t0 - 3:t0 + 512], ["uT"], [f"U{i}"])
                    kb.ts("dve", Yc[i][:], U[i][:, 0:512], cw[wi][:, 0:1], None, ALU.mult, None,
                          [f"U{i}", f"cw{wi}"], [f"Yc{i}"])
                    for k in range(1, 4):
                        kb.stt("dve", Yc[i][:], U[i][:, k:k + 512], cw[wi][:, k:k + 1], Yc[i][:], ALU.mult, ALU.add,
                               [f"U{i}", f"cw{wi}", f"Yc{i}"], [f"Yc{i}"])
                    kb.act(Yc[i][:], Yc[i][:], AF.Silu, [f"Yc{i}"], [f"Yc{i}"])
                    if j == 2:
                        kb.cp("pool", yb[i][:], Yc[i][:], [f"Yc{i}"], [f"yb{i}"])
                    else:
                        pi = uc % 2
                        kb.act(sq[i][:], Yc[i][:], AF.Square, [f"Yc{i}"], [f"sq{i}"])
                        kb.mm(psf[pi][:, :], onesb[:], sq[i][:], ["k_onesb", f"sq{i}"], [f"psf{pi}"])
                        kb.ts("dve", rs[i][:], psf[pi][:, :], EPS, None, ALU.add, None, [f"psf{pi}"], [f"rs{i}"])
                        if _os.environ.get("RSV", "1") == "0":
                            kb.act(rs[i][:], rs[i][:], AF.Ln, [f"rs{i}"], [f"rs{i}"])
                            kb.act(rs[i][:], rs[i][:], AF.Exp, [f"rs{i}"], [f"rs{i}"], scale=-0.5)
                        else:
                            kb.act(sq32[i][:], rs[i][:], AF.Sqrt, [f"rs{i}"], [f"sq32{i}"])
                            kb.op("dve", lambda g, i=i: g.reciprocal(rs[i][:], sq32[i][:]), [f"sq32{i}"], [f"rs{i}"])
                        if j == 0:
                            kb.stt("dve", yb[i][:], Yc[i][:], 128.0 ** -0.5, rs[i][:], ALU.mult, ALU.mult,
                                   [f"Yc{i}", f"rs{i}"], [f"yb{i}"])
                        else:
                            kb.tt("dve", Yc[i][:], Yc[i][:], rs[i][:], ALU.mult, [f"Yc{i}", f"rs{i}"], [f"Yc{i}"])
                            kb.cp("pool", yb[i][:], Yc[i][:], [f"Yc{i}"], [f"yb{i}"])
                            kb.dma(KnF[h * 128:(h + 1) * 128, t0:t0 + 512], Yc[i][:], [f"Yc{i}"], ["KnF"], q="pool")
                    kb.dma(QKVn[r0:r0 + 128, t0:t0 + 512], yb[i][:], [f"yb{i}"], ["QKVn"], q="pool")
        kb.phase_end()
        P3CUT = int(_os.environ.get("P3CUT", "9"))

        kb.phase_begin()
        NHS = min(HB, 4)
        HBX = HB if P3CUT >= 2 else 0
        GBall = kb.sb("GBall", [128, NT, 2 * HB], F32)
        kb.dma(GBall[:], GBs[0:SEQ, :].rearrange("(c p) j -> p c j", p=128), ["GBs"], ["GBall"])
        gon = kb.sb("gon", [128, 128], F32)
        kb.dma(gon[:], g_onorm.partition_broadcast(128), [], ["gon"])

        def T_(nm, shape, dt):
            return [kb.sb(f"{nm}{i}", shape, dt) for i in range(NHS)]
        gH, bH, gcs, eg, etail, cdec, nbeg, nbet = [T_(n, [128, NT], F32) for n in
                                                   ("gH", "bH", "gcs", "eg", "etail", "cdec", "nbeg", "nbet")]
        S32 = [kb.sb(f"S32_{h}", [128, 128], F32) for h in range(HB)]
        Sbf = [kb.sb(f"Sbf_{h}", [128, 128], BF16) for h in range(HB)]
        KcT, QcT, VcT = T_("KcT", [128, 128], BF16), T_("QcT", [128, 128], BF16), T_("VcT", [128, 128], BF16)
        Ktail, bV = T_("Ktail", [128, 128], BF16), T_("bV", [128, 128], F32)
        KcF = T_("KcF", [128, 128], F32)
        An, AnT, T32, TT32 = T_("An", [128, 128], F32), T_("AnT", [128, 128], F32), T_("T32", [128, 128], F32), \
            T_("TT32", [128, 128], F32)
        Pb, PTb, Tb, TTb, AoT, M1b = [T_(n, [128, 128], BF16) for n in ("Pb", "PTb", "Tb", "TTb", "AoT", "M1b")]
        cm = {}
        for nm in ("md8", "mo8T", "mo16T", "mo32T", "mo64T"):
            cm[nm] = kb.sb("cm_" + nm, [128, 128], F32)
            kb.dma(cm[nm][:], cst[nm], [], ["k_" + nm])
        gU, Em, ETm = T_("gU", [128, 128], F32), T_("Em", [128, 128], F32), T_("ETm", [128, 128], F32)
        Pm, PTm = [T_("Pm_a", [128, 128], BF16), T_("Pm_b", [128, 128], BF16)], \
                  [T_("PTm_a", [128, 128], BF16), T_("PTm_b", [128, 128], BF16)]
        Y32, Ybf, attnT = T_("Y32", [128, 128], F32), T_("Ybf", [128, 128], BF16), T_("attnT", [128, 128], BF16)
        Rt, Vnb, O2s, Ot = T_("Rt", [128, 128], BF16), T_("Vnb", [128, 128], BF16), T_("O2s", [128, 128], F32), \
            T_("Ot", [128, 128], F32)
        oss, zbt, Gbt, GbTt = T_("oss", [128, 1], F32), T_("zbt", [128, 128], BF16), T_("Gbt", [128, 128], BF16), \
            T_("GbTt", [128, 128], BF16)
        ojunk = T_("ojunk", [128, 128], F32)
        pc = [0]

        def PS():
            pc[0] += 1
            i = pc[0] % 6
            return psf[i], f"psf{i}"

        def PB():
            pc[0] += 1
            i = pc[0] % 2
            return psb[i], f"psb{i}"

        for h in range(HB):
            kb.op("pool", lambda g, h=h: g.memset(S32[h][:], 0.0), [], [f"S32_{h}"])
            kb.op("pool", lambda g, h=h: g.memset(Sbf[h][:], 0.0), [], [f"Sbf_{h}"])
        for hg in range(0, HB, NHS):
            heads = list(range(hg, min(HB, hg + NHS)))
            for h in heads:
                s_ = h % NHS
                n_ = lambda nm: f"{nm}{s_}"
                kb.cp("dve", gH[s_][:], GBall[:, :, h], ["GBall"], [n_("gH")])
                kb.cp("dve", bH[s_][:], GBall[:, :, HB + h], ["GBall"], [n_("bH")])
                p1, p1n = psf[4], "psf4"
                kb.mm(p1[:, 0:NT], u1[:], gH[s_][:], ["k_u1", n_("gH")], [p1n])
                kb.cp("act", gcs[s_][:], p1[:, 0:NT], [p1n], [n_("gcs")])
                p2, p2n = psf[5], "psf5"
                kb.mm(p2[:, 0:NT], onesf[:], gH[s_][:], ["k_onesf", n_("gH")], [p2n])
                kb.act(eg[s_][:], gcs[s_][:], AF.Exp, [n_("gcs")], [n_("eg")])
                kb.act(cdec[s_][:], p2[:, 0:NT], AF.Exp, [p2n], [n_("cdec")])
                kb.tt("dve", etail[s_][:], p2[:, 0:NT], gcs[s_][:], ALU.subtract, [p2n, n_("gcs")], [n_("etail")])
                kb.act(etail[s_][:], etail[s_][:], AF.Exp, [n_("etail")], [n_("etail")])
                kb.stt("dve", nbeg[s_][:], eg[s_][:], -1.0, bH[s_][:], ALU.mult, ALU.mult, [n_("eg"), n_("bH")],
                       [n_("nbeg")])
                kb.ts("dve", nbet[s_][:], bH[s_][:], -1.0, None, ALU.mult, None, [n_("bH")], [n_("nbet")])
            for c in range(NT):
                t0 = c * 128
                def prescan(h, c=c, t0=t0):
                    s_ = h % NHS
                    n_ = lambda nm, s_=s_: f"{nm}{s_}"
                    kb.dma(QcT[s_][:], QKVn[h * 128:(h + 1) * 128, t0:t0 + 128], ["QKVn"], [n_("QcT")])
                    kb.dma(KcT[s_][:], QKVn[WB + h * 128:WB + (h + 1) * 128, t0:t0 + 128], ["QKVn"], [n_("KcT")])
                    kb.dma(VcT[s_][:], QKVn[2 * WB + h * 128:2 * WB + (h + 1) * 128, t0:t0 + 128], ["QKVn"],
                           [n_("VcT")])
                    kb.dma(zbt[s_][:], zbs[t0:t0 + 128, h * 128:(h + 1) * 128], ["zbs"], [n_("zbt")])
                    kb.dma(KcF[s_][:], KnF[h * 128:(h + 1) * 128, t0:t0 + 128], ["KnF"], [n_("KcF")])
                    bank, bn = psf[s_], f"psf{s_}"
                    pb, pbn = bank[:, 384:512].bitcast(BF16), bn
                    kb.tr(pb[:, 0:128], KcT[s_][:], identb[:], [n_("KcT"), "k_identb"], [pbn])
                    kb.tr(pb[:, 128:256], VcT[s_][:], identb[:], [n_("VcT"), "k_identb"], [pbn])
                    kb.ts("dve", Ktail[s_][:], pb[:, 0:128], etail[s_][:, c:c + 1], None, ALU.mult, None,
                          [pbn, n_("etail")], [n_("Ktail")])
                    kb.ts("dve", bV[s_][:], pb[:, 128:256], bH[s_][:, c:c + 1], None, ALU.mult, None,
                          [pbn, n_("bH")], [n_("bV")])
                    kb.ts("dve", gU[s_][:], sl1[:], gH[s_][:, c:c + 1], None, ALU.mult, None,
                          ["k_sl1", n_("gH")], [n_("gU")])
                    pD, pDn = bank, bn
                    kb.mm(pD[:, 0:128], u1[:], gU[s_][:], ["k_u1", n_("gU")], [pDn])
                    kb.mm(pD[:, 128:256], gU[s_][:], u1[:], ["k_u1", n_("gU")], [pDn])
                    kb.act(Em[s_][:], pD[:, 0:128], AF.Exp, [pDn], [n_("Em")])
                    kb.act(ETm[s_][:], pD[:, 128:256], AF.Exp, [pDn], [n_("ETm")])
                    kb.tt("pool", Em[s_][:], Em[s_][:], sl1[:], ALU.mult, [n_("Em"), "k_sl1"], [n_("Em")])
                    kb.tt("pool", ETm[s_][:], ETm[s_][:], u1[:], ALU.mult, [n_("ETm"), "k_u1"], [n_("ETm")])
                    pG, pGn = bank[:, 256:512], bn
                    kb.mm(pG[:, 0:128], KcF[s_][:], KcF[s_][:], [n_("KcF")], [pGn])
                    kb.mm(pG[:, 128:256], KcT[s_][:], QcT[s_][:], [n_("KcT"), n_("QcT")], [pGn])
                    kb.stt("dve", An[s_][:], pG[:, 0:128], nbet[s_][:, c:c + 1], Em[s_][:], ALU.mult, ALU.mult,
                           [pGn, n_("nbet"), n_("Em")], [n_("An")])
                    kb.tt("dve", attnT[s_][:], pG[:, 128:256], ETm[s_][:], ALU.mult, [pGn, n_("ETm")],
                          [n_("attnT")])
                    pA, pAn = bank, bn
                    kb.op("pe", lambda e, pA=pA, s_=s_: e.transpose(pA[:, 0:128], An[s_][:], identf[:]),
                          [n_("An"), "k_identf"], [pAn])
                    kb.cp("act", AnT[s_][:], pA[:, 0:128], [pAn], [n_("AnT")])
                    kb.tt("pool", Pb[s_][:], An[s_][:], cm["md8"][:], ALU.mult, [n_("An"), "k_md8"], [n_("Pb")])
                    kb.tt("pool", PTb[s_][:], AnT[s_][:], cm["md8"][:], ALU.mult, [n_("AnT"), "k_md8"], [n_("PTb")])
                    kb.tt("dve", T32[s_][:], Pb[s_][:], identf[:], ALU.add, [n_("Pb"), "k_identf"], [n_("T32")])
                    kb.tt("dve", TT32[s_][:], PTb[s_][:], identf[:], ALU.add, [n_("PTb"), "k_identf"], [n_("TT32")])
                    kb.cp("act", Tb[s_][:], T32[s_][:], [n_("T32")], [n_("Tb")])
                    kb.cp("act", TTb[s_][:], TT32[s_][:], [n_("TT32")], [n_("TTb")])
                    for jj in range(2):
                        pP, pPn = bank, bn
                        kb.mm(pP[:, 0:128], Pb[s_][:], PTb[s_][:], [n_("Pb"), n_("PTb")], [pPn])
                        kb.mm(pP[:, 128:256], PTb[s_][:], Pb[s_][:], [n_("Pb"), n_("PTb")], [pPn])
                        kb.cp("act", PTb[s_][:], pP[:, 0:128], [pPn], [n_("PTb")])
                        kb.cp("dve", Pb[s_][:], pP[:, 128:256], [pPn], [n_("Pb")])
                        pY, pYn = bank[:, 256:512], bn
                        kb.mm(pY[:, 0:128], PTb[s_][:], Tb[s_][:], [n_("PTb"), n_("Tb")], [pYn])
                        kb.mm(pY[:, 128:256], Tb[s_][:], PTb[s_][:], [n_("PTb"), n_("Tb")], [pYn])
                        kb.tt("dve", T32[s_][:], pY[:, 0:128], T32[s_][:], ALU.add, [pYn, n_("T32")], [n_("T32")])
                        kb.tt("dve", TT32[s_][:], pY[:, 128:256], TT32[s_][:], ALU.add, [pYn, n_("TT32")],
                              [n_("TT32")])
                        kb.cp("act", Tb[s_][:], T32[s_][:], [n_("T32")], [n_("Tb")])
                        kb.cp("pool", TTb[s_][:], TT32[s_][:], [n_("TT32")], [n_("TTb")])
                    for li, b in enumerate((8, 16, 32, 64)):
                        last = (b == 64)
                        kb.tt("pool", AoT[s_][:], AnT[s_][:], cm[f"mo{b}T"][:], ALU.mult, [n_("AnT"), f"k_mo{b}T"],
                              [n_("AoT")])
                        pM, pMn = bank, bn
                        kb.mm(pM[:, 0:128], AoT[s_][:], Tb[s_][:], [n_("AoT"), n_("Tb")], [pMn])
                        kb.cp("act", M1b[s_][:], pM[:, 0:128], [pMn], [n_("M1b")])
                        pT, pTn = bank[:, 256:512], bn
                        kb.mm(pT[:, 128:256], M1b[s_][:], TTb[s_][:], [n_("M1b"), n_("TTb")], [pTn])
                        if not last:
                            kb.mm(pT[:, 0:128], TTb[s_][:], M1b[s_][:], [n_("M1b"), n_("TTb")], [pTn])
                            kb.tt("dve", T32[s_][:], pT[:, 0:128], T32[s_][:], ALU.add, [pTn, n_("T32")],
                                  [n_("T32")])
                            kb.cp("act", Tb[s_][:], T32[s_][:], [n_("T32")], [n_("Tb")])
                        kb.tt("dve", TT32[s_][:], pT[:, 128:256], TT32[s_][:], ALU.add, [pTn, n_("TT32")],
                              [n_("TT32")])
                        if last:
                            kb.cp("pool", Ybf[s_][:], TT32[s_][:], [n_("TT32")], [n_("Ybf")])
                        else:
                            kb.cp("pool", TTb[s_][:], TT32[s_][:], [n_("TT32")], [n_("TTb")])
                RR([lambda h=h: prescan(h) for h in heads], kb).run()
                pKS, pVn, pO1, pO2, pdS = {}, {}, {}, {}, {}

                def scan(h, c=c, t0=t0):
                    s_ = h % NHS
                    bank, bn = psf[s_], f"psf{s_}"
                    pKS[h] = (bank, bn)
                    kb.mm(pKS[h][0][:, 0:128], KcF[s_][:], S32[h][:], [f"KcF{s_}", f"S32_{h}"], [pKS[h][1]])
                    pO1[h] = (bank[:, 256:512], bn)
                    kb.mm(pO1[h][0][:, 0:128], QcT[s_][:], Sbf[h][:], [f"QcT{s_}", f"Sbf_{h}"], [pO1[h][1]])
                    kb.stt("dve", Rt[s_][:], pKS[h][0][:, 0:128], nbeg[s_][:, c:c + 1], bV[s_][:], ALU.mult, ALU.add,
                           [pKS[h][1], f"nbeg{s_}", f"bV{s_}"], [f"Rt{s_}"])
                    kb.mm(pKS[h][0][:, 128:256], Ybf[s_][:], Rt[s_][:], [f"Ybf{s_}", f"Rt{s_}"], [pKS[h][1]])
                    kb.cp("act", Vnb[s_][:], pKS[h][0][:, 128:256], [pKS[h][1]], [f"Vnb{s_}"])
                    kb.mm(pO1[h][0][:, 128:256], attnT[s_][:], Vnb[s_][:], [f"attnT{s_}", f"Vnb{s_}"], [pO1[h][1]])
                    kb.mm(pKS[h][0][:, 0:128], Ktail[s_][:], Vnb[s_][:], [f"Ktail{s_}", f"Vnb{s_}"], [pKS[h][1]])
                    kb.cp("act", O2s[s_][:], pO1[h][0][:, 128:256], [pO1[h][1]], [f"O2s{s_}"])
                    kb.stt("dve", Ot[s_][:], pO1[h][0][:, 0:128], eg[s_][:, c:c + 1], O2s[s_][:], ALU.mult, ALU.add,
                           [pO1[h][1], f"eg{s_}", f"O2s{s_}"], [f"Ot{s_}"])
                    kb.stt("dve", S32[h][:], S32[h][:], cdec[s_][:, c:c + 1], pKS[h][0][:, 0:128], ALU.mult, ALU.add,
                           [f"S32_{h}", f"cdec{s_}", pKS[h][1]], [f"S32_{h}"])
                    kb.cp("act", Sbf[h][:], S32[h][:], [f"S32_{h}"], [f"Sbf_{h}"])
                    kb.act(ojunk[s_][:], Ot[s_][:], AF.Square, [f"Ot{s_}"], [f"ojunk{s_}", f"oss{s_}"],
                           accum=oss[s_][:])
                    kb.rsqrt(oss[s_][:], oss[s_][:], 1.0 / 128, EPS, [f"oss{s_}"], [f"oss{s_}"])
                    kb.stt("dve", Ot[s_][:], Ot[s_][:], oss[s_][:, 0:1], gon[:], ALU.mult, ALU.mult,
                           [f"Ot{s_}", f"oss{s_}", "gon"], [f"Ot{s_}"])
                    kb.tt("pool", Gbt[s_][:], Ot[s_][:], zbt[s_][:], ALU.mult, [f"Ot{s_}", f"zbt{s_}"], [f"Gbt{s_}"])
                    pb, pbn = bank[:, 128:256].bitcast(BF16), bn
                    kb.tr(pb[:, 0:128], Gbt[s_][:], identb[:], [f"Gbt{s_}", "k_identb"], [pbn])
                    kb.cp("act", GbTt[s_][:], pb[:, 0:128], [pbn], [f"GbTt{s_}"])
                    kb.dma(GbT[h * 128:(h + 1) * 128, t0:t0 + 128], GbTt[s_][:], [f"GbTt{s_}"], ["GbT"], q="pool")
                RR([lambda h=h: scan(h) for h in heads], kb).run()
        for h in range(HB):
            kb.dma(s_p[h], S32[h][:], [f"S32_{h}"], ["s_p"], q="pool", final=True)
        kb.phase_end()


    if "ps" in phases:
        kb.phase_begin()
        PAST, NPOOL = cfg["PAST"], cfg["NPOOL"]
        NPG = PAST // 128
        NBK = NPG // 2
        U32 = mybir.dt.uint32
        ck = inp("ck", [NPOOL, 128 * WA])
        cv = inp("cv", [NPOOL, 128 * WA])
        ptab = inp("ptab", [NS, NPG], I32)
        sg_in = inp("sg_in", [NS * HB, 128, 128])
        cb_in = inp("cb_in", [NS, 3, CC])
        riota_d = inp("riota", [128, HA * 6], I32)
        s_s = outp("s_s", [NS * HB, 128, 128])
        conv_s = outp("conv_s", [NS, 3, CC])
        selb = kb.dram("selb", [HA * 3, 1], I32)
        pgs = kb.dram("pgs", [1, HA * 6], I32)
        oas = kb.dram("oas", [NS, WA], F32)
        ck16 = ck.rearrange("n (j c) -> (n j) c", j=16)
        pidx = kb.sb("pidx", [128, 16], I32)
        ckrows = ck.rearrange("n (r h d) -> (n r h) d", h=HA, d=64)
        cvrows = cv.rearrange("n (r h d) -> (n r h) d", h=HA, d=64)
        pair = kb.sb("pair", [128, 64], F32)
        kb.dma(pair[:], cst["pair"], [], ["pair"])
        riota = kb.sb("riota_t", [128, HA * 6], I32)
        kb.dma(riota[:], riota_d, [], ["riota"])
        ptc = kb.sb("ptc", [128, 1], I32)
        Gp = [kb.sb(f"Gp{i}", [128, 8 * WA], F32) for i in range(2)]
        red = kb.sb("red", [128, WA], F32)
        acc = kb.sb("acc", [128, WA], F32)
        qb = kb.sb("qb", [128, WA], BF16)
        prod = kb.sb("prod", [128, WA], F32)
        gp = kb.sb("gp", [128, HA], F32)
        NG = max(NBK, 8)
        gsx = kb.sb("gsx", [HA, NG], F32)
        m8 = kb.sb("m8", [HA, 8], F32)
        i8 = kb.sb("i8", [HA, 8], U32)
        sel24 = kb.sb("sel24", [HA * 3, 1], I32)
        pg24 = kb.sb("pg24", [HA * 3, 2], I32)
        pgb = kb.sb("pgb", [128, HA * 6], I32)
        idx = kb.sb("idx", [128, HA * 6], I32)
        Kg = kb.sb("Kg", [128, HA * 6, 64], F32)
        Vg = kb.sb("Vg", [128, HA * 6, 64], F32)
        sc6 = kb.sb("sc6", [128, HA * 6], F32)
        e6 = kb.sb("e6", [128, HA * 6], F32)
        qrow = kb.sb("qrow", [1, WA], BF16)
        krow = kb.sb("krow", [1, WA], F32)
        vrow = kb.sb("vrow", [1, WA], F32)
        r1 = kb.sb("r1", [1, WA], F32)
        sn = kb.sb("sn", [1, HA], F32)
        den = kb.sb("den", [1, HA], F32)
        tot = kb.sb("tot", [1, HA * 6], F32)
        orow = kb.sb("orow", [1, WA], F32)
        ptv2 = ptab.rearrange("i (n e) -> (i n) e", e=2)
        for i in range(NS):
            kb.dma(ptc[0:NPG, :], ptab[i:i + 1, :].rearrange("o n -> n o"), [], ["ptc"], slow=True)
            for j in range(16):
                kb.ts("pool", pidx[0:NPG, j:j + 1], ptc[0:NPG, 0:1], 16, j, ALU.mult, ALU.add, ["ptc"], ["pidx"])
            for j in range(16):
                gi = j % 2
                kb.S.dma("pool", lambda e, gi=gi, j=j: e.indirect_dma_start(
                    out=Gp[gi][0:NPG, :], out_offset=None, in_=ck16,
                    in_offset=bass.IndirectOffsetOnAxis(ap=pidx[0:NPG, j:j + 1], axis=0)),
                    kb.bl(["pidx"]), kb.bl([f"Gp{gi}"]))
                if j == 0:
                    kb.op("dve", lambda g, gi=gi: g.tensor_reduce(
                        acc[0:NPG, :], Gp[gi][0:NPG, :].rearrange("p (r c) -> p c r", r=8), AX.X, ALU.add),
                        [f"Gp{gi}"], ["acc"])
                else:
                    kb.op("dve", lambda g, gi=gi: g.tensor_reduce(
                        red[0:NPG, :], Gp[gi][0:NPG, :].rearrange("p (r c) -> p c r", r=8), AX.X, ALU.add),
                        [f"Gp{gi}"], ["red"])
                    kb.tt("dve", acc[0:NPG, :], acc[0:NPG, :], red[0:NPG, :], ALU.add, ["acc", "red"], ["acc"])
            kb.dma(qb[:], Qtok[SEQ + i, :].partition_broadcast(128), ["Qtok"], ["qb"])
            kb.tt("dve", prod[0:NPG, :], acc[0:NPG, :], qb[0:NPG, :], ALU.mult, ["acc", "qb"], ["prod"])
            kb.op("dve", lambda g: g.tensor_reduce(gp[0:NPG, :], prod[0:NPG, :].rearrange("p (h d) -> p h d", d=64),
                                                   AX.X, ALU.add), ["prod"], ["gp"])
            kb.mm(psf[0][0:HA, 0:NBK], gp[0:NPG, :], pair[0:NPG, 0:NBK], ["gp", "pair"], ["psf0"])
            kb.op("pool", lambda g: g.memset(gsx[:], -1e30), [], ["gsx"])
            kb.cp("dve", gsx[:, 0:NBK], psf[0][0:HA, 0:NBK], ["psf0"], ["gsx"])
            kb.op("dve", lambda g: g.max(m8[:], gsx[:]), ["gsx"], ["m8"])
            kb.op("dve", lambda g: g.max_index(i8[:], m8[:], gsx[:]), ["gsx", "m8"], ["i8"])
            kb.dma(selb.rearrange("(h j) o -> h (j o)", j=3), i8[:, 0:3].bitcast(I32), ["i8"], ["selb"], q="pool")
            kb.dma(sel24[:], selb, ["selb"], ["sel24"])
            if i > 0:
                kb.ts("pool", sel24[:], sel24[:], i * NBK, None, ALU.add, None, ["sel24"], ["sel24"])
            kb.S.dma("pool", lambda e, i=i: e.indirect_dma_start(
                out=pg24[:], out_offset=None, in_=ptv2,
                in_offset=bass.IndirectOffsetOnAxis(ap=sel24[:, 0:1], axis=0)),
                kb.bl(["sel24"]), kb.bl(["pg24"]))
            kb.dma(pgs.rearrange("o (m e) -> (o m) e", e=2), pg24[:], ["pg24"], ["pgs"], q="pool")
            kb.dma(pgb[:], pgs[0, :].partition_broadcast(128), ["pgs"], ["pgb"])
            kb.ts("pool", idx[:], pgb[:], 128 * HA, None, ALU.mult, None, ["pgb"], ["idx"])
            kb.tt("pool", idx[:], idx[:], riota[:], ALU.add, ["idx", "riota"], ["idx"])
            for m in range(HA * 6):
                kb.S.dma("pool", lambda e, m=m: e.indirect_dma_start(
                    out=Kg[:, m, :], out_offset=None, in_=ckrows,
                    in_offset=bass.IndirectOffsetOnAxis(ap=idx[:, m:m + 1], axis=0)), kb.bl(["idx"]), kb.bl(["Kg"]))
                kb.S.dma("pool", lambda e, m=m: e.indirect_dma_start(
                    out=Vg[:, m, :], out_offset=None, in_=cvrows,
                    in_offset=bass.IndirectOffsetOnAxis(ap=idx[:, m:m + 1], axis=0)), kb.bl(["idx"]), kb.bl(["Vg"]))
            for h in range(HA):
                for je in range(6):
                    m = h * 6 + je
                    kb.tt("dve", Kg[:, m, :], Kg[:, m, :], qb[:, h * 64:(h + 1) * 64], ALU.mult, ["Kg", "qb"], ["Kg"])
            kb.op("dve", lambda g: g.tensor_reduce(sc6[:], Kg[:], AX.X, ALU.add), ["Kg"], ["sc6"])
            kb.act(e6[:], sc6[:], AF.Exp, ["sc6"], ["e6"])
            kb.dma(qrow[:], Qtok[SEQ + i:SEQ + i + 1, :], ["Qtok"], ["qrow"])
            kb.dma(krow[:], k_p[SEQ + i:SEQ + i + 1, :], ["k_p"], ["krow"])
            kb.dma(vrow[:], v_p[SEQ + i:SEQ + i + 1, :], ["v_p"], ["vrow"])
            kb.tt("dve", r1[:], krow[:], qrow[:], ALU.mult, ["krow", "qrow"], ["r1"])
            kb.op("dve", lambda g: g.tensor_reduce(sn[:], r1[:].rearrange("p (h d) -> p h d", d=64), AX.X, ALU.add),
                  ["r1"], ["sn"])
            kb.act(sn[:], sn[:], AF.Exp, ["sn"], ["sn"])
            kb.mm(psf[1][0:1, 0:HA * 6], onesf[:, 0:1], e6[:], ["k_onesf", "e6"], ["psf1"])
            kb.cp("act", tot[:], psf[1][0:1, 0:HA * 6], ["psf1"], ["tot"])
            kb.op("dve", lambda g: g.tensor_reduce(den[:], tot[:].rearrange("p (h j) -> p h j", j=6), AX.X, ALU.add),
                  ["tot"], ["den"])
            kb.tt("dve", den[:], den[:], sn[:], ALU.add, ["den", "sn"], ["den"])
            kb.op("dve", lambda g: g.reciprocal(den[:], den[:]), ["den"], ["den"])
            for h in range(HA):
                for je in range(6):
                    m = h * 6 + je
                    kb.mm(psf[2][0:1, h * 64:(h + 1) * 64], e6[:, m:m + 1], Vg[:, m, :], ["e6", "Vg"], ["psf2"],
                          start=(je == 0), stop=(je == 5))
            for h in range(HA):
                hs = slice(h * 64, (h + 1) * 64)
                kb.stt("dve", orow[:, hs], vrow[:, hs], sn[:, h:h + 1], psf[2][0:1, hs], ALU.mult, ALU.add,
                       ["vrow", "sn", "psf2", "orow"], ["orow"])
                kb.ts("dve", orow[:, hs], orow[:, hs], den[:, h:h + 1], None, ALU.mult, None, ["orow", "den"], ["orow"])
            kb.dma(oas[i:i + 1, :], orow[:], ["orow"], ["oas"], q="pool")
        KA_ = WA // 128
        oaT = kb.sb("oaT", [128, KA_, NS], F32)
        zaS = kb.sb("zaS", [128, KA_, NS], BF16)
        gaS = kb.sb("gaS", [128, KA_, NS], BF16)
        for k in range(KA_):
            kb.dma(oaT[:, k, :], oas[:, k * 128:(k + 1) * 128].rearrange("i p -> p i"), ["oas"], ["oaT"], slow=True)
            kb.dma(zaS[:, k, :], zaT[k * 128:(k + 1) * 128, SEQ:SEQ + NS], ["zaT"], ["zaS"], slow=True)
        kb.tt("dve", gaS[:], oaT[:], zaS[:], ALU.mult, ["oaT", "zaS"], ["gaS"])
        for k in range(KA_):
            kb.dma(GaT[k * 128:(k + 1) * 128, SEQ:SEQ + NS], gaS[:, k, :], ["gaS"], ["GaT"], q="pool", slow=True)

        NU = CC // 128
        M = HB * NS
        uS = kb.sb("uS", [128, NU, NS], BF16)
        uF = kb.sb("uF", [128, NU, NS], F32)
        cbS = kb.sb("cbS", [128, NU, NS, 3], F32)
        cwS = kb.sb("cwS", [128, NU, 4], F32)
        yS = kb.sb("yS", [128, NU, NS], F32)
        tS = kb.sb("tS", [128, NU, NS], F32)
        for u in range(NU):
            kb.dma(uS[:, u, :], uT[u * 128:(u + 1) * 128, SEQ:SEQ + NS], ["uT"], ["uS"], slow=True)
            kb.dma(cwS[:, u, :], conv_w[:, u * 128:(u + 1) * 128].rearrange("j p -> p j"), [], ["cwS"], slow=True)
            for i in range(NS):
                kb.dma(cbS[:, u, i, :], cb_in[i, :, u * 128:(u + 1) * 128].rearrange("j p -> p j"), [], ["cbS"],
                       slow=True)
        kb.cp("dve", uF[:], uS[:], ["uS"], ["uF"])
        for i in range(NS):
            kb.dma(conv_s[i, 0:2, :], cb_in[i, 1:3, :], [], ["conv_s"], q="pool", final=True)
            for u in range(NU):
                kb.dma(conv_s[i, 2:3, u * 128:(u + 1) * 128].rearrange("o p -> p o"), uF[:, u, i:i + 1], ["uF"],
                       ["conv_s"], q="pool", final=True, slow=True)
        for u in range(NU):
            kb.ts("dve", yS[:, u, :], uF[:, u, :], cwS[:, u, 3:4], None, ALU.mult, None, ["uF", "cwS"], ["yS"])
            for j in range(3):
                kb.stt("dve", yS[:, u, :], cbS[:, u, :, j], cwS[:, u, j:j + 1], yS[:, u, :], ALU.mult, ALU.add,
                       ["cbS", "cwS", "yS"], ["yS"])
        kb.act(yS[:], yS[:], AF.Silu, ["yS"], ["yS"])
        kb.tt("dve", tS[:, 0:2 * HB, :], yS[:, 0:2 * HB, :], yS[:, 0:2 * HB, :], ALU.mult, ["yS"], ["tS"])
        kb.mm(psf[3][:, 0:2 * M], onesf[:], tS[:, 0:2 * HB, :].rearrange("p u i -> p (u i)"), ["k_onesf", "tS"],
              ["psf3"])
        rS = kb.sb("rS", [128, 2 * HB, NS], F32)
        kb.ts("dve", rS[:].rearrange("p u i -> p (u i)"), psf[3][:, 0:2 * M], EPS, None, ALU.add, None, ["psf3"], ["rS"])
        kb.act(rS[:], rS[:], AF.Ln, ["rS"], ["rS"])
        kb.act(rS[:], rS[:], AF.Exp, ["rS"], ["rS"], scale=-0.5)
        kb.tt("dve", yS[:, 0:2 * HB, :], yS[:, 0:2 * HB, :], rS[:], ALU.mult, ["yS", "rS"], ["yS"])
        kb.ts("dve", yS[:, 0:HB, :], yS[:, 0:HB, :], 128.0 ** -0.5, None, ALU.mult, None, ["yS"], ["yS"])
        gbb = kb.sb("gbb", [128, NS, 2 * HB], F32)
        kb.dma(gbb[:].rearrange("p i j -> p (i j)"),
               GBs[SEQ:SEQ + NS, :].rearrange("i j -> (i j)").partition_broadcast(128), ["GBs"], ["gbb"])
        egb = kb.sb("egb", [128, NS, HB], F32)
        negb = kb.sb("negb", [128, NS, HB], F32)
        kb.act(egb[:], gbb[:, :, 0:HB], AF.Exp, ["gbb"], ["egb"])
        kb.ts("dve", negb[:], egb[:], -1.0, None, ALU.mult, None, ["egb"], ["negb"])
        S0 = [kb.sb(f"S0_{i}", [128, 128], F32) for i in range(M)] if M <= 8 else None
        Sn = [kb.sb(f"Sn{i}", [128, 128], F32) for i in range(2)]
        Sall = kb.sb("Sall", [128, M, 128], F32)
        dcol = kb.sb("dcol", [128, M], F32)
        kcol = kb.sb("kcol", [128, M], F32)
        ocol = kb.sb("ocol", [128, M], F32)
        for i in range(NS):
            for h in range(HB):
                m = i * HB + h
                kb.dma(Sall[:, m, :], sg_in[m], [], [f"Sall{m}"])
                kb.cp("pool", kcol[:, m:m + 1], yS[:, HB + h, i:i + 1], ["yS"], ["kcol"])
                kb.mm(psf[4][:, m:m + 1], Sall[:, m, :], yS[:, HB + h, i:i + 1], [f"Sall{m}", "yS"], ["psf4"])
                kb.stt("dve", dcol[:, m:m + 1], psf[4][:, m:m + 1], negb[:, i, h:h + 1], yS[:, 2 * HB + h, i:i + 1],
                       ALU.mult, ALU.add, ["psf4", "negb", "yS"], ["dcol"])
                kb.ts("dve", dcol[:, m:m + 1], dcol[:, m:m + 1], gbb[:, i, HB + h:HB + h + 1], None, ALU.mult, None,
                      ["dcol", "gbb"], ["dcol"])
        Krows = kb.sb("Krows", [M, 128], F32)
        Drows = kb.sb("Drows", [M, 128], F32)
        Dm = [kb.sb(f"Dm{i}", [M, 128], F32) for i in range(2)]
        pT0 = kb.ps("pT0", [128, 256], F32) if False else None
        kb.op("pe", lambda e: e.transpose(psf[5][0:M, 0:128], kcol[:], identf[:]), ["kcol", "k_identf"], ["psf5"])
        kb.op("pe", lambda e: e.transpose(psf[5][0:M, 128:256], dcol[:], identf[:]), ["dcol", "k_identf"], ["psf5"])
        kb.cp("act", Krows[:], psf[5][0:M, 0:128], ["psf5"], ["Krows"])
        kb.cp("act", Drows[:], psf[5][0:M, 128:256], ["psf5"], ["Drows"])
        for i in range(NS):
            for h in range(HB):
                m = i * HB + h
                di = m % 2
                kb.ts("dve", Dm[di][:], Drows[:], identf[0:M, m:m + 1], None, ALU.mult, None, ["Drows", "k_identf"],
                      [f"Dm{di}"])
                kb.mm(psf[di][:, 0:128], Krows[:], Dm[di][:], ["Krows", f"Dm{di}"], [f"psf{di}"])
                kb.stt("dve", Sn[di][:], Sall[:, m, :], egb[:, i, h:h + 1], psf[di][:, 0:128], ALU.mult, ALU.add,
                       [f"Sall{m}", "egb", f"psf{di}"], [f"Sn{di}"])
                kb.dma(s_s[m], Sn[di][:], [f"Sn{di}"], ["s_s"], q="pool", final=True)
                kb.mm(psf[3][:, m:m + 1], Sn[di][:], yS[:, h, i:i + 1], [f"Sn{di}", "yS"], ["psf3"])
        kb.cp("act", ocol[:], psf[3][:, 0:M], ["psf3"], ["ocol"])
        osq = kb.sb("osq", [128, M], F32)
        kb.tt("dve", osq[:], ocol[:], ocol[:], ALU.mult, ["ocol"], ["osq"])
        kb.mm(psf[4][:, 0:M], onesf[:], osq[:], ["k_onesf", "osq"], ["psf4"])
        orr = kb.sb("orr", [128, M], F32)
        kb.ts("dve", orr[:], psf[4][:, 0:M], 1.0 / 128, EPS, ALU.mult, ALU.add, ["psf4"], ["orr"])
        kb.act(orr[:], orr[:], AF.Ln, ["orr"], ["orr"])
        kb.act(orr[:], orr[:], AF.Exp, ["orr"], ["orr"], scale=-0.5)
        gcol = kb.sb("gcol", [128, 1], F32)
        kb.dma(gcol[:], g_onorm.rearrange("(p o) -> p o", o=1), [], ["gcol"], slow=True)
        kb.stt("dve", ocol[:], ocol[:], gcol[:, 0:1], orr[:], ALU.mult, ALU.mult, ["ocol", "gcol", "orr"], ["ocol"])
        zbS = kb.sb("zbS", [128, NS, HB], BF16)
        for i in range(NS):
            kb.dma(zbS[:, i, :], zbs[SEQ + i, :].rearrange("(h p) -> p h", p=128), ["zbs"], ["zbS"], slow=True)
        gbS = kb.sb("gbS", [128, NS, HB], BF16)
        kb.tt("dve", gbS[:].rearrange("p i h -> p (i h)"), ocol[:], zbS[:].rearrange("p i h -> p (i h)"), ALU.mult,
              ["ocol", "zbS"], ["gbS"])
        for h in range(HB):
            kb.dma(GbT[h * 128:(h + 1) * 128, SEQ:SEQ + NS], gbS[:, :, h], ["gbS"], ["GbT"], q="pool", slow=True)
        kb.phase_end()

    if "p4" in phases:
        kb.phase_begin()
        KA_, KB_, KP_ = WA // 128, WB // 128, PLE // 128
        wpa = kb.sb("wpa", [128, KA_, D], BF16)
        wpb = kb.sb("wpb", [128, KB_, D], BF16)
        wo = kb.sb("wo", [128, KD, D], BF16)
        wpg = kb.sb("wpg", [128, KD, D], BF16)
        wpl = kb.sb("wpl", [128, KP_, D], BF16)
        for t_, src, nm in ((wpa, w_pa_b, "w_pa_b"), (wpb, w_pb_b, "w_pb_b"), (wo, w_o_b, "w_o_b"),
                            (wpg, w_pg_b, "w_pg_b"), (wpl, w_ple_b, "w_ple_b")):
            kb.dma(t_[:], src.rearrange("(k p) n -> p k n", p=128), [nm], ["w4_" + nm])
        W4 = ["w4_w_pa_b", "w4_w_pb_b", "w4_w_o_b", "w4_w_pg_b", "w4_w_ple_b"]
        gple = kb.sb("gple", [128, D], F32)
        gfin = kb.sb("gfin", [128, D], F32)
        kb.dma(gple[:], g_ple.partition_broadcast(128), [], ["gple"])
        kb.dma(gfin[:], g_final.partition_broadcast(128), [], ["gfin"])
        GaTt = [kb.sb(f"GaTt{i}", [128, KA_, 512], BF16) for i in range(2)]
        GbTt4 = [kb.sb(f"GbTt4{i}", [128, KB_, 512], BF16) for i in range(2)]
        sga = [kb.sb(f"sga{i}", [128, 512], BF16) for i in range(2)]
        sgb = [kb.sb(f"sgb{i}", [128, 512], BF16) for i in range(2)]
        m1 = [kb.sb(f"m1{i}", [128, 512], F32) for i in range(2)]
        m2 = [kb.sb(f"m2{i}", [128, 512], F32) for i in range(2)]
        mixT = [kb.sb(f"mixT{i}", [128, KD, 512], BF16) for i in range(2)]
        x1 = [kb.sb(f"x1{i}", [128, D], F32) for i in range(2)]
        x2 = [kb.sb(f"x2{i}", [128, D], F32) for i in range(2)]
        hpb = [kb.sb(f"hpb{i}", [128, D], BF16) for i in range(2)]
        hpT = [kb.sb(f"hpT{i}", [128, KD, 128], BF16) for i in range(2)]
        pt_ = [kb.sb(f"pt{i}", [128, PLE], F32) for i in range(2)]
        ptb = [kb.sb(f"ptb{i}", [128, PLE], BF16) for i in range(2)]
        pTt = [kb.sb(f"pTt{i}", [128, KP_, 128], BF16) for i in range(2)]
        g2 = [kb.sb(f"g2{i}", [128, 512], F32) for i in range(2)]
        j4 = kb.sb("j4", [128, D], F32)
        ss4 = [kb.sb(f"ss4{i}", [128, 1], F32) for i in range(2)]
        yt = [kb.sb(f"yt{i}", [128, D], F32) for i in range(2)]
        sc = 0
        tc = 0
        pcn = [0]

        def PS4():
            pcn[0] += 1
            i = pcn[0] % 6
            return psf[i], f"psf{i}"
        for st0, sw in [(a, min(512, SEQ - a)) for a in range(0, SEQ, 512)] + [(SEQ, 128)]:
            i = sc % 2
            sc += 1
            kb.dma(GaTt[i][:, :, :sw], GaT.rearrange("(k p) t -> p k t", p=128)[:, :, st0:st0 + sw], ["GaT"],
                   [f"GaTt{i}"])
            kb.dma(GbTt4[i][:, :, :sw], GbT.rearrange("(k p) t -> p k t", p=128)[:, :, st0:st0 + sw], ["GbT"],
                   [f"GbTt4{i}"])
            for cu in range(KD):
                ui = cu % 2
                cs_ = slice(cu * 128, (cu + 1) * 128)
                kb.dma(sga[ui][:, :sw], sgT[cu * 128:(cu + 1) * 128, st0:st0 + sw], ["sgT"], [f"sga{ui}"])
                kb.dma(sgb[ui][:, :sw], sgT[D + cu * 128:D + (cu + 1) * 128, st0:st0 + sw], ["sgT"], [f"sgb{ui}"])
                pa, pan = PS4()
                for k in range(KA_):
                    kb.mm(pa[:, :sw], wpa[:, k, cs_], GaTt[i][:, k, :sw], W4 + [f"GaTt{i}"], [pan],
                          start=(k == 0), stop=(k == KA_ - 1))
                pb_, pbn_ = PS4()
                for k in range(KB_):
                    kb.mm(pb_[:, :sw], wpb[:, k, cs_], GbTt4[i][:, k, :sw], W4 + [f"GbTt4{i}"], [pbn_],
                          start=(k == 0), stop=(k == KB_ - 1))
                kb.tt("dve", m1[ui][:, :sw], pa[:, :sw], sga[ui][:, :sw], ALU.mult, [pan, f"sga{ui}"], [f"m1{ui}"])
                kb.tt("dve", m2[ui][:, :sw], pb_[:, :sw], sgb[ui][:, :sw], ALU.mult, [pbn_, f"sgb{ui}"], [f"m2{ui}"])
                kb.tt("pool", mixT[i][:, cu, :sw], m1[ui][:, :sw], m2[ui][:, :sw], ALU.add, [f"m1{ui}", f"m2{ui}"],
                      [f"mixT{i}"])
            for tt in range(sw // 128):
                t0 = st0 + tt * 128
                j = tc % 2
                tc += 1
                ts_ = slice(tt * 128, (tt + 1) * 128)
                kb.dma(x1[j][:], x[t0:t0 + 128, :], [], [f"x1{j}"])
                kb.dma(pt_[j][:], p_in[t0:t0 + 128, :], [], [f"pt{j}"])
                for n0 in range(0, D, 512):
                    nw = min(512, D - n0)
                    po, pon = PS4()
                    for k in range(KD):
                        kb.mm(po[:, :nw], mixT[i][:, k, ts_], wo[:, k, n0:n0 + nw], W4 + [f"mixT{i}"], [pon],
                              start=(k == 0), stop=(k == KD - 1))
                    kb.tt("dve", x1[j][:, n0:n0 + nw], po[:, :nw], x1[j][:, n0:n0 + nw], ALU.add, [pon, f"x1{j}"],
                          [f"x1{j}"])
                kb.act(j4[:], x1[j][:], AF.Square, [f"x1{j}"], ["j4", f"ss4{j}"], accum=ss4[j][:])
                kb.rsqrt(ss4[j][:], ss4[j][:], 1.0 / D, EPS, [f"ss4{j}"], [f"ss4{j}"])
                kb.stt("dve", hpb[j][:], x1[j][:], ss4[j][:, 0:1], gple[:], ALU.mult, ALU.mult,
                       [f"x1{j}", f"ss4{j}", "gple"], [f"hpb{j}"])
                kb.cp("pool", ptb[j][:], pt_[j][:], [f"pt{j}"], [f"ptb{j}"])
                bi = tc % 2
                for k in range(KD):
                    kb.tr(psb[bi][:, k * 128:(k + 1) * 128], hpb[j][:, k * 128:(k + 1) * 128], identb[:],
                          [f"hpb{j}", "k_identb"], [f"psb{bi}"])
                kb.cp("act", hpT[j][:].rearrange("p k t -> p (k t)"), psb[bi][:, 0:KD * 128], [f"psb{bi}"],
                      [f"hpT{j}"])
                for k in range(KP_):
                    kb.tr(psb[bi][:, k * 128:(k + 1) * 128], ptb[j][:, k * 128:(k + 1) * 128], identb[:],
                          [f"ptb{j}", "k_identb", f"hpT{j}"], [f"psb{bi}"])
                kb.cp("act", pTt[j][:].rearrange("p k t -> p (k t)"), psb[bi][:, 0:KP_ * 128], [f"psb{bi}"],
                      [f"pTt{j}"])
                for n0 in range(0, D, 512):
                    nw = min(512, D - n0)
                    pg_, pgn = PS4()
                    for k in range(KD):
                        kb.mm(pg_[:, :nw], hpT[j][:, k, :], wpg[:, k, n0:n0 + nw], W4 + [f"hpT{j}"], [pgn],
                              start=(k == 0), stop=(k == KD - 1))
                    pl_, pln = PS4()
                    for k in range(KP_):
                        kb.mm(pl_[:, :nw], pTt[j][:, k, :], wpl[:, k, n0:n0 + nw], W4 + [f"pTt{j}"], [pln],
                              start=(k == 0), stop=(k == KP_ - 1))
                    kb.act(g2[j][:, :nw], pg_[:, :nw], AF.Sigmoid, [pgn], [f"g2{j}"])
                    kb.tt("dve", g2[j][:, :nw], pl_[:, :nw], g2[j][:, :nw], ALU.mult, [pln, f"g2{j}"], [f"g2{j}"])
                    kb.tt("pool", x2[j][:, n0:n0 + nw], x1[j][:, n0:n0 + nw], g2[j][:, :nw], ALU.add,
                          [f"x1{j}", f"g2{j}"], [f"x2{j}"])
                kb.act(j4[:], x2[j][:], AF.Square, [f"x2{j}"], ["j4", f"ss4{j}"], accum=ss4[j][:])
                kb.rsqrt(ss4[j][:], ss4[j][:], 1.0 / D, EPS, [f"ss4{j}"], [f"ss4{j}"])
                kb.stt("dve", yt[j][:], x2[j][:], ss4[j][:, 0:1], gfin[:], ALU.mult, ALU.mult,
                       [f"x2{j}", f"ss4{j}", "gfin"], [f"yt{j}"])
                kb.dma(y_p[t0:t0 + 128, :], yt[j][:], [f"yt{j}"], ["y_p"], q="pool", final=True)
        kb.phase_end()


    print('total ops', getattr(kb.S, 'total', 0))
    kb.S.barrier()
    kb.S.emit()
    kb.stack.close()
    return nc


_CACHE = {}


def kernel(x_prompt, x_sample, p_prompt, p_sample, cache_k, cache_v, page_table, state_gdn_s, state_gdn_conv,
           g_mix, w_in, conv_w, a_log, dt_bias, g_onorm, w_pa, w_pb, w_o, g_ple, w_ple_gate, w_ple, g_final):
    cfg = FULL
    NC_, NS, SEQ, D, PLE = cfg["NCORES"], cfg["NS"], cfg["SEQ"], cfg["D"], cfg["PLE"]
    HA, HB = cfg["HA"], cfg["HB"]
    WA, WB = HA * 64, HB * 128
    CC = 3 * WB
    if "nc" not in _CACHE:
        _CACHE["nc"] = build(cfg)
    nc = _CACHE["nc"]
    f32 = lambda a: np.ascontiguousarray(np.asarray(a), dtype=np.float32)
    consts = host_consts(cfg)
    rio = riota_table(cfg)
    ck = f32(cache_k[0]).reshape(cfg["NPOOL"], 128 * WA)
    cv = f32(cache_v[0]).reshape(cfg["NPOOL"], 128 * WA)
    xp, xs, pp, ps_ = f32(x_prompt), f32(x_sample), f32(p_prompt[0]), f32(p_sample[0])
    pt = np.ascontiguousarray(np.asarray(page_table), dtype=np.int32)
    sg = f32(state_gdn_s[0])
    cb = f32(state_gdn_conv[0])
    shared = dict(w_in=f32(w_in[0]), g_mix=f32(g_mix[0]), conv_w=f32(conv_w[0]), a_log=f32(a_log[0]),
                  dt_bias=f32(dt_bias[0]), g_onorm=f32(g_onorm[0]), w_pa=f32(w_pa[0]), w_pb=f32(w_pb[0]),
                  w_o=f32(w_o[0]), g_ple=f32(g_ple[0]), w_pg=f32(w_ple_gate[0]), w_ple=f32(w_ple[0]),
                  g_final=f32(g_final), ck=ck, cv=cv, riota=rio)
    for k, v in consts.items():
        shared["c_" + k] = v
    in_maps = []
    for c in range(NC_):
        b = c % 2
        sl = slice(c * NS, (c + 1) * NS)
        xe = np.zeros((SEQ + 128, D), np.float32)
        xe[:SEQ] = xp[b]
        xe[SEQ:SEQ + NS] = xs[sl, 0]
        pe = np.zeros((SEQ + 128, PLE), np.float32)
        pe[:SEQ] = pp[b]
        pe[SEQ:SEQ + NS] = ps_[sl, 0]
        m = dict(shared)
        m.update(x=xe, p=pe, ptab=np.ascontiguousarray(pt[sl]), sg_in=np.ascontiguousarray(sg[sl]).reshape(NS * HB, 128, 128),
                 cb_in=np.ascontiguousarray(cb[sl]))
        in_maps.append(m)
    res = run_bass_kernel_spmd(nc, in_maps, core_ids=list(range(NC_))).results
    B = 2
    y_prompt = np.stack([res[b]["y_p"][:SEQ] for b in range(B)])
    y_sample = np.concatenate([res[c]["y_p"][SEQ:SEQ + NS] for c in range(NC_)])[:, None, :]
    k_prompt = np.stack([res[b]["k_p"][:SEQ] for b in range(B)]).reshape(1, B, SEQ, HA, 64)
    v_prompt = np.stack([res[b]["v_p"][:SEQ] for b in range(B)]).reshape(1, B, SEQ, HA, 64)
    s_prompt = np.stack([res[b]["s_p"] for b in range(B)])[None]
    conv_prompt = np.stack([res[b]["conv_p"] for b in range(B)])[None]
    k_sample = np.concatenate([res[c]["k_p"][SEQ:SEQ + NS] for c in range(NC_)]).reshape(1, NC_ * NS, 1, HA, 64)
    v_sample = np.concatenate([res[c]["v_p"][SEQ:SEQ + NS] for c in range(NC_)]).reshape(1, NC_ * NS, 1, HA, 64)
    s_sample = np.concatenate([res[c]["s_s"] for c in range(NC_)]).reshape(1, NC_ * NS, HB, 128, 128)
    conv_sample = np.concatenate([res[c]["conv_s"] for c in range(NC_)])[None]
    return tuple(np.ascontiguousarray(a, dtype=np.float32) for a in
                 (y_prompt, y_sample, k_prompt, v_prompt, s_prompt, conv_prompt, k_sample, v_sample, s_sample,
                  conv_sample))
```
